# Optimizing a Trainium2 kernel written in Bass

```python
import jax, jax.numpy as jnp
from jax import lax
import numpy as np

D_MODEL = 1024
BATCH = 2
SEQ = 16384
DEPTH = 1
DEC_BATCH = 16
DEC_SEQ = 16
PAST_LEN = 2048

CHUNK = 64
N_PREV_CHUNKS = 8
ATT_PAST = N_PREV_CHUNKS * CHUNK
BAND = ATT_PAST + CHUNK
D_MIX = D_MODEL
D_ATT = D_MIX // 2
ATT_HEADS = 8
ATT_HEAD_DIM = D_ATT // ATT_HEADS
D_CONV = D_MIX - D_ATT
CONV_WIDTH = 31
REL_CLIP = 128
N_MEM = 256
MEM_HEADS = 4
MEM_HEAD_DIM = D_MODEL // MEM_HEADS
D_FF = -(-8 * D_MODEL // (3 * 256)) * 256
D_IN = 3 * D_ATT + 2 * D_CONV
EPS = 1e-6
NEG_INF = -1e30

kernel_name = "hybrid_streaming_encoder_step"


def rmsnorm(x, g):
    xf = x.astype(jnp.float32)
    y = xf * lax.rsqrt(jnp.mean(xf * xf, axis=-1, keepdims=True) + EPS)
    return (y * g.astype(jnp.float32)).astype(x.dtype)


def layernorm(x, g, b):
    xf = x.astype(jnp.float32)
    mu = jnp.mean(xf, axis=-1, keepdims=True)
    xc = xf - mu
    var = jnp.mean(xc * xc, axis=-1, keepdims=True)
    y = xc * lax.rsqrt(var + EPS) * g.astype(jnp.float32) + b.astype(jnp.float32)
    return y.astype(x.dtype)


def band_attention(q, k_ext, v_ext, key_valid, rel_bias):
    n_chunks = q.shape[1] // CHUNK
    qi = jnp.arange(CHUNK)[:, None]
    kj = jnp.arange(BAND)[None, :]
    dist = jnp.clip(qi + ATT_PAST - kj, -REL_CLIP, REL_CLIP) + REL_CLIP
    bias = rel_bias[:, dist].astype(jnp.float32)
    scale = ATT_HEAD_DIM ** -0.5

    def one_chunk(c):
        start = c * CHUNK
        qc = lax.dynamic_slice_in_dim(q, start, CHUNK, axis=1)
        kc = lax.dynamic_slice_in_dim(k_ext, start, BAND, axis=1)
        vc = lax.dynamic_slice_in_dim(v_ext, start, BAND, axis=1)
        valid = lax.dynamic_slice_in_dim(key_valid, start, BAND)
        s = jnp.einsum("bqhd,bkhd->bhqk", qc, kc, preferred_element_type=jnp.float32) * scale + bias
        s = jnp.where(valid[None, None, None, :], s, NEG_INF)
        p = jax.nn.softmax(s, axis=-1)
        return jnp.einsum("bhqk,bkhd->bqhd", p.astype(vc.dtype), vc)

    out = lax.map(one_chunk, jnp.arange(n_chunks))
    return jnp.moveaxis(out, 0, 1).reshape(q.shape)


def parallel_mixer(h, past_k, past_v, n_past_valid, keep, conv_buf,
                   w_in, rel_bias, conv_w, conv_b, cln_g, cln_b, w_out):
    B, T, _ = h.shape
    proj = h @ w_in
    q, k, v, u_val, u_gate = jnp.split(
        proj, [D_ATT, 2 * D_ATT, 3 * D_ATT, 3 * D_ATT + D_CONV], axis=-1)
    q = q.reshape(B, T, ATT_HEADS, ATT_HEAD_DIM)
    k = k.reshape(B, T, ATT_HEADS, ATT_HEAD_DIM)
    v = v.reshape(B, T, ATT_HEADS, ATT_HEAD_DIM)

    t_pad = (-T) % CHUNK
    pad = lambda a: jnp.pad(a, ((0, 0), (0, t_pad), (0, 0), (0, 0)))
    k_ext = jnp.concatenate([past_k, pad(k)], axis=1)
    v_ext = jnp.concatenate([past_v, pad(v)], axis=1)
    key_valid = jnp.concatenate([jnp.arange(ATT_PAST) >= ATT_PAST - n_past_valid,
                                 jnp.arange(T + t_pad) < T])
    att = band_attention(pad(q), k_ext, v_ext, key_valid, rel_bias)[:, :T]
    att = att.reshape(B, T, D_ATT)
    end = ATT_PAST + T
    new_k = k_ext[:, end - keep:end]
    new_v = v_ext[:, end - keep:end]

    u = u_val * jax.nn.sigmoid(u_gate)
    u_ext = jnp.concatenate([conv_buf, u], axis=1)
    c = lax.conv_general_dilated(
        u_ext, conv_w[:, None, :], window_strides=(1,), padding="VALID",
        dimension_numbers=("NWC", "WIO", "NWC"), feature_group_count=D_CONV) + conv_b
    c = jax.nn.silu(layernorm(c, cln_g, cln_b))
    new_conv = u_ext[:, -(CONV_WIDTH - 1):]

    y = jnp.concatenate([att, c], axis=-1) @ w_out
    return y, new_k, new_v, new_conv


def memory_kv(mem, g_mem, w_mk, w_mv):
    B = mem.shape[0]
    m = rmsnorm(mem, g_mem)
    mk = (m @ w_mk).reshape(B, N_MEM, MEM_HEADS, MEM_HEAD_DIM)
    mv = (m @ w_mv).reshape(B, N_MEM, MEM_HEADS, MEM_HEAD_DIM)
    return mk, mv


def memory_attention(h, mk, mv, w_mq, w_mo):
    B, T, _ = h.shape
    q = (h @ w_mq).reshape(B, T, MEM_HEADS, MEM_HEAD_DIM)
    s = jnp.einsum("bthd,bmhd->bhtm", q, mk, preferred_element_type=jnp.float32) * (MEM_HEAD_DIM ** -0.5)
    p = jax.nn.softmax(s, axis=-1)
    o = jnp.einsum("bhtm,bmhd->bthd", p.astype(mv.dtype), mv).reshape(B, T, D_MODEL)
    return o @ w_mo


def encoder_layer(x, past_k, past_v, n_past_valid, keep, conv_buf, mk, mv,
                  g_mix_pre, g_mix_post, w_in, rel_bias, conv_w, conv_b, cln_g, cln_b, w_out,
                  g_mem_pre, g_mem_post, w_mq, w_mo,
                  g_ffn_pre, g_ffn_post, w_gate, w_up, w_down):
    h = rmsnorm(x, g_mix_pre)
    y, new_k, new_v, new_conv = parallel_mixer(h, past_k, past_v, n_past_valid, keep, conv_buf,
                                               w_in, rel_bias, conv_w, conv_b, cln_g, cln_b, w_out)
    x = x + rmsnorm(y, g_mix_post)
    h = rmsnorm(x, g_mem_pre)
    x = x + rmsnorm(memory_attention(h, mk, mv, w_mq, w_mo), g_mem_post)
    h = rmsnorm(x, g_ffn_pre)
    f = (jax.nn.silu(h @ w_gate) * (h @ w_up)) @ w_down
    x = x + rmsnorm(f, g_ffn_post)
    return x, new_k, new_v, new_conv


def setup_inputs(seed: int = 0) -> dict:
    key = jax.random.key(seed)
    ks = iter(jax.random.split(key, 40))
    f32 = jnp.float32
    nrm = lambda shape, s: jax.random.normal(next(ks), shape, f32) * s
    gain = lambda shape: 1.0 + nrm(shape, 0.02)
    R = min(ATT_PAST, PAST_LEN)
    return {
        "x_prompt": nrm((BATCH, SEQ, D_MODEL), 1.0),
        "x_sample": nrm((DEC_BATCH, DEC_SEQ, D_MODEL), 1.0),
        "cache_att_k": nrm((DEPTH, DEC_BATCH, R, ATT_HEADS, ATT_HEAD_DIM), 1.0),
        "cache_att_v": nrm((DEPTH, DEC_BATCH, R, ATT_HEADS, ATT_HEAD_DIM), 1.0),
        "cache_conv": nrm((DEPTH, DEC_BATCH, CONV_WIDTH - 1, D_CONV), 0.5),
        "cache_mem_k": nrm((DEPTH, DEC_BATCH, N_MEM, MEM_HEADS, MEM_HEAD_DIM), 1.0),
        "cache_mem_v": nrm((DEPTH, DEC_BATCH, N_MEM, MEM_HEADS, MEM_HEAD_DIM), 1.0),
        "mem_prompt": nrm((BATCH, N_MEM, D_MODEL), 1.0),
        "g_mix_pre": gain((DEPTH, D_MODEL)),
        "g_mix_post": gain((DEPTH, D_MODEL)),
        "w_in": nrm((DEPTH, D_MODEL, D_IN), D_MODEL ** -0.5),
        "rel_bias": nrm((DEPTH, ATT_HEADS, 2 * REL_CLIP + 1), 0.1),
        "conv_w": nrm((DEPTH, CONV_WIDTH, D_CONV), CONV_WIDTH ** -0.5),
        "conv_b": nrm((DEPTH, D_CONV), 0.01),
        "cln_g": gain((DEPTH, D_CONV)),
        "cln_b": nrm((DEPTH, D_CONV), 0.01),
        "w_out": nrm((DEPTH, D_MIX, D_MODEL), D_MIX ** -0.5),
        "g_mem_pre": gain((DEPTH, D_MODEL)),
        "g_mem_post": gain((DEPTH, D_MODEL)),
        "g_mem_kv": gain((DEPTH, D_MODEL)),
        "w_mq": nrm((DEPTH, D_MODEL, D_MODEL), D_MODEL ** -0.5),
        "w_mk": nrm((DEPTH, D_MODEL, D_MODEL), D_MODEL ** -0.5),
        "w_mv": nrm((DEPTH, D_MODEL, D_MODEL), D_MODEL ** -0.5),
        "w_mo": nrm((DEPTH, D_MODEL, D_MODEL), D_MODEL ** -0.5),
        "g_ffn_pre": gain((DEPTH, D_MODEL)),
        "g_ffn_post": gain((DEPTH, D_MODEL)),
        "w_gate": nrm((DEPTH, D_MODEL, D_FF), D_MODEL ** -0.5),
        "w_up": nrm((DEPTH, D_MODEL, D_FF), D_MODEL ** -0.5),
        "w_down": nrm((DEPTH, D_FF, D_MODEL), D_FF ** -0.5),
    }


def reference(x_prompt, x_sample, cache_att_k, cache_att_v, cache_conv, cache_mem_k, cache_mem_v,
              mem_prompt, g_mix_pre, g_mix_post, w_in, rel_bias, conv_w, conv_b, cln_g, cln_b, w_out,
              g_mem_pre, g_mem_post, g_mem_kv, w_mq, w_mk, w_mv, w_mo,
              g_ffn_pre, g_ffn_post, w_gate, w_up, w_down):
    B, T_p, _ = x_prompt.shape
    Bs, T_s, _ = x_sample.shape
    R = cache_att_k.shape[2]
    keep_prompt = min(ATT_PAST, T_p)
    xp, xs = x_prompt, x_sample
    akp, avp, cvp, mkp, mvp, aks, avs, cvs = [], [], [], [], [], [], [], []
    for l in range(DEPTH):
        w = (g_mix_pre[l], g_mix_post[l], w_in[l], rel_bias[l], conv_w[l], conv_b[l], cln_g[l], cln_b[l],
             w_out[l], g_mem_pre[l], g_mem_post[l], w_mq[l], w_mo[l],
             g_ffn_pre[l], g_ffn_post[l], w_gate[l], w_up[l], w_down[l])
        zeros_kv = jnp.zeros((B, ATT_PAST, ATT_HEADS, ATT_HEAD_DIM), xp.dtype)
        zeros_conv = jnp.zeros((B, CONV_WIDTH - 1, D_CONV), xp.dtype)
        mk_p, mv_p = memory_kv(mem_prompt, g_mem_kv[l], w_mk[l], w_mv[l])
        xp, nk_p, nv_p, nc_p = encoder_layer(xp, zeros_kv, zeros_kv, 0, keep_prompt, zeros_conv,
                                             mk_p, mv_p, *w)
        past_pad = ((0, 0), (ATT_PAST - R, 0), (0, 0), (0, 0))
        pk = jnp.pad(cache_att_k[l], past_pad)
        pv = jnp.pad(cache_att_v[l], past_pad)
        xs, nk_s, nv_s, nc_s = encoder_layer(xs, pk, pv, R, R, cache_conv[l],
                                             cache_mem_k[l], cache_mem_v[l], *w)
        akp.append(nk_p); avp.append(nv_p); cvp.append(nc_p); mkp.append(mk_p); mvp.append(mv_p)
        aks.append(nk_s); avs.append(nv_s); cvs.append(nc_s)
    new_att_k_prompt = jnp.stack(akp, 0)
    new_att_v_prompt = jnp.stack(avp, 0)
    new_conv_prompt = jnp.stack(cvp, 0)
    new_mem_k_prompt = jnp.stack(mkp, 0)
    new_mem_v_prompt = jnp.stack(mvp, 0)
    new_att_k_sample = jnp.stack(aks, 0)
    new_att_v_sample = jnp.stack(avs, 0)
    new_conv_sample = jnp.stack(cvs, 0)
    return (xp, xs, new_att_k_prompt, new_att_v_prompt, new_conv_prompt, new_mem_k_prompt,
            new_mem_v_prompt, new_att_k_sample, new_att_v_sample, new_conv_sample)
```

```python
import contextlib
import numpy as np
import concourse.bass as bass
import concourse.mybir as mybir
from concourse.bass_utils import run_bass_kernel_spmd

F32 = mybir.dt.float32
BF16 = mybir.dt.bfloat16
AF = mybir.ActivationFunctionType
ALU = mybir.AluOpType

D = 1024
NBLK = 8
NB = 512
D_FF = 2816
EPS = 1e-6
NEG = -1e30
DEBUG = False


class Buf:
    __slots__ = ("name", "w", "r")

    def __init__(self, name):
        self.name = name
        self.w = None
        self.r = {}


class Prog:
    ENG = ("sp", "act", "dve", "pool", "pe")
    SAME_WAIT = {"pool", "act", "dve"}

    def __init__(self):
        self.q = {e: [] for e in self.ENG}
        self.cnt = {e: 0 for e in self.ENG}
        self.seen = {e: {} for e in self.ENG}
        self.dval = {}

    def _waits(self, eng, reads, writes):
        need = {}
        for b in reads:
            if b.w is not None:
                need[b.w[0]] = max(need.get(b.w[0], 0), b.w[1])
        for b in writes:
            if b.w is not None:
                need[b.w[0]] = max(need.get(b.w[0], 0), b.w[1])
            for k, v in b.r.items():
                need[k] = max(need.get(k, 0), v)
        for k, v in need.items():
            if k == eng and eng not in self.SAME_WAIT:
                continue
            if self.seen[eng].get(k, 0) >= v:
                continue
            self.seen[eng][k] = v
            self.q[eng].append(("wait", k, v))

    def op(self, eng, fn, reads=(), writes=(), signal=True, touch=()):
        self._waits(eng, reads, writes)
        if signal:
            self.cnt[eng] += 1
            tok = (eng, self.cnt[eng])
        else:
            tok = (eng, self.cnt[eng] + 1)
        self.q[eng].append(("op", fn, signal))
        for b in list(reads) + list(touch):
            b.r[eng] = max(b.r.get(eng, 0), tok[1])
        for b in writes:
            b.w = tok
            b.r = {}
        return tok

    def dma(self, qeng, sem, out, in_, reads=(), writes=()):
        self._waits(qeng, reads, writes)
        self.dval[sem] = self.dval.get(sem, 0) + 16
        tok = (sem, self.dval[sem])
        self.q[qeng].append(("dma", out, in_, sem))
        for b in reads:
            b.r[sem] = max(b.r.get(sem, 0), tok[1])
        for b in writes:
            b.w = tok
            b.r = {}
        return tok


def f_mm(out, lhsT, rhs, start, stop):
    return lambda e: e.matmul(out, lhsT=lhsT, rhs=rhs, start=start, stop=stop)


def f_tr(out, in_, ident):
    return lambda e: e.transpose(out=out, in_=in_, identity=ident)


def f_act(out, in_, func, scale=None, bias=None, accum=None):
    def f(e):
        kw = {}
        if scale is not None:
            kw["scale"] = scale
        if bias is not None:
            kw["bias"] = bias
        if accum is not None:
            kw["accum_out"] = accum
        return e.activation(out=out, in_=in_, func=func, **kw)
    return f


def f_tt(out, a, b, op):
    return lambda e: e.tensor_tensor(out=out, in0=a, in1=b, op=op)


def f_stt(out, a, s, b, op0, op1):
    return lambda e: e.scalar_tensor_tensor(out=out, in0=a, scalar=s, in1=b, op0=op0, op1=op1)


def f_ts(out, a, s1, s2, op0, op1):
    return lambda e: e.tensor_scalar(out=out, in0=a, scalar1=s1, scalar2=s2, op0=op0, op1=op1)


def f_ts1(out, a, s1, op0):
    return lambda e: e.tensor_scalar(out=out, in0=a, scalar1=s1, scalar2=None, op0=op0)


def f_copy(out, a):
    return lambda e: e.tensor_copy(out=out, in_=a)


def f_memset(ap, v):
    return lambda e: e.memset(ap, v)


def f_recip(out, a):
    return lambda e: e.reciprocal(out=out, in_=a)


def build_nc():
    nc = bass.Bass("TRN2", target_bir_lowering=False)
    P = Prog()
    es = contextlib.ExitStack()

    def din(name, shape):
        return nc.dram_tensor(name, list(shape), F32, kind="ExternalInput")

    def dout(name, shape):
        return nc.dram_tensor(name, list(shape), F32, kind="ExternalOutput")

    xp = din("xp", [4096, D]).ap()
    xh = din("xh", [512, D]).ap()
    vflag_d = din("vflag", [128, 1]).ap()
    xs_d = din("xs", [32, D]).ap()
    ck_d = din("ck", [2, 512, 512]).ap()
    cv_d = din("cv", [2, 512, 512]).ap()
    cc_d = din("cc", [2, 30, 512]).ap()
    cmk_d = din("cmk", [2, 256, D]).ap()
    cmv_d = din("cmv", [2, 256, D]).ap()
    memp_d = din("memp", [256, D]).ap()
    gv = {n: din(n, [1, D]).ap() for n in
          ("g_mix_pre", "g_mix_post", "g_mem_pre", "g_mem_post", "g_mem_kv", "g_ffn_pre", "g_ffn_post")}
    rb_d = din("rel_bias", [8, 257]).ap()
    cvec_d = din("cvec", [34, 512]).ap()
    W = {"w_in": din("w_in", [D, 2560]), "w_out": din("w_out", [D, D]), "w_mq": din("w_mq", [D, D]),
         "w_mk": din("w_mk", [D, D]), "w_mv": din("w_mv", [D, D]), "w_mo": din("w_mo", [D, D]),
         "w_gate": din("w_gate", [D, D_FF]), "w_up": din("w_up", [D, D_FF]), "w_down": din("w_down", [D_FF, D])}

    y_d = dout("y", [4096, D]).ap()
    ys_d = dout("ys", [32, D]).ap()
    ktail_d = dout("ktail", [512, 512]).ap()
    vtail_d = dout("vtail", [512, 512]).ap()
    ctail_d = dout("ctail", [30, 512]).ap()
    mko_d = dout("mko", [256, D]).ap()
    mvo_d = dout("mvo", [256, D]).ap()
    kso_d = dout("kso", [2, 512, 512]).ap()
    vso_d = dout("vso", [2, 512, 512]).ap()
    cso_d = dout("cso", [2, 30, 512]).ap()

    nsl = {"w_in": 5, "w_out": 2, "w_mq": 2, "w_mk": 2, "w_mv": 2, "w_mo": 2, "w_gate": 6, "w_up": 6, "w_down": 6}
    wsc = {n: nc.dram_tensor("ws_" + n, [k, 128, 8, 512], BF16) for n, k in nsl.items()}
    wsc_buf = {n: Buf("ws_" + n) for n in nsl}
    per = nc.dram_tensor("per", [8, 129, 768], F32)
    per_buf = Buf("per")

    def slab_geom(name, idx):
        if name == "w_down":
            n, ks = idx // 3, idx % 3
            return ks * 1024, (8, 8, 6)[ks], n * 512, 512
        if name in ("w_gate", "w_up"):
            return 0, 8, idx * 512, min(512, D_FF - idx * 512)
        return 0, 8, idx * 512, 512

    def sb(name, shape, dtype):
        return es.enter_context(nc.sbuf_tensor("t_" + name, list(shape), dtype))

    NSLOT = 4
    wring = [sb(f"wring{i}", [128, 8, 512], BF16) for i in range(NSLOT)]
    wring_buf = [Buf(f"wring{i}") for i in range(NSLOT)]
    xres = [sb(f"xres{i}", [128, 4, D], F32) for i in range(2)]
    xres_buf = [[Buf(f"xres{p}_{i}") for i in range(4)] for p in range(2)]
    actT = [sb(f"actT{i}", [128, 8, 512], BF16) for i in range(2)]
    actT_buf = [Buf(f"actT{i}") for i in range(2)]
    oT_kbuf = [Buf(f"oTk{k}") for k in range(8)]
    qT = sb("qT", [128, 4, 512], BF16)
    qT_buf = Buf("qT")
    kT = sb("kT", [128, 4, 1024], BF16)
    kT_buf = [Buf("kT0"), Buf("kT1")]
    Vext = sb("Vext", [128, 8, 8, 65], BF16)
    V_buf = [Buf("V0"), Buf("V1")]
    uT = sb("uT", [128, 4, 30 + 512], F32)
    uT_buf = Buf("uT")
    uTs = sb("uTs", [128, 4, 2, 46], F32)
    uTs_buf = Buf("uTs")
    cacc = sb("cacc", [128, 4, 512], F32)
    cacc_buf = [Buf(f"cacc{c}") for c in range(4)]
    sg = [sb(f"sg{i}", [128, 512], F32) for i in range(2)]
    sg_buf = [Buf(f"sg{i}") for i in range(2)]
    att = sb("att", [128, 512], BF16)
    att_buf = Buf("att")
    pT = [sb(f"pT{i}", [128, 5, 128], BF16) for i in range(3)]
    pT_buf = [Buf(f"pT{i}") for i in range(3)]
    Btab = sb("Btab", [128, 3, 8, 128], F32)
    Btab_buf = Buf("Btab")
    hid = sb("hid", [128, 22, 512], BF16)
    hid_buf = [Buf(f"hid{f}") for f in range(22)]
    pm = [sb(f"pm{i}", [128, 2, 512], BF16) for i in range(2)]
    pm_buf = [Buf(f"pm{i}") for i in range(2)]
    rs = [sb(f"rs{i}", [128, 512], F32) for i in range(3)]
    rs_buf = [Buf(f"rs{i}") for i in range(3)]
    tb = [rs[i][:, 0:384].rearrange("p (s n) -> p s n", s=3) for i in range(3)]
    tb_buf = rs_buf
    mkT = sb("mkT", [128, 8, 256], BF16)
    mkT_buf = Buf("mkT")
    mv = sb("mv", [128, 2, D], BF16)
    mv_buf = Buf("mv")
    gbuf = [sb(f"gbuf{i}", [128, D], F32) for i in range(2)]
    gbuf_buf = [Buf(f"gbuf{i}") for i in range(2)]
    hbf = [sb(f"hbf{i}", [128, D], BF16) for i in range(2)]
    hbf_buf = [Buf(f"hbf{i}") for i in range(2)]
    zc = [hbf[i // 2][:, (i % 2) * 512:(i % 2 + 1) * 512] for i in range(4)]
    zc_buf = [hbf_buf[i // 2] for i in range(4)]
    tmp = [sb(f"tmp{i}", [128, D], F32) for i in range(2)]
    tmp_buf = [Buf(f"tmp{i}") for i in range(2)]
    stage = [sb(f"stage{i}", [128, 512], F32) for i in range(1)]
    stage_buf = [Buf(f"stage{i}") for i in range(1)]
    ident = sb("ident", [128, 128], BF16)
    identf = sb("identf", [128, 128], F32)
    ones = sb("ones", [128, 128], BF16)
    const_buf = Buf("const")
    cvec_s = tmp[1]
    cvec = sb("cvecT", [128, 4, 34], F32)
    cvec_buf = Buf("cvec")
    cvs_buf = tmp_buf[1]
    frow = tmp[0]
    frow_buf = tmp_buf[0]
    vflag = sb("vflag_s", [128, 1], F32)
    vflag_buf = Buf("vflag")
    sm = sb("sm", [128, 64], F32)
    sm_buf = [Buf(f"sm{i}") for i in range(8)]
    cst = sb("cst", [128, 4], F32)
    bnst = sb("bnst", [128, 4, 8], F32)
    ps = es.enter_context(nc.psum_tensor("ps", [128, 8, 512], F32))
    bank_buf = [Buf(f"bank{i}") for i in range(8)]

    st = {"rr": 0, "reserved": set(), "slab_pos": 0, "sm": 0, "g": 0, "hbf": 0, "tmp": 0, "stage": 0, "sg": 0,
          "tb": 0, "pm": 0, "rs": 0, "hslot": 0}

    def bank(n=1):
        for _ in range(16):
            p = st["rr"]
            if n == 2 and p % 2 == 1:
                p = (p + 1) % 8
            cand = [(p + i) % 8 for i in range(n)]
            st["rr"] = (p + n) % 8
            if not any(c in st["reserved"] for c in cand):
                return cand if n > 1 else cand[0]
        raise RuntimeError(f"no psum bank n={n} reserved={sorted(st['reserved'])} rr={st['rr']}")

    def rot(key, n):
        v = st[key]
        st[key] = (v + 1) % n
        return v

    def bk(b):
        return ps[:, b, :]

    def bk_bf(b, k, n):
        return ps[:, b, :].bitcast(BF16).rearrange("p (k n) -> p k n", k=k)[:, :, 0:n]

    def blk_seq(with_next, first):
        seq = [("w_out", 0), ("w_out", 1)]
        if with_next:
            seq += [("w_in", 0), ("w_in", 1)]
        if first:
            seq += [("w_mk", 0), ("w_mk", 1), ("w_mv", 0), ("w_mv", 1)]
        seq += [("w_mq", 0), ("w_mq", 1), ("w_mo", 0), ("w_mo", 1)]
        if with_next:
            seq += [("w_in", 2), ("w_in", 3), ("w_in", 4)]
        seq += [x for s in range(6) for x in (("w_gate", s), ("w_up", s))]
        seq += [("w_down", i) for i in range(6)]
        return seq
    sched = [("w_in", 3), ("w_in", 4), ("w_in", 0), ("w_in", 1), ("w_in", 2), ("w_in", 3), ("w_in", 4), ("w_in", 1), ("w_in", 2)]
    for _b in range(NBLK + 1):
        sched += blk_seq(_b < NBLK, _b == 0)
    DEPTH = 3
    loaded = {"n": 0}

    def issue_loads(upto):
        while loaded["n"] < min(upto, len(sched)):
            p = loaded["n"]
            name, idx = sched[p]
            r0, kt, c0, ncols = slab_geom(name, idx)
            slot = p % NSLOT
            ensure_cast(name, idx)
            P.dma("sp", f"wr{slot}", wring[slot][:, 0:kt, 0:ncols], wsc[name].ap()[idx][:, 0:kt, 0:ncols],
                  reads=[slab_cast_buf[(name, idx)]], writes=[wring_buf[slot]])
            loaded["n"] += 1

    def use_slab(name, idx):
        p = st["slab_pos"]
        assert sched[p] == (name, idx), (p, sched[p], name, idx)
        issue_loads(p + DEPTH)
        emit_casts(2)
        st["slab_pos"] = p + 1
        slot = p % NSLOT
        return wring[slot], wring_buf[slot]

    slab_cast_buf = {}
    cast_pending = []

    def cast_one(name, idx):
        r0, kt, c0, ncols = slab_geom(name, idx)
        src = W[name].ap()[r0:r0 + kt * 128, c0:c0 + ncols].rearrange("(ko p) c -> p ko c", p=128)
        bf = Buf(f"ws_{name}_{idx}")
        P.dma("pool", f"cast_{name}_{idx}", wsc[name].ap()[idx][:, 0:kt, 0:ncols], src, writes=[bf])
        slab_cast_buf[(name, idx)] = bf

    def emit_casts(n):
        k = 0
        while cast_pending and k < n:
            cast_one(*cast_pending.pop(0))
            k += 1

    def ensure_cast(name, idx):
        while (name, idx) not in slab_cast_buf:
            assert cast_pending, ("no cast scheduled for", name, idx)
            cast_one(*cast_pending.pop(0))

    def cast_group(names):
        for name in names:
            order = [3, 4, 0, 1, 2] if name == "w_in" else list(range(nsl[name]))
            for idx in order:
                r0, kt, c0, ncols = slab_geom(name, idx)
                src = W[name].ap()[r0:r0 + kt * 128, c0:c0 + ncols].rearrange("(ko p) c -> p ko c", p=128)
                bf = Buf(f"ws_{name}_{idx}")
                P.dma("pool", f"cast_{name}_{idx}", wsc[name].ap()[idx][:, 0:kt, 0:ncols], src, writes=[bf])
                slab_cast_buf[(name, idx)] = bf

    P.op("pool", f_memset(ident[:], 0.0), writes=[const_buf])
    P.op("pool", lambda e: e.affine_select(out=ident[:], in_=ident[:], pattern=[[-1, 128]], compare_op=ALU.not_equal,
                                           fill=1.0, base=0, channel_multiplier=1), writes=[const_buf])
    P.op("pool", f_memset(identf[:], 0.0), writes=[const_buf])
    P.op("pool", lambda e: e.affine_select(out=identf[:], in_=identf[:], pattern=[[-1, 128]], compare_op=ALU.not_equal,
                                           fill=1.0, base=0, channel_multiplier=1), writes=[const_buf])
    P.op("pool", f_memset(ones[:], 1.0), writes=[const_buf])
    P.op("pool", f_memset(cst[:, 0:1], -0.5), writes=[const_buf])
    P.op("pool", f_memset(cst[:, 1:2], EPS), writes=[const_buf])
    P.op("pool", f_memset(Vext[:], 1.0), writes=[V_buf[0], V_buf[1]])
    P.op("pool", f_memset(uT[:], 0.0), writes=[uT_buf])

    cast_group(["w_in"])
    for i in range(4):
        P.dma("sp", f"x1{i}", xres[1][:, i, :], xh[i * 128:(i + 1) * 128, :], writes=[xres_buf[1][i]])
    for i in range(4):
        P.dma("sp", f"x0{i}", xres[0][:, i, :], xp[i * 128:(i + 1) * 128, :], writes=[xres_buf[0][i]])
    P.dma("sp", "setup", vflag[:], vflag_d, writes=[vflag_buf])
    P.dma("sp", "setup", cvec_s[0:34, 0:512], cvec_d, writes=[cvs_buf])
    P.dma("sp", "setup", frow[0:8, 0:256], rb_d[:, 1:257], writes=[frow_buf])
    setup_tok = ("setup", P.dval["setup"])
    vflag_buf.w = cvs_buf.w = frow_buf.w = setup_tok
    P.op("dve", f_copy(frow[0:8, 256:768], frow[0:8, 255:256].to_broadcast([8, 512])), reads=[frow_buf], writes=[frow_buf])
    for h in range(8):
        P.dma("pool", "per", per.ap()[h:h + 1], frow[h:h + 1, 0:768].unsqueeze(1).to_broadcast([1, 129, 768]),
              reads=[frow_buf], writes=[])
    per_buf.w = ("per", P.dval["per"])

    b = bank()
    for c in range(4):
        P.op("pe", f_tr(ps[:, b, c * 34:(c + 1) * 34], cvec_s[0:34, c * 128:(c + 1) * 128], identf[0:34, 0:34]),
             reads=[cvs_buf, const_buf], writes=[bank_buf[b]], signal=(c == 3))
    P.op("dve", f_copy(cvec[:].rearrange("p c j -> p (c j)"), ps[:, b, 0:136]), reads=[bank_buf[b]], writes=[cvec_buf])
    P.op("dve", f_ts1(cvec[:, :, 0:31], cvec[:, :, 0:31], 0.5, ALU.mult), reads=[cvec_buf], writes=[cvec_buf])

    def build_btab():
        for si, t in enumerate((0, 3, 4)):
            src = bass.AP(per, 639 - 128 * t, [[767, 128], [129 * 768, 8], [1, 128]])
            P.dma("sp", "btab", Btab[:, si], src, reads=[per_buf], writes=[])
        Btab_buf.w = ("btab", P.dval["btab"])
        P.op("pool", f_memset(Btab[0:64, 0, :, 64:128], NEG), reads=[], writes=[Btab_buf])
        P.op("pool", f_memset(Btab[64:128, 2, :, 0:64], NEG), reads=[], writes=[Btab_buf])

    def aslist(b):
        return list(b) if isinstance(b, (list, tuple)) else [b]

    def sm_cols(n=8):
        g = rot("sm", 8)
        return sm[:, g * 8:g * 8 + n], sm_buf[g], g

    def load_g(name):
        i = rot("g", 2)
        P.dma("sp", f"g{i}", gbuf[i][:], gv[name].to_broadcast([128, D]), writes=[gbuf_buf[i]])
        return gbuf[i], gbuf_buf[i]

    def rstd_from_ss(col_ap, nr, smb, n_terms):
        if n_terms == 2:
            P.op("pool", f_tt(col_ap[0:nr, 0:1], col_ap[0:nr, 0:1], col_ap[0:nr, 1:2], ALU.add), reads=[smb], writes=[smb])
        P.op("pool", f_ts(col_ap[0:nr, 3:4], col_ap[0:nr, 0:1], 1.0 / D, EPS, ALU.mult, ALU.add), reads=[smb], writes=[smb])
        P.op("pool", f_tt(col_ap[0:nr, 2:3], col_ap[0:nr, 3:4], cst[0:nr, 0:1], ALU.pow), reads=[smb, const_buf], writes=[smb])

    def norm_begin(srcs, g_name, n_early=2, bufs=None):
        g_t, g_b = load_g(g_name)
        info = []
        for (src_ap, src_buf, nr, col0) in srcs:
            cols, smb, _ = sm_cols()
            if bufs is None:
                hi = rot("hbf", 2)
                hb_ap, hb_buf = hbf[hi], hbf_buf[hi]
            else:
                hb_ap, hb_buf = bufs[len(info) % len(bufs)]
            P.op("act", f_act(hb_ap[0:nr, :], src_ap, AF.Square, accum=cols[0:nr, 0:1]), reads=[src_buf], writes=[smb, hb_buf])
            info.append([cols, smb, (hb_ap, hb_buf), False])
        for (src_ap, src_buf, nr, col0), (cols, smb, hi, _) in zip(srcs, info):
            rstd_from_ss(cols, nr, smb, 1)
        stt = {"srcs": srcs, "info": info, "g": (g_t, g_b)}
        for i in range(min(n_early, len(srcs))):
            norm_h(stt, i)
        return stt

    def norm_h(stt, i):
        (src_ap, src_buf, nr, col0) = stt["srcs"][i]
        cols, smb, hi, done = stt["info"][i]
        if done:
            return
        g_t, g_b = stt["g"]
        hb_ap, hb_buf = hi
        P.op("dve", f_stt(hb_ap[0:nr, :], src_ap, cols[0:nr, 2:3], g_t[0:nr, :], ALU.mult, ALU.mult),
             reads=[src_buf, smb, g_b], writes=[hb_buf])
        stt["info"][i][3] = True

    def norm_finish(stt, dstT, dstT_buf):
        for i, (src_ap, src_buf, nr, col0) in enumerate(stt["srcs"]):
            norm_h(stt, i)
            cols, smb, (hb_ap, hb_buf), _ = stt["info"][i]
            b = bank()
            pv = bk_bf(b, 8, 128)
            for k in range(8):
                P.op("pe", f_tr(pv[:, k, 0:nr], hb_ap[0:nr, k * 128:(k + 1) * 128], ident[0:nr, 0:nr]),
                     reads=[hb_buf, const_buf], writes=[bank_buf[b]], signal=(k == 7))
            P.op("act", f_act(dstT[:, :, col0:col0 + nr], pv[:, :, 0:nr], AF.Copy), reads=[bank_buf[b]], writes=aslist(dstT_buf))

    def row_buffers():
        return [(hbf[0], hbf_buf[0]), (hbf[1], hbf_buf[1]),
                (pm[0][:].rearrange("p a b -> p (a b)"), pm_buf[0]), (pm[1][:].rearrange("p a b -> p (a b)"), pm_buf[1])]

    def norm_tiles(srcs, g_name, dstT, dstT_buf, wide=False):
        if wide:
            norm_finish(norm_begin(srcs, g_name, n_early=4, bufs=row_buffers()), dstT, dstT_buf)
        else:
            norm_finish(norm_begin(srcs, g_name, n_early=0), dstT, dstT_buf)

    def proj_fm(slab, slab_b, ncol_tiles, src, src_buf, N, evac):
        for f in range(ncol_tiles):
            b = bank()
            for k in range(8):
                P.op("pe", f_mm(ps[:, b, 0:N], slab[:, k, f * 128:(f + 1) * 128], src[:, k, 0:N], k == 0, k == 7),
                     reads=[slab_b] + aslist(src_buf), writes=[bank_buf[b]], signal=(k == 7))
            evac(f, b)

    def proj_tm(slab, slab_b, src, src_buf, c0, nr, b, kt=8, k0=0, start=True, stop=True, ncols=512):
        for k in range(kt):
            P.op("pe", f_mm(ps[0:nr, b, 0:ncols], src[:, k0 + k, c0:c0 + nr], slab[:, k, 0:ncols],
                            start and k == 0, stop and k == kt - 1),
                 reads=[slab_b] + aslist(src_buf), writes=[bank_buf[b]], signal=(stop and k == kt - 1))

    def store_rows(dst_ap, src_bank, nr, ncols, scale=None, col0=0, extra=()):
        si = rot("stage", 1)
        if scale is None:
            P.op("act", f_act(stage[si][0:nr, 0:ncols], ps[0:nr, src_bank, col0:col0 + ncols], AF.Copy),
                 reads=[bank_buf[src_bank]] + list(extra), writes=[stage_buf[si]])
        else:
            P.op("act", f_act(stage[si][0:nr, 0:ncols], ps[0:nr, src_bank, col0:col0 + ncols], AF.Copy, scale=scale),
                 reads=[bank_buf[src_bank]] + list(extra), writes=[stage_buf[si]])
        P.dma("sp", f"stg{si}", dst_ap, stage[si][0:nr, 0:ncols], reads=[stage_buf[si]])

    class PostNorm:
        def __init__(self, g_name):
            self.g_t, self.g_b = load_g(g_name)
            self.pend = None

        def add(self, b0, b1, nr, x_ap, x_buf, after=None):
            g_t, g_b = self.g_t, self.g_b
            cols, smb, _ = sm_cols()
            ti = rot("tmp", 2)
            for n, bb in enumerate((b0, b1)):
                P.op("act", f_act(tmp[ti][0:nr, n * 512:(n + 1) * 512], ps[0:nr, bb, :], AF.Square, accum=cols[0:nr, n:n + 1]),
                     reads=[bank_buf[bb]], writes=[smb, tmp_buf[ti]])
            for n, bb in enumerate((b0, b1)):
                P.op("dve", f_tt(tmp[ti][0:nr, n * 512:(n + 1) * 512], ps[0:nr, bb, :], g_t[0:nr, n * 512:(n + 1) * 512], ALU.mult),
                     reads=[bank_buf[bb], smb, g_b], writes=[tmp_buf[ti]])
            rstd_from_ss(cols, nr, smb, 2)

            def xupd():
                P.op("dve", f_stt(x_ap, tmp[ti][0:nr, :], cols[0:nr, 2:3], x_ap, ALU.mult, ALU.add),
                     reads=[tmp_buf[ti], smb, x_buf], writes=[x_buf])
                if after is not None:
                    after()
            if self.pend is not None:
                self.pend()
            self.pend = xupd

        def finish(self):
            if self.pend is not None:
                self.pend()
            self.pend = None

    pending = []

    def drain(n):
        k = 0
        while pending and k < n:
            pending.pop(0)()
            k += 1

    def hslot():
        j = rot("hslot", 11)
        ap = hid[:, 2 * j:2 * j + 2, :].rearrange("p a b -> p (a b)").bitcast(F32)
        return ap, [hid_buf[2 * j], hid_buf[2 * j + 1]], f"hst{j}"

    SLOT = {1: 0, 2: 1, 0: 2, 3: 3, 4: 4}

    def attention_block(segs, dstT, dstT_buf):
        obs = {}

        def s_stage(si, h):
            q0, nq, ktiles = segs[si]
            base = 64 * (h % 2)
            f = h // 2
            sbk = bank(2)
            assert sbk[1] == sbk[0] + 1
            spv = ps[:, sbk[0]:sbk[0] + 2, :].rearrange("p b n -> p (b n)")
            sbufs = [bank_buf[sbk[0]], bank_buf[sbk[1]]]
            for t in range(5):
                kcol, nk, vt, kb, vb = ktiles[t]
                s = SLOT[t]
                P.op("pe", f_mm(spv[0:nk, s * 128:s * 128 + nq], kT[base:base + 64, f, kcol:kcol + nk],
                                qT[base:base + 64, f, q0:q0 + nq], True, True),
                     reads=[kb, qT_buf], writes=sbufs, signal=(t == 4))
            ti = rot("tb", 3)
            s3 = spv[:, 256:640].rearrange("p (s n) -> p s n", s=3)[:, :, 0:nq]
            P.op("dve", f_tt(tb[ti][:, :, 0:nq], s3, Btab[:, :, h, 0:nq], ALU.add),
                 reads=sbufs + [Btab_buf], writes=[tb_buf[ti]])
            s2 = spv[:, 0:256].rearrange("p (s n) -> p s n", s=2)[:, :, 0:nq]
            P.op("act", f_act(pT[ti][:, 0:2, 0:nq], s2, AF.Exp, bias=Btab[:, 0, h, 0:1]),
                 reads=sbufs + [Btab_buf, tb_buf[ti]], writes=[pT_buf[ti]])
            P.op("act", f_act(pT[ti][:, 2:5, 0:nq], tb[ti][:, :, 0:nq], AF.Exp), reads=[tb_buf[ti]], writes=[pT_buf[ti]])
            return ti

        def pv_stage(si, h, ti):
            q0, nq, ktiles = segs[si]
            if h == 0:
                ob = bank(2)
                st["reserved"].add(ob[0])
                st["reserved"].add(ob[1])
                obs[si] = ob
            ob = obs[si]
            o = ob[h // 4]
            hc = (h % 4) * 65
            for t in range(5):
                kcol, nk, vt, kb, vb = ktiles[t]
                P.op("pe", f_mm(ps[0:nq, o, hc:hc + 65], pT[ti][0:nk, SLOT[t], 0:nq], Vext[0:nk, vt, h, :], t == 0, t == 4),
                     reads=[pT_buf[ti], vb], writes=[bank_buf[o]], signal=(t == 4))
            if h == 7:
                cols, smb, _ = sm_cols()
                for i2 in range(2):
                    ov = ps[0:nq, ob[i2], 0:260].rearrange("p (h d) -> p h d", h=4)
                    P.op("dve", f_recip(cols[0:nq, i2 * 4:i2 * 4 + 4], ov[:, :, 64]), reads=[bank_buf[ob[i2]]], writes=[smb])
                    P.op("dve", f_tt(att[0:nq, i2 * 256:(i2 + 1) * 256].rearrange("p (h d) -> p h d", h=4), ov[:, :, 0:64],
                                     cols[0:nq, i2 * 4:i2 * 4 + 4].unsqueeze(2).to_broadcast([nq, 4, 64]), ALU.mult),
                         reads=[bank_buf[ob[i2]], smb], writes=[att_buf])
                st["reserved"].discard(ob[0])
                st["reserved"].discard(ob[1])
                b = bank()
                pv = bk_bf(b, 4, 128)
                for c in range(4):
                    P.op("pe", f_tr(pv[:, c, 0:nq], att[0:nq, c * 128:(c + 1) * 128], ident[0:nq, 0:nq]),
                         reads=[att_buf, const_buf], writes=[bank_buf[b]], signal=(c == 3))
                P.op("act", f_act(dstT[:, 0:4, q0:q0 + nq], pv[:, :, 0:nq], AF.Copy), reads=[bank_buf[b]], writes=[dstT_buf])

        fifo = []
        for si in range(len(segs)):
            for h in range(8):
                ti = s_stage(si, h)
                fifo.append((si, h, ti))
                if len(fifo) > 2:
                    pv_stage(*fifo.pop(0))
        while fifo:
            pv_stage(*fifo.pop(0))

    def mem_attention(q2T, q2T_buf, q0, nq, oT, oT_buf):
        def s_stage(h):
            pi = rot("pm", 2)
            for m in range(2):
                b = bank()
                for dk in range(2):
                    P.op("pe", f_mm(ps[:, b, 0:nq], mkT[:, 2 * h + dk, m * 128:(m + 1) * 128], q2T[:, 2 * h + dk, q0:q0 + nq],
                                    dk == 0, dk == 1), reads=[mkT_buf, q2T_buf], writes=[bank_buf[b]], signal=(dk == 1))
                P.op("act", f_act(pm[pi][:, m, 0:nq], ps[:, b, 0:nq], AF.Exp), reads=[bank_buf[b]], writes=[pm_buf[pi]])
            return pi

        def o_stage(h, pi):
            b = bank()
            for m in range(2):
                P.op("pe", f_mm(ps[:, b, 0:nq], ones[:], pm[pi][:, m, 0:nq], m == 0, m == 1),
                     reads=[pm_buf[pi], const_buf], writes=[bank_buf[b]], signal=(m == 1))
            ri = rot("rs", 3)
            P.op("dve", f_recip(rs[ri][:, 0:nq], ps[:, b, 0:nq]), reads=[bank_buf[b]], writes=[rs_buf[ri]])
            for dk in range(2):
                b = bank()
                for m in range(2):
                    c0 = 256 * h + 128 * dk
                    P.op("pe", f_mm(ps[:, b, 0:nq], mv[:, m, c0:c0 + 128], pm[pi][:, m, 0:nq], m == 0, m == 1),
                         reads=[pm_buf[pi], mv_buf], writes=[bank_buf[b]], signal=(m == 1))
                P.op("dve", f_tt(oT[:, 2 * h + dk, q0:q0 + nq], ps[:, b, 0:nq], rs[ri][:, 0:nq], ALU.mult),
                     reads=[bank_buf[b], rs_buf[ri]], writes=[oT_buf, oT_kbuf[2 * h + dk]])
        prev = None
        for h in range(4):
            pi = s_stage(h)
            if prev is not None:
                o_stage(*prev)
            prev = (h, pi)
        o_stage(*prev)

    A, B = 0, 1

    def geom(kind, bi):
        sample = kind == "sample"
        tiles = [(0, 32)] if sample else [(i * 128, 128) for i in range(4)]
        N = 32 if sample else 512
        if kind == "halo":
            cur, xpar = 1, 1
        elif sample:
            cur, xpar = 1, 0
        else:
            cur, xpar = bi % 2, bi % 2
        return sample, tiles, N, cur, xpar

    def emit_xload(kind, bi):
        if kind == "halo":
            for i in range(4):
                P.dma("sp", f"x1{i}", xres[1][:, i, :], xh[i * 128:(i + 1) * 128, :], writes=[xres_buf[1][i]])
        elif kind == "sample":
            P.dma("sp", "x00", xres[0][0:32, 0, :], xs_d[:, :], writes=[xres_buf[0][0]])
        else:
            xpar = bi % 2
            for i in range(4):
                P.dma("sp", f"x{xpar}{i}", xres[xpar][:, i, :], xp[bi * NB + i * 128: bi * NB + (i + 1) * 128, :],
                      writes=[xres_buf[xpar][i]])

    def glu(kind, bi, src=None):
        sample, tiles, N, cur, xpar = geom(kind, bi)
        hT, hTb = src if src is not None else (actT[A], actT_buf[A])
        slab_v, sbf_v = use_slab("w_in", 3)
        slab_g, sbf_g = use_slab("w_in", 4)
        for c in range(4):
            bv = bank()
            for k in range(8):
                P.op("pe", f_mm(ps[:, bv, 0:N], slab_v[:, k, c * 128:(c + 1) * 128], hT[:, k, 0:N], k == 0, k == 7),
                     reads=[sbf_v] + aslist(hTb), writes=[bank_buf[bv]], signal=(k == 7))
            bg = bank()
            for k in range(8):
                P.op("pe", f_mm(ps[:, bg, 0:N], slab_g[:, k, c * 128:(c + 1) * 128], hT[:, k, 0:N], k == 0, k == 7),
                     reads=[sbf_g] + aslist(hTb), writes=[bank_buf[bg]], signal=(k == 7))
            gi = rot("sg", 2)
            P.op("act", f_act(sg[gi][:, 0:N], ps[:, bg, 0:N], AF.Tanh, scale=0.5), reads=[bank_buf[bg]], writes=[sg_buf[gi]])
            if sample:
                dst = uTs[:, c, :, 30:46]
                srcs = sg[gi][:, 0:32].rearrange("p (s t) -> p s t", s=2)
                srcv = ps[:, bv, 0:32].rearrange("p (s t) -> p s t", s=2)
                P.op("dve", f_stt(dst, srcs, 1.0, srcv, ALU.add, ALU.mult), reads=[sg_buf[gi], bank_buf[bv]], writes=[uTs_buf])
            else:
                P.op("dve", f_stt(uT[:, c, 30:30 + N], sg[gi][:, 0:N], 1.0, ps[:, bv, 0:N], ALU.add, ALU.mult),
                     reads=[sg_buf[gi], bank_buf[bv]], writes=[uT_buf])

    def queue_conv(kind, bi):
        sample, tiles, N, cur, xpar = geom(kind, bi)
        for j in range(31):
            for c in range(4):
                if sample:
                    src = uTs[:, c, :, j:j + 16]
                    dstc = cacc[:, c, 0:32].rearrange("p (s t) -> p s t", s=2)
                    ub = uTs_buf
                else:
                    src = uT[:, c, j:j + N]
                    dstc = cacc[:, c, 0:N]
                    ub = uT_buf
                if j == 0:
                    pending.append(lambda src=src, dstc=dstc, ub=ub, c=c: P.op(
                        "dve", f_ts(dstc, src, cvec[:, c, 0:1], cvec[:, c, 31:32], ALU.mult, ALU.add),
                        reads=[ub, cvec_buf], writes=[cacc_buf[c]]))
                else:
                    pending.append(lambda src=src, dstc=dstc, ub=ub, c=c, j=j: P.op(
                        "dve", f_stt(dstc, src, cvec[:, c, j:j + 1], dstc, ALU.mult, ALU.add),
                        reads=[ub, cvec_buf], writes=[cacc_buf[c]]))

    def conv_tail_out(kind, bi):
        sample, tiles, N, cur, xpar = geom(kind, bi)
        nseg = 2 if sample else 1
        for s in range(nseg):
            b = bank()
            nrow = 16 if sample else 30
            for c in range(4):
                srcu = uTs[:, c, s, 30:46] if sample else uT[:, c, N:N + 30]
                P.op("pe", f_tr(ps[0:nrow, b, c * 128:(c + 1) * 128], srcu, identf[:, :]),
                     reads=[uTs_buf if sample else uT_buf, const_buf], writes=[bank_buf[b]], signal=(c == 3))
            store_rows(cso_d[s, 14:30, :] if sample else ctail_d[:, :], b, nrow, 512, scale=0.5)

    H1 = hid[:, 0:8, :]
    H1b = [hid_buf[f] for f in range(8)]

    def pre_begin(kind, bi):
        sample, tiles, N, cur, xpar = geom(kind, bi)
        xr, xb = xres[xpar], xres_buf[xpar]
        return norm_begin([(xr[0:nr, i, :], xb[i], nr, r0) for i, (r0, nr) in enumerate(tiles)], "g_mix_pre",
                          n_early=4, bufs=row_buffers())

    def pre1(kind, bi, stt):
        sample, tiles, N, cur, xpar = geom(kind, bi)
        last = (kind == "prompt" and bi == NBLK - 1)
        norm_finish(stt, H1, H1b)
        slab, sbf = use_slab("w_in", 0)
        proj_fm(slab, sbf, 4, H1, H1b, N, lambda f, b: P.op(
            "act", f_act(qT[:, f, 0:N], ps[:, b, 0:N], AF.Copy, scale=0.125), reads=[bank_buf[b]], writes=[qT_buf]))
        slab, sbf = use_slab("w_in", 1)
        kc0 = cur * 512
        proj_fm(slab, sbf, 4, H1, H1b, N, lambda f, b: P.op(
            "act", f_act(kT[:, f, kc0:kc0 + N], ps[:, b, 0:N], AF.Copy), reads=[bank_buf[b]], writes=[kT_buf[cur]]))
        if last:
            for i, (r0, nr) in enumerate(tiles):
                b = bank()
                proj_tm(slab, sbf, H1, H1b, r0, nr, b)
                store_rows(ktail_d[r0:r0 + nr, :], b, nr, 512)
        if sample:
            for s in range(2):
                b = bank()
                proj_tm(slab, sbf, H1, H1b, s * 16, 16, b)
                store_rows(kso_d[s, 496:512, :], b, 16, 512)

    def pre2(kind, bi):
        sample, tiles, N, cur, xpar = geom(kind, bi)
        last = (kind == "prompt" and bi == NBLK - 1)
        slab, sbf = use_slab("w_in", 2)
        if kind == "prompt" and bi == 1:
            P.op("pool", f_memset(Vext[:, 4 * cur:4 * cur + 4, :, 64:65], 1.0), writes=[V_buf[cur]])
        vsegs = [(s * 16, 16, 4 * cur + s) for s in range(2)] if sample else [(r0, nr, 4 * cur + i) for i, (r0, nr) in enumerate(tiles)]
        for si_, (c0, nr, vt) in enumerate(vsegs):
            b = bank()
            proj_tm(slab, sbf, H1, H1b, c0, nr, b)
            P.op("act", f_act(Vext[0:nr, vt, :, 0:64], ps[0:nr, b, :].rearrange("p (h d) -> p h d", h=8), AF.Copy),
                 reads=[bank_buf[b]], writes=[V_buf[cur]])
            if last:
                store_rows(vtail_d[c0:c0 + nr, :], b, nr, 512)
            if sample:
                store_rows(vso_d[si_, 496:512, :], b, 16, 512)
        if not sample:
            P.op("dve", f_copy(uT[:, :, 0:30], uT[:, :, 512:542]), reads=[uT_buf], writes=[uT_buf])
        glu(kind, bi, src=(H1, H1b))
        if sample or last:
            conv_tail_out(kind, bi)
        queue_conv(kind, bi)

    def halo_a():
        sample, tiles, N, cur, xpar = geom("halo", 0)
        xr, xb = xres[xpar], xres_buf[xpar]
        norm_tiles([(xr[0:nr, i, :], xb[i], nr, r0) for i, (r0, nr) in enumerate(tiles)], "g_mix_pre", actT[B], actT_buf[B])
        P.op("dve", f_copy(Vext[:, 4 * cur:4 * cur + 4, :, 64:65].rearrange("p t h o -> p (t h o)"),
                           vflag[:, 0:1].to_broadcast([128, 32])), reads=[vflag_buf], writes=[V_buf[cur]])
        glu("halo", 0, src=(actT[B], actT_buf[B]))

    def halo_b():
        sample, tiles, N, cur, xpar = geom("halo", 0)
        hT, hTb = actT[B], actT_buf[B]
        slab, sbf = use_slab("w_in", 1)
        kc0 = cur * 512
        proj_fm(slab, sbf, 4, hT, hTb, N, lambda f, b: P.op(
            "act", f_act(kT[:, f, kc0:kc0 + N], ps[:, b, 0:N], AF.Copy), reads=[bank_buf[b]], writes=[kT_buf[cur]]))
        slab, sbf = use_slab("w_in", 2)
        for i, (r0, nr) in enumerate(tiles):
            b = bank()
            proj_tm(slab, sbf, hT, hTb, r0, nr, b)
            P.op("act", f_act(Vext[0:nr, 4 * cur + i, :, 0:64], ps[0:nr, b, :].rearrange("p (h d) -> p h d", h=8), AF.Copy),
                 reads=[bank_buf[b]], writes=[V_buf[cur]])

    def main_stage(kind, bi, next_pre=None, post_b_hook=None, mid_hook=None, next_begin=None, next_mid=None):
        sample, tiles, N, cur, xpar = geom(kind, bi)
        last = (kind == "prompt" and bi == NBLK - 1)
        prev = 1 - cur
        xr, xb = xres[xpar], xres_buf[xpar]
        drain(10 ** 6)
        mixT, mixTb = actT[B], actT_buf[B]
        lninfo = []
        for i, (r0, nr) in enumerate(tiles):
            b = bank()
            for c in range(4):
                P.op("pe", f_tr(ps[0:nr, b, c * 128:(c + 1) * 128], cacc[:, c, r0:r0 + nr], identf[:, :]),
                     reads=[cacc_buf[c], const_buf], writes=[bank_buf[b]], signal=(c == 3))
            cols, smb, _ = sm_cols()
            P.op("dve", lambda e, nr=nr, b=b, i=i: e.bn_stats(out=bnst[0:nr, i, 0:6], in_=ps[0:nr, b, :]),
                 reads=[bank_buf[b]], writes=[smb])
            P.op("dve", lambda e, nr=nr, cols=cols, i=i: e.bn_aggr(out=cols[0:nr, 4:6], in_=bnst[0:nr, i, 0:6]),
                 reads=[smb], writes=[smb])
            lninfo.append((b, cols, smb))
        for i, (r0, nr) in enumerate(tiles):
            b, cols, smb = lninfo[i]
            P.op("pool", f_ts1(cols[0:nr, 3:4], cols[0:nr, 5:6], EPS, ALU.add), reads=[smb], writes=[smb])
            P.op("pool", f_tt(cols[0:nr, 2:3], cols[0:nr, 3:4], cst[0:nr, 0:1], ALU.pow), reads=[smb, const_buf], writes=[smb])
        for i, (r0, nr) in enumerate(tiles):
            b, cols, smb = lninfo[i]
            P.op("dve", f_stt(cols[0:nr, 6:7], cols[0:nr, 4:5], -1.0, cols[0:nr, 2:3], ALU.mult, ALU.mult), reads=[smb], writes=[smb])
            P.op("act", f_act(zc[i][0:nr, :], ps[0:nr, b, :], AF.Identity, scale=cols[0:nr, 2:3], bias=cols[0:nr, 6:7]),
                 reads=[bank_buf[b], smb], writes=[zc_buf[i]])

        def c2b():
            cb = bank(2)
            st["reserved"].add(cb[0])
            st["reserved"].add(cb[1])
            for i, (r0, nr) in enumerate(tiles):
                for c in range(4):
                    pvc = ps[:, cb[c // 2], :].bitcast(BF16)[:, (c % 2) * 512:(c % 2) * 512 + 512]
                    P.op("pe", f_tr(pvc[:, r0:r0 + nr], zc[i][0:nr, c * 128:(c + 1) * 128], ident[0:nr, 0:nr]),
                         reads=[zc_buf[i], const_buf], writes=[bank_buf[cb[c // 2]]], signal=(c == 3))
            for c in range(4):
                pvc = ps[:, cb[c // 2], :].bitcast(BF16)[:, (c % 2) * 512:(c % 2) * 512 + 512]
                P.op("act", f_act(mixT[:, 4 + c, 0:N], pvc[:, 0:N], AF.Silu, scale=cvec[:, c, 32:33], bias=cvec[:, c, 33:34]),
                     reads=[bank_buf[cb[c // 2]], cvec_buf], writes=[mixTb])
            st["reserved"].discard(cb[0])
            st["reserved"].discard(cb[1])

        if sample:
            c2b()
            for s in range(2):
                for t in range(4):
                    sap, sbufs_, ssem = hslot()
                    P.dma("sp", ssem, sap, ck_d[s, t * 128:(t + 1) * 128, :], writes=sbufs_)
                    hi = rot("hbf", 2)
                    P.op("dve", f_copy(hbf[hi][:, 0:512], sap), reads=sbufs_, writes=[hbf_buf[hi]])
                    b = bank()
                    pv = bk_bf(b, 4, 128)
                    for f in range(4):
                        P.op("pe", f_tr(pv[:, f, :], hbf[hi][:, f * 128:(f + 1) * 128], ident[:, :]),
                             reads=[hbf_buf[hi], const_buf], writes=[bank_buf[b]], signal=(f == 3))
                    P.op("act", f_act(kT[:, :, prev * 512 + t * 128: prev * 512 + (t + 1) * 128], pv[:, :, :], AF.Copy),
                         reads=[bank_buf[b]], writes=[kT_buf[prev]])
                    sap, sbufs_, ssem = hslot()
                    P.dma("sp", ssem, sap, cv_d[s, t * 128:(t + 1) * 128, :], writes=sbufs_)
                    P.op("dve", f_copy(Vext[:, 4 * prev + t, :, 0:64], sap.rearrange("p (h d) -> p h d", h=8)),
                         reads=sbufs_, writes=[V_buf[prev]])
                kt_l = [(prev * 512 + t * 128, 128, 4 * prev + t, kT_buf[prev], V_buf[prev]) for t in range(4)]
                kt_l.append((cur * 512 + s * 16, 16, 4 * cur + s, kT_buf[cur], V_buf[cur]))
                attention_block([(s * 16, 16, kt_l)], mixT, mixTb)
        else:
            segs = []
            for j in range(4):
                kt_l = []
                for t in range(5):
                    gt = j + t
                    half = prev if gt < 4 else cur
                    kt_l.append((half * 512 + (gt % 4) * 128, 128, 4 * half + gt % 4, kT_buf[half], V_buf[half]))
                segs.append((j * 128, 128, kt_l))
            attention_block(segs, mixT, mixTb)

        if not sample:
            c2b()

        nstate = next_begin() if next_begin is not None else None
        s0, s0b = use_slab("w_out", 0)
        s1, s1b = use_slab("w_out", 1)
        pn = PostNorm("g_mix_post")
        for i, (r0, nr) in enumerate(tiles):
            b0 = bank()
            proj_tm(s0, s0b, mixT, mixTb, r0, nr, b0)
            b1 = bank()
            proj_tm(s1, s1b, mixT, mixTb, r0, nr, b1)
            pn.add(b0, b1, nr, xr[0:nr, i, :], xb[i])
        pn.finish()
        if next_mid is not None:
            next_mid(nstate)

        if mid_hook is not None:
            mid_hook()
        norm_tiles([(xr[0:nr, i, :], xb[i], nr, r0) for i, (r0, nr) in enumerate(tiles)], "g_mem_pre", actT[A], actT_buf[A], wide=not sample)
        q2T, q2Tb = actT[B], actT_buf[B]
        for n in range(2):
            slab, sbf = use_slab("w_mq", n)
            proj_fm(slab, sbf, 4, actT[A], actT_buf[A], N, lambda f, b, n=n: P.op(
                "act", f_act(q2T[:, 4 * n + f, 0:N], ps[:, b, 0:N], AF.Copy, scale=1.0 / 16.0), reads=[bank_buf[b]], writes=[q2Tb]))
        oT, oTb = actT[A], actT_buf[A]
        if sample:
            for s in range(2):
                for m in range(2):
                    for src_d, is_k in ((cmk_d, True), (cmv_d, False)):
                        for hf in range(2):
                            sap, sbufs_, ssem = hslot()
                            P.dma("sp", ssem, sap, src_d[s, m * 128:(m + 1) * 128, hf * 512:(hf + 1) * 512], writes=sbufs_)
                            if is_k:
                                hi = rot("hbf", 2)
                                P.op("dve", f_copy(hbf[hi][:, 0:512], sap), reads=sbufs_, writes=[hbf_buf[hi]])
                                b = bank()
                                pv = bk_bf(b, 4, 128)
                                for f in range(4):
                                    P.op("pe", f_tr(pv[:, f, :], hbf[hi][:, f * 128:(f + 1) * 128], ident[:, :]),
                                         reads=[hbf_buf[hi], const_buf], writes=[bank_buf[b]], signal=(f == 3))
                                P.op("act", f_act(mkT[:, 4 * hf:4 * hf + 4, m * 128:(m + 1) * 128], pv[:, :, :], AF.Copy),
                                     reads=[bank_buf[b]], writes=[mkT_buf])
                            else:
                                P.op("dve", f_copy(mv[:, m, hf * 512:(hf + 1) * 512], sap),
                                     reads=sbufs_, writes=[mv_buf])
                mem_attention(q2T, q2Tb, s * 16, 16, oT, oTb)
        else:
            mem_attention(q2T, q2Tb, 0, N, oT, oTb)
        s0, s0b = use_slab("w_mo", 0)
        s1, s1b = use_slab("w_mo", 1)
        pn = PostNorm("g_mem_post")
        mob = {}
        for i in range(len(tiles)):
            for n in range(2):
                b = bank()
                mob[(i, n)] = b
                st["reserved"].add(b)
        for khalf in range(2):
            for i, (r0, nr) in enumerate(tiles):
                for n, (sl, slb) in enumerate(((s0, s0b), (s1, s1b))):
                    b = mob[(i, n)]
                    for k in range(4 * khalf, 4 * khalf + 4):
                        P.op("pe", f_mm(ps[0:nr, b, :], oT[:, k, r0:r0 + nr], sl[:, k, :], k == 0, k == 7),
                             reads=[slb, oT_kbuf[k]], touch=[oTb], writes=[bank_buf[b]], signal=(k % 4 == 3))
        for i, (r0, nr) in enumerate(tiles):
            pn.add(mob[(i, 0)], mob[(i, 1)], nr, xr[0:nr, i, :], xb[i])
            st["reserved"].discard(mob[(i, 0)])
            st["reserved"].discard(mob[(i, 1)])
        pn.finish()

        if next_pre is not None:
            next_pre()

        norm_tiles([(xr[0:nr, i, :], xb[i], nr, r0) for i, (r0, nr) in enumerate(tiles)], "g_ffn_pre", actT[B], actT_buf[B], wide=not sample)
        h3, h3b = actT[B], actT_buf[B]
        for s in range(6):
            sg_, sgb = use_slab("w_gate", s)
            su_, sub = use_slab("w_up", s)
            nf = 4 if s < 5 else 2
            for ff in range(nf):
                f = s * 4 + ff
                bg = bank()
                for k in range(8):
                    P.op("pe", f_mm(ps[:, bg, 0:N], sg_[:, k, ff * 128:(ff + 1) * 128], h3[:, k, 0:N], k == 0, k == 7),
                         reads=[sgb, h3b], writes=[bank_buf[bg]], signal=(k == 7))
                bu = bank()
                for k in range(8):
                    P.op("pe", f_mm(ps[:, bu, 0:N], su_[:, k, ff * 128:(ff + 1) * 128], h3[:, k, 0:N], k == 0, k == 7),
                         reads=[sub, h3b], writes=[bank_buf[bu]], signal=(k == 7))
                gi = rot("sg", 2)
                P.op("act", f_act(sg[gi][:, 0:N], ps[:, bg, 0:N], AF.Silu), reads=[bank_buf[bg]], writes=[sg_buf[gi]])
                P.op("dve", f_tt(hid[:, f, 0:N], sg[gi][:, 0:N], ps[:, bu, 0:N], ALU.mult),
                     reads=[sg_buf[gi], bank_buf[bu]], writes=[hid_buf[f]])
                drain(4)
        dbanks = {}
        for n in range(2):
            for i in range(len(tiles)):
                b = bank()
                dbanks[(i, n)] = b
                st["reserved"].add(b)
        for n in range(2):
            for ks in range(3):
                slab, sbf = use_slab("w_down", n * 3 + ks)
                kt = (8, 8, 6)[ks]
                for i, (r0, nr) in enumerate(tiles):
                    b = dbanks[(i, n)]
                    for k in range(kt):
                        fidx = ks * 8 + k
                        last_mm = (ks == 2 and k == kt - 1)
                        P.op("pe", f_mm(ps[0:nr, b, :], hid[:, fidx, r0:r0 + nr], slab[:, k, :], ks == 0 and k == 0, last_mm),
                             reads=[sbf, hid_buf[fidx]], writes=[bank_buf[b]], signal=(k == kt - 1))
                drain(8)
        drain(10 ** 6)
        pn = PostNorm("g_ffn_post")
        for i, (r0, nr) in enumerate(tiles):
            if sample:
                dsty = ys_d[r0:r0 + nr, :]
            else:
                dsty = y_d[bi * NB + r0: bi * NB + r0 + nr, :]
            pn.add(dbanks[(i, 0)], dbanks[(i, 1)], nr, xr[0:nr, i, :], xb[i],
                   after=lambda i=i, nr=nr, dsty=dsty: P.dma("sp", f"x{xpar}{i}", dsty, xr[0:nr, i, :], reads=[xb[i]]))
            st["reserved"].discard(dbanks[(i, 0)])
            st["reserved"].discard(dbanks[(i, 1)])
        pn.finish()

    def mem_kv():
        mT, mTb = actT[1], actT_buf[1]
        srcs = []
        for m in range(2):
            si = rot("tmp", 2)
            P.dma("sp", f"tmpl{si}", tmp[si][:], memp_d[m * 128:(m + 1) * 128, :], writes=[tmp_buf[si]])
            srcs.append((tmp[si][:, :], tmp_buf[si], 128, m * 128))
        norm_tiles(srcs, "g_mem_kv", mT, mTb)
        for n in range(2):
            slab, sbf = use_slab("w_mk", n)
            proj_fm(slab, sbf, 4, mT, mTb, 256, lambda f, b, n=n: P.op(
                "act", f_act(mkT[:, 4 * n + f, :], ps[:, b, 0:256], AF.Copy), reads=[bank_buf[b]], writes=[mkT_buf]))
            for m in range(2):
                b = bank()
                proj_tm(slab, sbf, mT, mTb, m * 128, 128, b)
                store_rows(mko_d[m * 128:(m + 1) * 128, n * 512:(n + 1) * 512], b, 128, 512)
        for n in range(2):
            slab, sbf = use_slab("w_mv", n)
            for m in range(2):
                b = bank()
                proj_tm(slab, sbf, mT, mTb, m * 128, 128, b)
                P.op("dve", f_copy(mv[:, m, n * 512:(n + 1) * 512], ps[:, b, :]), reads=[bank_buf[b]], writes=[mv_buf])
                store_rows(mvo_d[m * 128:(m + 1) * 128, n * 512:(n + 1) * 512], b, 128, 512, extra=[mv_buf])

    def sample_prep():
        for s in range(2):
            si = rot("stage", 1)
            P.dma("sp", f"stg{si}", stage[si][0:30, 0:512], cc_d[s], writes=[stage_buf[si]])
            b = bank()
            for c in range(4):
                P.op("pe", f_tr(ps[:, b, c * 32:c * 32 + 30], stage[si][0:30, c * 128:(c + 1) * 128], identf[0:30, 0:30]),
                     reads=[stage_buf[si], const_buf], writes=[bank_buf[b]], signal=(c == 3))
            P.op("act", f_act(uTs[:, :, s, 0:30], ps[:, b, 0:128].rearrange("p (c j) -> p c j", c=4)[:, :, 0:30], AF.Copy, scale=2.0),
                 reads=[bank_buf[b]], writes=[uTs_buf])
            P.dma("sp", "copy", kso_d[s, 0:496, :], ck_d[s, 16:512, :])
            P.dma("sp", "copy", vso_d[s, 0:496, :], cv_d[s, 16:512, :])
            P.dma("sp", "copy", cso_d[s, 0:14, :], cc_d[s, 16:30, :])
        P.op("pool", f_memset(Vext[:, :, :, 64:65], 1.0), writes=[V_buf[0], V_buf[1]])
        emit_xload("sample", 0)

    import os
    kstop = int(os.environ.get("KSTOP", "99"))
    steps = []
    gate_dummy = Buf("gate_dummy")

    def gated_casts(names, gate_bufs):
        P.op("pool", f_memset(cst[:, 2:3], 0.0), reads=list(gate_bufs), writes=[gate_dummy])
        cast_group(names)
    def queue_late_casts():
        cast_pending.extend([("w_mq", 0), ("w_mq", 1), ("w_mo", 0), ("w_mo", 1)])
        for i in range(6):
            cast_pending.extend([("w_gate", i), ("w_up", i)])
        cast_pending.extend([("w_down", i) for i in range(6)])
    steps.append(halo_a)
    steps.append(lambda: pre1("prompt", 0, pre_begin("prompt", 0)))
    steps.append(lambda: (pre2("prompt", 0), drain(10 ** 6)))
    steps.append(lambda: (gated_casts(["w_out", "w_mk", "w_mv"], [uT_buf]), halo_b(),
                          build_btab(), queue_late_casts()))
    for bi in range(NBLK):
        if bi + 1 < NBLK:
            nbeg = (lambda b=bi: (emit_xload("prompt", b + 1), pre_begin("prompt", b + 1))[1])
            nmid = (lambda stt, b=bi: pre1("prompt", b + 1, stt))
            nxt = (lambda b=bi: pre2("prompt", b + 1))
        else:
            nbeg = (lambda: (sample_prep(), pre_begin("sample", 0))[1])
            nmid = (lambda stt: pre1("sample", 0, stt))
            nxt = (lambda: pre2("sample", 0))
        if bi == 0:
            steps.append(lambda nxt=nxt, nbeg=nbeg, nmid=nmid: main_stage("prompt", 0, nxt, mid_hook=mem_kv, next_begin=nbeg, next_mid=nmid))
        else:
            steps.append(lambda bi=bi, nxt=nxt, nbeg=nbeg, nmid=nmid: main_stage("prompt", bi, nxt, next_begin=nbeg, next_mid=nmid))
    steps.append(lambda: main_stage("sample", 0, None))
    for i, stp in enumerate(steps):
        if i >= kstop:
            break
        stp()
    if kstop >= len(steps):
        assert st["slab_pos"] == len(sched), (st["slab_pos"], len(sched))
        assert not pending

    sem_names = ["act", "dve", "pool", "pe"] + sorted(P.dval.keys())
    sems = {n: es.enter_context(nc.semaphore("s_" + n)) for n in sem_names}
    for n, v in P.dval.items():
        P.q["sp"].append(("wait", n, v))
    for e in ("act", "dve", "pool", "pe"):
        P.q["sp"].append(("wait", e, P.cnt[e]))

    def replay(engname, eobj):
        for it in P.q[engname]:
            if it[0] == "wait":
                eobj.wait_ge(sems[it[1]], it[2])
            elif it[0] == "op":
                ins = it[1](eobj)
                if it[2]:
                    ins.then_inc(sems[engname], 1)
            else:
                eobj.dma_start(out=it[1], in_=it[2]).then_inc(sems[it[3]], 16)

    with nc.Block() as block:
        @block.sync
        def _(e):
            replay("sp", e)

        @block.scalar
        def _(e):
            replay("act", e)

        @block.vector
        def _(e):
            replay("dve", e)

        @block.gpsimd
        def _(e):
            replay("pool", e)

        @block.tensor
        def _(e):
            replay("pe", e)
    es.close()
    nc._prog_stats = {e: len(P.q[e]) for e in P.ENG}
    return nc


_NC_CACHE = {}


def kernel(x_prompt, x_sample, cache_att_k, cache_att_v, cache_conv, cache_mem_k, cache_mem_v, mem_prompt,
           g_mix_pre, g_mix_post, w_in, rel_bias, conv_w, conv_b, cln_g, cln_b, w_out,
           g_mem_pre, g_mem_post, g_mem_kv, w_mq, w_mk, w_mv, w_mo, g_ffn_pre, g_ffn_post, w_gate, w_up, w_down):
    f = lambda a: np.ascontiguousarray(np.asarray(a, dtype=np.float32))
    x_prompt, x_sample = f(x_prompt), f(x_sample)
    if "nc" not in _NC_CACHE:
        _NC_CACHE["nc"] = build_nc()
    nc = _NC_CACHE["nc"]
    cvec = np.concatenate([f(conv_w)[0], f(conv_b)[0][None], f(cln_g)[0][None], f(cln_b)[0][None]], axis=0)
    shared = {
        "g_mix_pre": f(g_mix_pre), "g_mix_post": f(g_mix_post), "g_mem_pre": f(g_mem_pre), "g_mem_post": f(g_mem_post),
        "g_mem_kv": f(g_mem_kv), "g_ffn_pre": f(g_ffn_pre), "g_ffn_post": f(g_ffn_post),
        "rel_bias": f(rel_bias)[0], "cvec": f(cvec),
        "w_in": f(w_in)[0], "w_out": f(w_out)[0], "w_mq": f(w_mq)[0], "w_mk": f(w_mk)[0], "w_mv": f(w_mv)[0],
        "w_mo": f(w_mo)[0], "w_gate": f(w_gate)[0], "w_up": f(w_up)[0], "w_down": f(w_down)[0],
    }
    ck, cv = f(cache_att_k)[0], f(cache_att_v)[0]
    cc, cmk, cmv = f(cache_conv)[0], f(cache_mem_k)[0], f(cache_mem_v)[0]
    memp = f(mem_prompt)
    in_maps = []
    for c in range(8):
        b, qt = c // 4, c % 4
        m = dict(shared)
        m["xp"] = x_prompt[b, qt * 4096:(qt + 1) * 4096]
        m["xh"] = x_prompt[b, qt * 4096 - 512:qt * 4096] if qt > 0 else np.zeros((512, D), np.float32)
        m["vflag"] = np.full((128, 1), 1.0 if qt > 0 else 0.0, np.float32)
        m["xs"] = x_sample[2 * c:2 * c + 2].reshape(32, D)
        m["ck"] = ck[2 * c:2 * c + 2].reshape(2, 512, 512)
        m["cv"] = cv[2 * c:2 * c + 2].reshape(2, 512, 512)
        m["cc"] = cc[2 * c:2 * c + 2]
        m["cmk"] = cmk[2 * c:2 * c + 2].reshape(2, 256, D)
        m["cmv"] = cmv[2 * c:2 * c + 2].reshape(2, 256, D)
        m["memp"] = memp[b]
        in_maps.append({k: np.ascontiguousarray(v) for k, v in m.items()})
    res = run_bass_kernel_spmd(nc, in_maps, core_ids=list(range(8)))
    R = res.results
    y_prompt = np.stack([np.concatenate([R[4 * b + q]["y"] for q in range(4)], axis=0) for b in range(2)], axis=0)
    y_sample = np.concatenate([R[c]["ys"].reshape(2, 16, D) for c in range(8)], axis=0)
    nk = np.stack([R[4 * b + 3]["ktail"].reshape(512, 8, 64) for b in range(2)], axis=0)[None]
    nv = np.stack([R[4 * b + 3]["vtail"].reshape(512, 8, 64) for b in range(2)], axis=0)[None]
    ncv = np.stack([R[4 * b + 3]["ctail"] for b in range(2)], axis=0)[None]
    nmk = np.stack([R[4 * b]["mko"].reshape(256, 4, 256) for b in range(2)], axis=0)[None]
    nmv = np.stack([R[4 * b]["mvo"].reshape(256, 4, 256) for b in range(2)], axis=0)[None]
    ks = np.concatenate([R[c]["kso"].reshape(2, 512, 8, 64) for c in range(8)], axis=0)[None]
    vs = np.concatenate([R[c]["vso"].reshape(2, 512, 8, 64) for c in range(8)], axis=0)[None]
    cs = np.concatenate([R[c]["cso"] for c in range(8)], axis=0)[None]
    out = (y_prompt, y_sample, nk, nv, ncv, nmk, nmv, ks, vs, cs)
    return tuple(np.ascontiguousarray(o, dtype=np.float32) for o in out)
```

```python
import contextlib
import numpy as np
import concourse.bass as bass
import concourse.mybir as mybir
from concourse.bass_utils import run_bass_kernel_spmd

F32 = mybir.dt.float32
BF16 = mybir.dt.bfloat16
AF = mybir.ActivationFunctionType
ALU = mybir.AluOpType

D = 1024
NBLK = 8
NB = 512
D_FF = 2816
EPS = 1e-6
NEG = -1e30
DEBUG = False


class Buf:
    __slots__ = ("name", "w", "r")

    def __init__(self, name):
        self.name = name
        self.w = None
        self.r = {}


class Prog:
    ENG = ("sp", "act", "dve", "pool", "pe")
    SAME_WAIT = {"pool", "act", "dve"}

    def __init__(self):
        self.q = {e: [] for e in self.ENG}
        self.cnt = {e: 0 for e in self.ENG}
        self.seen = {e: {} for e in self.ENG}
        self.dval = {}

    def _waits(self, eng, reads, writes):
        need = {}
        for b in reads:
            if b.w is not None:
                need[b.w[0]] = max(need.get(b.w[0], 0), b.w[1])
        for b in writes:
            if b.w is not None:
                need[b.w[0]] = max(need.get(b.w[0], 0), b.w[1])
            for k, v in b.r.items():
                need[k] = max(need.get(k, 0), v)
        for k, v in need.items():
            if k == eng and eng not in self.SAME_WAIT:
                continue
            if self.seen[eng].get(k, 0) >= v:
                continue
            self.seen[eng][k] = v
            self.q[eng].append(("wait", k, v))

    def op(self, eng, fn, reads=(), writes=(), signal=True, touch=()):
        self._waits(eng, reads, writes)
        if signal:
            self.cnt[eng] += 1
            tok = (eng, self.cnt[eng])
        else:
            tok = (eng, self.cnt[eng] + 1)
        self.q[eng].append(("op", fn, signal))
        for b in list(reads) + list(touch):
            b.r[eng] = max(b.r.get(eng, 0), tok[1])
        for b in writes:
            b.w = tok
            b.r = {}
        return tok

    def dma(self, qeng, sem, out, in_, reads=(), writes=()):
        self._waits(qeng, reads, writes)
        self.dval[sem] = self.dval.get(sem, 0) + 16
        tok = (sem, self.dval[sem])
        self.q[qeng].append(("dma", out, in_, sem))
        for b in reads:
            b.r[sem] = max(b.r.get(sem, 0), tok[1])
        for b in writes:
            b.w = tok
            b.r = {}
        return tok


def f_mm(out, lhsT, rhs, start, stop):
    return lambda e: e.matmul(out, lhsT=lhsT, rhs=rhs, start=start, stop=stop)


def f_tr(out, in_, ident):
    return lambda e: e.transpose(out=out, in_=in_, identity=ident)


def f_act(out, in_, func, scale=None, bias=None, accum=None):
    def f(e):
        kw = {}
        if scale is not None:
            kw["scale"] = scale
        if bias is not None:
            kw["bias"] = bias
        if accum is not None:
            kw["accum_out"] = accum
        return e.activation(out=out, in_=in_, func=func, **kw)
    return f


def f_tt(out, a, b, op):
    return lambda e: e.tensor_tensor(out=out, in0=a, in1=b, op=op)


def f_stt(out, a, s, b, op0, op1):
    return lambda e: e.scalar_tensor_tensor(out=out, in0=a, scalar=s, in1=b, op0=op0, op1=op1)


def f_ts(out, a, s1, s2, op0, op1):
    return lambda e: e.tensor_scalar(out=out, in0=a, scalar1=s1, scalar2=s2, op0=op0, op1=op1)


def f_ts1(out, a, s1, op0):
    return lambda e: e.tensor_scalar(out=out, in0=a, scalar1=s1, scalar2=None, op0=op0)


def f_copy(out, a):
    return lambda e: e.tensor_copy(out=out, in_=a)


def f_memset(ap, v):
    return lambda e: e.memset(ap, v)


def f_recip(out, a):
    return lambda e: e.reciprocal(out=out, in_=a)


def build_nc():
    nc = bass.Bass("TRN2", target_bir_lowering=False)
    P = Prog()
    es = contextlib.ExitStack()

    def din(name, shape):
        return nc.dram_tensor(name, list(shape), F32, kind="ExternalInput")

    def dout(name, shape):
        return nc.dram_tensor(name, list(shape), F32, kind="ExternalOutput")

    xp = din("xp", [4096, D]).ap()
    xh = din("xh", [512, D]).ap()
    vflag_d = din("vflag", [128, 1]).ap()
    xs_d = din("xs", [32, D]).ap()
    ck_d = din("ck", [2, 512, 512]).ap()
    cv_d = din("cv", [2, 512, 512]).ap()
    cc_d = din("cc", [2, 30, 512]).ap()
    cmk_d = din("cmk", [2, 256, D]).ap()
    cmv_d = din("cmv", [2, 256, D]).ap()
    memp_d = din("memp", [256, D]).ap()
    gv = {n: din(n, [1, D]).ap() for n in
          ("g_mix_pre", "g_mix_post", "g_mem_pre", "g_mem_post", "g_mem_kv", "g_ffn_pre", "g_ffn_post")}
    rb_d = din("rel_bias", [8, 257]).ap()
    cvec_d = din("cvec", [34, 512]).ap()
    W = {"w_in": din("w_in", [D, 2560]), "w_out": din("w_out", [D, D]), "w_mq": din("w_mq", [D, D]),
         "w_mk": din("w_mk", [D, D]), "w_mv": din("w_mv", [D, D]), "w_mo": din("w_mo", [D, D]),
         "w_gate": din("w_gate", [D, D_FF]), "w_up": din("w_up", [D, D_FF]), "w_down": din("w_down", [D_FF, D])}

    y_d = dout("y", [4096, D]).ap()
    ys_d = dout("ys", [32, D]).ap()
    ktail_d = dout("ktail", [512, 512]).ap()
    vtail_d = dout("vtail", [512, 512]).ap()
    ctail_d = dout("ctail", [30, 512]).ap()
    mko_d = dout("mko", [256, D]).ap()
    mvo_d = dout("mvo", [256, D]).ap()
    kso_d = dout("kso", [2, 512, 512]).ap()
    vso_d = dout("vso", [2, 512, 512]).ap()
    cso_d = dout("cso", [2, 30, 512]).ap()

    nsl = {"w_in": 5, "w_out": 2, "w_mq": 2, "w_mk": 2, "w_mv": 2, "w_mo": 2, "w_gate": 6, "w_up": 6, "w_down": 6}
    wsc = {n: nc.dram_tensor("ws_" + n, [k, 128, 8, 512], BF16) for n, k in nsl.items()}
    wsc_buf = {n: Buf("ws_" + n) for n in nsl}
    per = nc.dram_tensor("per", [8, 129, 768], F32)
    per_buf = Buf("per")

    def slab_geom(name, idx):
        if name == "w_down":
            n, ks = idx // 3, idx % 3
            return ks * 1024, (8, 8, 6)[ks], n * 512, 512
        if name in ("w_gate", "w_up"):
            return 0, 8, idx * 512, min(512, D_FF - idx * 512)
        return 0, 8, idx * 512, 512

    def sb(name, shape, dtype):
        return es.enter_context(nc.sbuf_tensor("t_" + name, list(shape), dtype))

    NSLOT = 4
    wring = [sb(f"wring{i}", [128, 8, 512], BF16) for i in range(NSLOT)]
    wring_buf = [Buf(f"wring{i}") for i in range(NSLOT)]
    xres = [sb(f"xres{i}", [128, 4, D], F32) for i in range(2)]
    xres_buf = [[Buf(f"xres{p}_{i}") for i in range(4)] for p in range(2)]
    actT = [sb(f"actT{i}", [128, 8, 512], BF16) for i in range(2)]
    actT_buf = [Buf(f"actT{i}") for i in range(2)]
    oT_kbuf = [Buf(f"oTk{k}") for k in range(8)]
    qT = sb("qT", [128, 4, 512], BF16)
    qT_buf = Buf("qT")
    kT = sb("kT", [128, 4, 1024], BF16)
    kT_buf = [Buf("kT0"), Buf("kT1")]
    Vext = sb("Vext", [128, 8, 8, 65], BF16)
    V_buf = [Buf("V0"), Buf("V1")]
    uT = sb("uT", [128, 4, 30 + 512], F32)
    uT_buf = Buf("uT")
    uTs = sb("uTs", [128, 4, 2, 46], F32)
    uTs_buf = Buf("uTs")
    cacc = sb("cacc", [128, 4, 512], F32)
    cacc_buf = [Buf(f"cacc{c}") for c in range(4)]
    sg = [sb(f"sg{i}", [128, 512], F32) for i in range(2)]
    sg_buf = [Buf(f"sg{i}") for i in range(2)]
    att = sb("att", [128, 512], BF16)
    att_buf = Buf("att")
    pT = [sb(f"pT{i}", [128, 5, 128], BF16) for i in range(3)]
    pT_buf = [Buf(f"pT{i}") for i in range(3)]
    Btab = sb("Btab", [128, 3, 8, 128], F32)
    Btab_buf = Buf("Btab")
    hid = sb("hid", [128, 22, 512], BF16)
    hid_buf = [Buf(f"hid{f}") for f in range(22)]
    pm = [sb(f"pm{i}", [128, 2, 512], BF16) for i in range(2)]
    pm_buf = [Buf(f"pm{i}") for i in range(2)]
    rs = [sb(f"rs{i}", [128, 512], F32) for i in range(3)]
    rs_buf = [Buf(f"rs{i}") for i in range(3)]
    tb = [rs[i][:, 0:384].rearrange("p (s n) -> p s n", s=3) for i in range(3)]
    tb_buf = rs_buf
    mkT = sb("mkT", [128, 8, 256], BF16)
    mkT_buf = Buf("mkT")
    mv = sb("mv", [128, 2, D], BF16)
    mv_buf = Buf("mv")
    gbuf = [sb(f"gbuf{i}", [128, D], F32) for i in range(2)]
    gbuf_buf = [Buf(f"gbuf{i}") for i in range(2)]
    hbf = [sb(f"hbf{i}", [128, D], BF16) for i in range(2)]
    hbf_buf = [Buf(f"hbf{i}") for i in range(2)]
    zc = [hbf[i // 2][:, (i % 2) * 512:(i % 2 + 1) * 512] for i in range(4)]
    zc_buf = [hbf_buf[i // 2] for i in range(4)]
    tmp = [sb(f"tmp{i}", [128, D], F32) for i in range(2)]
    tmp_buf = [Buf(f"tmp{i}") for i in range(2)]
    stage = [sb(f"stage{i}", [128, 512], F32) for i in range(1)]
    stage_buf = [Buf(f"stage{i}") for i in range(1)]
    ident = sb("ident", [128, 128], BF16)
    identf = sb("identf", [128, 128], F32)
    ones = sb("ones", [128, 128], BF16)
    const_buf = Buf("const")
    cvec_s = tmp[1]
    cvec = sb("cvecT", [128, 4, 34], F32)
    cvec_buf = Buf("cvec")
    cvs_buf = tmp_buf[1]
    frow = tmp[0]
    frow_buf = tmp_buf[0]
    vflag = sb("vflag_s", [128, 1], F32)
    vflag_buf = Buf("vflag")
    sm = sb("sm", [128, 64], F32)
    sm_buf = [Buf(f"sm{i}") for i in range(8)]
    cst = sb("cst", [128, 4], F32)
    bnst = sb("bnst", [128, 4, 8], F32)
    ps = es.enter_context(nc.psum_tensor("ps", [128, 8, 512], F32))
    bank_buf = [Buf(f"bank{i}") for i in range(8)]

    st = {"rr": 0, "reserved": set(), "slab_pos": 0, "sm": 0, "g": 0, "hbf": 0, "tmp": 0, "stage": 0, "sg": 0,
          "tb": 0, "pm": 0, "rs": 0, "hslot": 0}

    def bank(n=1):
        for _ in range(16):
            p = st["rr"]
            if n == 2 and p % 2 == 1:
                p = (p + 1) % 8
            cand = [(p + i) % 8 for i in range(n)]
            st["rr"] = (p + n) % 8
            if not any(c in st["reserved"] for c in cand):
                return cand if n > 1 else cand[0]
        raise RuntimeError(f"no psum bank n={n} reserved={sorted(st['reserved'])} rr={st['rr']}")

    def rot(key, n):
        v = st[key]
        st[key] = (v + 1) % n
        return v

    def bk(b):
        return ps[:, b, :]

    def bk_bf(b, k, n):
        return ps[:, b, :].bitcast(BF16).rearrange("p (k n) -> p k n", k=k)[:, :, 0:n]

    def blk_seq(with_next, first):
        seq = [("w_in", 0), ("w_out", 0), ("w_out", 1)]
        if with_next:
            seq += [("w_in", 1)]
        if first:
            seq += [("w_mk", 0), ("w_mk", 1), ("w_mv", 0), ("w_mv", 1)]
        seq += [("w_mq", 0), ("w_mq", 1), ("w_mo", 0), ("w_mo", 1)]
        if with_next:
            seq += [("w_in", 2), ("w_in", 3), ("w_in", 4)]
        seq += [x for s in range(6) for x in (("w_gate", s), ("w_up", s))]
        seq += [("w_down", i) for i in range(6)]
        return seq
    sched = [("w_in", 3), ("w_in", 4), ("w_in", 1), ("w_in", 2), ("w_in", 3), ("w_in", 4), ("w_in", 1), ("w_in", 2)]
    for _b in range(NBLK + 1):
        sched += blk_seq(_b < NBLK, _b == 0)
    DEPTH = 3
    loaded = {"n": 0}

    def issue_loads(upto):
        while loaded["n"] < min(upto, len(sched)):
            p = loaded["n"]
            name, idx = sched[p]
            r0, kt, c0, ncols = slab_geom(name, idx)
            slot = p % NSLOT
            ensure_cast(name, idx)
            P.dma("sp", f"wr{slot}", wring[slot][:, 0:kt, 0:ncols], wsc[name].ap()[idx][:, 0:kt, 0:ncols],
                  reads=[slab_cast_buf[(name, idx)]], writes=[wring_buf[slot]])
            loaded["n"] += 1

    def use_slab(name, idx):
        p = st["slab_pos"]
        assert sched[p] == (name, idx), (p, sched[p], name, idx)
        issue_loads(p + DEPTH)
        emit_casts(2)
        st["slab_pos"] = p + 1
        slot = p % NSLOT
        return wring[slot], wring_buf[slot]

    slab_cast_buf = {}
    cast_pending = []

    def cast_one(name, idx):
        r0, kt, c0, ncols = slab_geom(name, idx)
        src = W[name].ap()[r0:r0 + kt * 128, c0:c0 + ncols].rearrange("(ko p) c -> p ko c", p=128)
        bf = Buf(f"ws_{name}_{idx}")
        P.dma("pool", f"cast_{name}_{idx}", wsc[name].ap()[idx][:, 0:kt, 0:ncols], src, writes=[bf])
        slab_cast_buf[(name, idx)] = bf

    def emit_casts(n):
        k = 0
        while cast_pending and k < n:
            cast_one(*cast_pending.pop(0))
            k += 1

    def ensure_cast(name, idx):
        while (name, idx) not in slab_cast_buf:
            assert cast_pending, ("no cast scheduled for", name, idx)
            cast_one(*cast_pending.pop(0))

    def cast_group(names):
        for name in names:
            order = [3, 4, 1, 2, 0] if name == "w_in" else list(range(nsl[name]))
            for idx in order:
                r0, kt, c0, ncols = slab_geom(name, idx)
                src = W[name].ap()[r0:r0 + kt * 128, c0:c0 + ncols].rearrange("(ko p) c -> p ko c", p=128)
                bf = Buf(f"ws_{name}_{idx}")
                P.dma("pool", f"cast_{name}_{idx}", wsc[name].ap()[idx][:, 0:kt, 0:ncols], src, writes=[bf])
                slab_cast_buf[(name, idx)] = bf

    P.op("pool", f_memset(ident[:], 0.0), writes=[const_buf])
    P.op("pool", lambda e: e.affine_select(out=ident[:], in_=ident[:], pattern=[[-1, 128]], compare_op=ALU.not_equal,
                                           fill=1.0, base=0, channel_multiplier=1), writes=[const_buf])
    P.op("pool", f_memset(identf[:], 0.0), writes=[const_buf])
    P.op("pool", lambda e: e.affine_select(out=identf[:], in_=identf[:], pattern=[[-1, 128]], compare_op=ALU.not_equal,
                                           fill=1.0, base=0, channel_multiplier=1), writes=[const_buf])
    P.op("pool", f_memset(ones[:], 1.0), writes=[const_buf])
    P.op("pool", f_memset(cst[:, 0:1], -0.5), writes=[const_buf])
    P.op("pool", f_memset(cst[:, 1:2], EPS), writes=[const_buf])
    P.op("pool", f_memset(Vext[:], 1.0), writes=[V_buf[0], V_buf[1]])
    P.op("pool", f_memset(uT[:], 0.0), writes=[uT_buf])

    cast_group(["w_in"])
    for i in range(4):
        P.dma("sp", f"x1{i}", xres[1][:, i, :], xh[i * 128:(i + 1) * 128, :], writes=[xres_buf[1][i]])
    for i in range(4):
        P.dma("sp", f"x0{i}", xres[0][:, i, :], xp[i * 128:(i + 1) * 128, :], writes=[xres_buf[0][i]])
    P.dma("sp", "setup", vflag[:], vflag_d, writes=[vflag_buf])
    P.dma("sp", "setup", cvec_s[0:34, 0:512], cvec_d, writes=[cvs_buf])
    P.dma("sp", "setup", frow[0:8, 0:256], rb_d[:, 1:257], writes=[frow_buf])
    setup_tok = ("setup", P.dval["setup"])
    vflag_buf.w = cvs_buf.w = frow_buf.w = setup_tok
    P.op("dve", f_copy(frow[0:8, 256:768], frow[0:8, 255:256].to_broadcast([8, 512])), reads=[frow_buf], writes=[frow_buf])
    for h in range(8):
        P.dma("pool", "per", per.ap()[h:h + 1], frow[h:h + 1, 0:768].unsqueeze(1).to_broadcast([1, 129, 768]),
              reads=[frow_buf], writes=[])
    per_buf.w = ("per", P.dval["per"])

    b = bank()
    for c in range(4):
        P.op("pe", f_tr(ps[:, b, c * 34:(c + 1) * 34], cvec_s[0:34, c * 128:(c + 1) * 128], identf[0:34, 0:34]),
             reads=[cvs_buf, const_buf], writes=[bank_buf[b]], signal=(c == 3))
    P.op("dve", f_copy(cvec[:].rearrange("p c j -> p (c j)"), ps[:, b, 0:136]), reads=[bank_buf[b]], writes=[cvec_buf])
    P.op("dve", f_ts1(cvec[:, :, 0:31], cvec[:, :, 0:31], 0.5, ALU.mult), reads=[cvec_buf], writes=[cvec_buf])

    def build_btab():
        for si, t in enumerate((0, 3, 4)):
            src = bass.AP(per, 639 - 128 * t, [[767, 128], [129 * 768, 8], [1, 128]])
            P.dma("sp", "btab", Btab[:, si], src, reads=[per_buf], writes=[])
        Btab_buf.w = ("btab", P.dval["btab"])
        P.op("pool", f_memset(Btab[0:64, 0, :, 64:128], NEG), reads=[], writes=[Btab_buf])
        P.op("pool", f_memset(Btab[64:128, 2, :, 0:64], NEG), reads=[], writes=[Btab_buf])

    def aslist(b):
        return list(b) if isinstance(b, (list, tuple)) else [b]

    def sm_cols(n=8):
        g = rot("sm", 8)
        return sm[:, g * 8:g * 8 + n], sm_buf[g], g

    def load_g(name):
        i = rot("g", 2)
        P.dma("sp", f"g{i}", gbuf[i][:], gv[name].to_broadcast([128, D]), writes=[gbuf_buf[i]])
        return gbuf[i], gbuf_buf[i]

    def rstd_from_ss(col_ap, nr, smb, n_terms):
        if n_terms == 2:
            P.op("pool", f_tt(col_ap[0:nr, 0:1], col_ap[0:nr, 0:1], col_ap[0:nr, 1:2], ALU.add), reads=[smb], writes=[smb])
        P.op("pool", f_ts(col_ap[0:nr, 3:4], col_ap[0:nr, 0:1], 1.0 / D, EPS, ALU.mult, ALU.add), reads=[smb], writes=[smb])
        P.op("pool", f_tt(col_ap[0:nr, 2:3], col_ap[0:nr, 3:4], cst[0:nr, 0:1], ALU.pow), reads=[smb, const_buf], writes=[smb])

    def norm_begin(srcs, g_name, n_early=2, bufs=None):
        g_t, g_b = load_g(g_name)
        info = []
        for (src_ap, src_buf, nr, col0) in srcs:
            cols, smb, _ = sm_cols()
            if bufs is None:
                hi = rot("hbf", 2)
                hb_ap, hb_buf = hbf[hi], hbf_buf[hi]
            else:
                hb_ap, hb_buf = bufs[len(info) % len(bufs)]
            P.op("act", f_act(hb_ap[0:nr, :], src_ap, AF.Square, accum=cols[0:nr, 0:1]), reads=[src_buf], writes=[smb, hb_buf])
            info.append([cols, smb, (hb_ap, hb_buf), False])
        for (src_ap, src_buf, nr, col0), (cols, smb, hi, _) in zip(srcs, info):
            rstd_from_ss(cols, nr, smb, 1)
        stt = {"srcs": srcs, "info": info, "g": (g_t, g_b)}
        for i in range(min(n_early, len(srcs))):
            norm_h(stt, i)
        return stt

    def norm_h(stt, i):
        (src_ap, src_buf, nr, col0) = stt["srcs"][i]
        cols, smb, hi, done = stt["info"][i]
        if done:
            return
        g_t, g_b = stt["g"]
        hb_ap, hb_buf = hi
        P.op("dve", f_stt(hb_ap[0:nr, :], src_ap, cols[0:nr, 2:3], g_t[0:nr, :], ALU.mult, ALU.mult),
             reads=[src_buf, smb, g_b], writes=[hb_buf])
        stt["info"][i][3] = True

    def norm_finish(stt, dstT, dstT_buf):
        for i, (src_ap, src_buf, nr, col0) in enumerate(stt["srcs"]):
            norm_h(stt, i)
            cols, smb, (hb_ap, hb_buf), _ = stt["info"][i]
            b = bank()
            pv = bk_bf(b, 8, 128)
            for k in range(8):
                P.op("pe", f_tr(pv[:, k, 0:nr], hb_ap[0:nr, k * 128:(k + 1) * 128], ident[0:nr, 0:nr]),
                     reads=[hb_buf, const_buf], writes=[bank_buf[b]], signal=(k == 7))
            P.op("act", f_act(dstT[:, :, col0:col0 + nr], pv[:, :, 0:nr], AF.Copy), reads=[bank_buf[b]], writes=aslist(dstT_buf))

    def row_buffers():
        return [(hbf[0], hbf_buf[0]), (hbf[1], hbf_buf[1]),
                (pm[0][:].rearrange("p a b -> p (a b)"), pm_buf[0]), (pm[1][:].rearrange("p a b -> p (a b)"), pm_buf[1])]

    def norm_tiles(srcs, g_name, dstT, dstT_buf, wide=False):
        if wide:
            norm_finish(norm_begin(srcs, g_name, n_early=4, bufs=row_buffers()), dstT, dstT_buf)
        else:
            norm_finish(norm_begin(srcs, g_name, n_early=0), dstT, dstT_buf)

    def proj_fm(slab, slab_b, ncol_tiles, src, src_buf, N, evac):
        for f in range(ncol_tiles):
            b = bank()
            for k in range(8):
                P.op("pe", f_mm(ps[:, b, 0:N], slab[:, k, f * 128:(f + 1) * 128], src[:, k, 0:N], k == 0, k == 7),
                     reads=[slab_b] + aslist(src_buf), writes=[bank_buf[b]], signal=(k == 7))
            evac(f, b)

    def proj_tm(slab, slab_b, src, src_buf, c0, nr, b, kt=8, k0=0, start=True, stop=True, ncols=512):
        for k in range(kt):
            P.op("pe", f_mm(ps[0:nr, b, 0:ncols], src[:, k0 + k, c0:c0 + nr], slab[:, k, 0:ncols],
                            start and k == 0, stop and k == kt - 1),
                 reads=[slab_b] + aslist(src_buf), writes=[bank_buf[b]], signal=(stop and k == kt - 1))

    def store_rows(dst_ap, src_bank, nr, ncols, scale=None, col0=0, extra=()):
        si = rot("stage", 1)
        if scale is None:
            P.op("act", f_act(stage[si][0:nr, 0:ncols], ps[0:nr, src_bank, col0:col0 + ncols], AF.Copy),
                 reads=[bank_buf[src_bank]] + list(extra), writes=[stage_buf[si]])
        else:
            P.op("act", f_act(stage[si][0:nr, 0:ncols], ps[0:nr, src_bank, col0:col0 + ncols], AF.Copy, scale=scale),
                 reads=[bank_buf[src_bank]] + list(extra), writes=[stage_buf[si]])
        P.dma("sp", f"stg{si}", dst_ap, stage[si][0:nr, 0:ncols], reads=[stage_buf[si]])

    class PostNorm:
        def __init__(self, g_name):
            self.g_t, self.g_b = load_g(g_name)
            self.pend = None

        def add(self, b0, b1, nr, x_ap, x_buf, after=None):
            g_t, g_b = self.g_t, self.g_b
            cols, smb, _ = sm_cols()
            ti = rot("tmp", 2)
            for n, bb in enumerate((b0, b1)):
                P.op("act", f_act(tmp[ti][0:nr, n * 512:(n + 1) * 512], ps[0:nr, bb, :], AF.Square, accum=cols[0:nr, n:n + 1]),
                     reads=[bank_buf[bb]], writes=[smb, tmp_buf[ti]])
            for n, bb in enumerate((b0, b1)):
                P.op("dve", f_tt(tmp[ti][0:nr, n * 512:(n + 1) * 512], ps[0:nr, bb, :], g_t[0:nr, n * 512:(n + 1) * 512], ALU.mult),
                     reads=[bank_buf[bb], smb, g_b], writes=[tmp_buf[ti]])
            rstd_from_ss(cols, nr, smb, 2)

            def xupd():
                P.op("dve", f_stt(x_ap, tmp[ti][0:nr, :], cols[0:nr, 2:3], x_ap, ALU.mult, ALU.add),
                     reads=[tmp_buf[ti], smb, x_buf], writes=[x_buf])
                if after is not None:
                    after()
            if self.pend is not None:
                self.pend()
            self.pend = xupd

        def finish(self):
            if self.pend is not None:
                self.pend()
            self.pend = None

    pending = []

    def drain(n):
        k = 0
        while pending and k < n:
            pending.pop(0)()
            k += 1

    def hslot():
        j = rot("hslot", 11)
        ap = hid[:, 2 * j:2 * j + 2, :].rearrange("p a b -> p (a b)").bitcast(F32)
        return ap, [hid_buf[2 * j], hid_buf[2 * j + 1]], f"hst{j}"

    SLOT = {1: 0, 2: 1, 0: 2, 3: 3, 4: 4}

    def attention_block(segs, dstT, dstT_buf):
        obs = {}

        def s_stage(si, h):
            q0, nq, ktiles = segs[si]
            base = 64 * (h % 2)
            f = h // 2
            sbk = bank(2)
            assert sbk[1] == sbk[0] + 1
            spv = ps[:, sbk[0]:sbk[0] + 2, :].rearrange("p b n -> p (b n)")
            sbufs = [bank_buf[sbk[0]], bank_buf[sbk[1]]]
            for t in range(5):
                kcol, nk, vt, kb, vb = ktiles[t]
                s = SLOT[t]
                P.op("pe", f_mm(spv[0:nk, s * 128:s * 128 + nq], kT[base:base + 64, f, kcol:kcol + nk],
                                qT[base:base + 64, f, q0:q0 + nq], True, True),
                     reads=[kb, qT_buf], writes=sbufs, signal=(t == 4))
            ti = rot("tb", 3)
            s3 = spv[:, 256:640].rearrange("p (s n) -> p s n", s=3)[:, :, 0:nq]
            P.op("dve", f_tt(tb[ti][:, :, 0:nq], s3, Btab[:, :, h, 0:nq], ALU.add),
                 reads=sbufs + [Btab_buf], writes=[tb_buf[ti]])
            s2 = spv[:, 0:256].rearrange("p (s n) -> p s n", s=2)[:, :, 0:nq]
            P.op("act", f_act(pT[ti][:, 0:2, 0:nq], s2, AF.Exp, bias=Btab[:, 0, h, 0:1]),
                 reads=sbufs + [Btab_buf, tb_buf[ti]], writes=[pT_buf[ti]])
            P.op("act", f_act(pT[ti][:, 2:5, 0:nq], tb[ti][:, :, 0:nq], AF.Exp), reads=[tb_buf[ti]], writes=[pT_buf[ti]])
            return ti

        def pv_stage(si, h, ti):
            q0, nq, ktiles = segs[si]
            if h == 0:
                ob = bank(2)
                st["reserved"].add(ob[0])
                st["reserved"].add(ob[1])
                obs[si] = ob
            ob = obs[si]
            o = ob[h // 4]
            hc = (h % 4) * 65
            for t in range(5):
                kcol, nk, vt, kb, vb = ktiles[t]
                P.op("pe", f_mm(ps[0:nq, o, hc:hc + 65], pT[ti][0:nk, SLOT[t], 0:nq], Vext[0:nk, vt, h, :], t == 0, t == 4),
                     reads=[pT_buf[ti], vb], writes=[bank_buf[o]], signal=(t == 4))
            if h == 7:
                cols, smb, _ = sm_cols()
                for i2 in range(2):
                    ov = ps[0:nq, ob[i2], 0:260].rearrange("p (h d) -> p h d", h=4)
                    P.op("dve", f_recip(cols[0:nq, i2 * 4:i2 * 4 + 4], ov[:, :, 64]), reads=[bank_buf[ob[i2]]], writes=[smb])
                    P.op("dve", f_tt(att[0:nq, i2 * 256:(i2 + 1) * 256].rearrange("p (h d) -> p h d", h=4), ov[:, :, 0:64],
                                     cols[0:nq, i2 * 4:i2 * 4 + 4].unsqueeze(2).to_broadcast([nq, 4, 64]), ALU.mult),
                         reads=[bank_buf[ob[i2]], smb], writes=[att_buf])
                st["reserved"].discard(ob[0])
                st["reserved"].discard(ob[1])
                b = bank()
                pv = bk_bf(b, 4, 128)
                for c in range(4):
                    P.op("pe", f_tr(pv[:, c, 0:nq], att[0:nq, c * 128:(c + 1) * 128], ident[0:nq, 0:nq]),
                         reads=[att_buf, const_buf], writes=[bank_buf[b]], signal=(c == 3))
                P.op("act", f_act(dstT[:, 0:4, q0:q0 + nq], pv[:, :, 0:nq], AF.Copy), reads=[bank_buf[b]], writes=[dstT_buf])

        fifo = []
        for si in range(len(segs)):
            for h in range(8):
                ti = s_stage(si, h)
                fifo.append((si, h, ti))
                if len(fifo) > 2:
                    pv_stage(*fifo.pop(0))
        while fifo:
            pv_stage(*fifo.pop(0))

    def mem_attention(q2T, q2T_buf, q0, nq, oT, oT_buf):
        def s_stage(h):
            pi = rot("pm", 2)
            for m in range(2):
                b = bank()
                for dk in range(2):
                    P.op("pe", f_mm(ps[:, b, 0:nq], mkT[:, 2 * h + dk, m * 128:(m + 1) * 128], q2T[:, 2 * h + dk, q0:q0 + nq],
                                    dk == 0, dk == 1), reads=[mkT_buf, q2T_buf], writes=[bank_buf[b]], signal=(dk == 1))
                P.op("act", f_act(pm[pi][:, m, 0:nq], ps[:, b, 0:nq], AF.Exp), reads=[bank_buf[b]], writes=[pm_buf[pi]])
            return pi

        def o_stage(h, pi):
            b = bank()
            for m in range(2):
                P.op("pe", f_mm(ps[:, b, 0:nq], ones[:], pm[pi][:, m, 0:nq], m == 0, m == 1),
                     reads=[pm_buf[pi], const_buf], writes=[bank_buf[b]], signal=(m == 1))
            ri = rot("rs", 3)
            P.op("dve", f_recip(rs[ri][:, 0:nq], ps[:, b, 0:nq]), reads=[bank_buf[b]], writes=[rs_buf[ri]])
            for dk in range(2):
                b = bank()
                for m in range(2):
                    c0 = 256 * h + 128 * dk
                    P.op("pe", f_mm(ps[:, b, 0:nq], mv[:, m, c0:c0 + 128], pm[pi][:, m, 0:nq], m == 0, m == 1),
                         reads=[pm_buf[pi], mv_buf], writes=[bank_buf[b]], signal=(m == 1))
                P.op("dve", f_tt(oT[:, 2 * h + dk, q0:q0 + nq], ps[:, b, 0:nq], rs[ri][:, 0:nq], ALU.mult),
                     reads=[bank_buf[b], rs_buf[ri]], writes=[oT_buf, oT_kbuf[2 * h + dk]])
        prev = None
        for h in range(4):
            pi = s_stage(h)
            if prev is not None:
                o_stage(*prev)
            prev = (h, pi)
        o_stage(*prev)

    A, B = 0, 1

    def geom(kind, bi):
        sample = kind == "sample"
        tiles = [(0, 32)] if sample else [(i * 128, 128) for i in range(4)]
        N = 32 if sample else 512
        if kind == "halo":
            cur, xpar = 1, 1
        elif sample:
            cur, xpar = 1, 0
        else:
            cur, xpar = bi % 2, bi % 2
        return sample, tiles, N, cur, xpar

    def emit_xload(kind, bi):
        if kind == "halo":
            for i in range(4):
                P.dma("sp", f"x1{i}", xres[1][:, i, :], xh[i * 128:(i + 1) * 128, :], writes=[xres_buf[1][i]])
        elif kind == "sample":
            P.dma("sp", "x00", xres[0][0:32, 0, :], xs_d[:, :], writes=[xres_buf[0][0]])
        else:
            xpar = bi % 2
            for i in range(4):
                P.dma("sp", f"x{xpar}{i}", xres[xpar][:, i, :], xp[bi * NB + i * 128: bi * NB + (i + 1) * 128, :],
                      writes=[xres_buf[xpar][i]])

    def glu(kind, bi, src=None):
        sample, tiles, N, cur, xpar = geom(kind, bi)
        hT, hTb = src if src is not None else (actT[A], actT_buf[A])
        slab_v, sbf_v = use_slab("w_in", 3)
        slab_g, sbf_g = use_slab("w_in", 4)
        for c in range(4):
            bv = bank()
            for k in range(8):
                P.op("pe", f_mm(ps[:, bv, 0:N], slab_v[:, k, c * 128:(c + 1) * 128], hT[:, k, 0:N], k == 0, k == 7),
                     reads=[sbf_v] + aslist(hTb), writes=[bank_buf[bv]], signal=(k == 7))
            bg = bank()
            for k in range(8):
                P.op("pe", f_mm(ps[:, bg, 0:N], slab_g[:, k, c * 128:(c + 1) * 128], hT[:, k, 0:N], k == 0, k == 7),
                     reads=[sbf_g] + aslist(hTb), writes=[bank_buf[bg]], signal=(k == 7))
            gi = rot("sg", 2)
            P.op("act", f_act(sg[gi][:, 0:N], ps[:, bg, 0:N], AF.Tanh, scale=0.5), reads=[bank_buf[bg]], writes=[sg_buf[gi]])
            if sample:
                dst = uTs[:, c, :, 30:46]
                srcs = sg[gi][:, 0:32].rearrange("p (s t) -> p s t", s=2)
                srcv = ps[:, bv, 0:32].rearrange("p (s t) -> p s t", s=2)
                P.op("dve", f_stt(dst, srcs, 1.0, srcv, ALU.add, ALU.mult), reads=[sg_buf[gi], bank_buf[bv]], writes=[uTs_buf])
            else:
                P.op("dve", f_stt(uT[:, c, 30:30 + N], sg[gi][:, 0:N], 1.0, ps[:, bv, 0:N], ALU.add, ALU.mult),
                     reads=[sg_buf[gi], bank_buf[bv]], writes=[uT_buf])

    def queue_conv(kind, bi):
        sample, tiles, N, cur, xpar = geom(kind, bi)
        for j in range(31):
            for c in range(4):
                if sample:
                    src = uTs[:, c, :, j:j + 16]
                    dstc = cacc[:, c, 0:32].rearrange("p (s t) -> p s t", s=2)
                    ub = uTs_buf
                else:
                    src = uT[:, c, j:j + N]
                    dstc = cacc[:, c, 0:N]
                    ub = uT_buf
                if j == 0:
                    pending.append(lambda src=src, dstc=dstc, ub=ub, c=c: P.op(
                        "dve", f_ts(dstc, src, cvec[:, c, 0:1], cvec[:, c, 31:32], ALU.mult, ALU.add),
                        reads=[ub, cvec_buf], writes=[cacc_buf[c]]))
                else:
                    pending.append(lambda src=src, dstc=dstc, ub=ub, c=c, j=j: P.op(
                        "dve", f_stt(dstc, src, cvec[:, c, j:j + 1], dstc, ALU.mult, ALU.add),
                        reads=[ub, cvec_buf], writes=[cacc_buf[c]]))

    def conv_tail_out(kind, bi):
        sample, tiles, N, cur, xpar = geom(kind, bi)
        nseg = 2 if sample else 1
        for s in range(nseg):
            b = bank()
            nrow = 16 if sample else 30
            for c in range(4):
                srcu = uTs[:, c, s, 30:46] if sample else uT[:, c, N:N + 30]
                P.op("pe", f_tr(ps[0:nrow, b, c * 128:(c + 1) * 128], srcu, identf[:, :]),
                     reads=[uTs_buf if sample else uT_buf, const_buf], writes=[bank_buf[b]], signal=(c == 3))
            store_rows(cso_d[s, 14:30, :] if sample else ctail_d[:, :], b, nrow, 512, scale=0.5)

    H1 = hid[:, 0:8, :]
    H1b = [hid_buf[f] for f in range(8)]

    def pre_begin(kind, bi):
        sample, tiles, N, cur, xpar = geom(kind, bi)
        xr, xb = xres[xpar], xres_buf[xpar]
        return norm_begin([(xr[0:nr, i, :], xb[i], nr, r0) for i, (r0, nr) in enumerate(tiles)], "g_mix_pre",
                          n_early=4, bufs=row_buffers())

    def pre1(kind, bi, stt):
        sample, tiles, N, cur, xpar = geom(kind, bi)
        last = (kind == "prompt" and bi == NBLK - 1)
        norm_finish(stt, H1, H1b)
        slab, sbf = use_slab("w_in", 1)
        kc0 = cur * 512
        proj_fm(slab, sbf, 4, H1, H1b, N, lambda f, b: P.op(
            "act", f_act(kT[:, f, kc0:kc0 + N], ps[:, b, 0:N], AF.Copy), reads=[bank_buf[b]], writes=[kT_buf[cur]]))
        if last:
            for i, (r0, nr) in enumerate(tiles):
                b = bank()
                proj_tm(slab, sbf, H1, H1b, r0, nr, b)
                store_rows(ktail_d[r0:r0 + nr, :], b, nr, 512)
        if sample:
            for s in range(2):
                b = bank()
                proj_tm(slab, sbf, H1, H1b, s * 16, 16, b)
                store_rows(kso_d[s, 496:512, :], b, 16, 512)

    def pre2(kind, bi):
        sample, tiles, N, cur, xpar = geom(kind, bi)
        last = (kind == "prompt" and bi == NBLK - 1)
        slab, sbf = use_slab("w_in", 2)
        if kind == "prompt" and bi == 1:
            P.op("pool", f_memset(Vext[:, 4 * cur:4 * cur + 4, :, 64:65], 1.0), writes=[V_buf[cur]])
        vsegs = [(s * 16, 16, 4 * cur + s) for s in range(2)] if sample else [(r0, nr, 4 * cur + i) for i, (r0, nr) in enumerate(tiles)]
        for si_, (c0, nr, vt) in enumerate(vsegs):
            b = bank()
            proj_tm(slab, sbf, H1, H1b, c0, nr, b)
            P.op("act", f_act(Vext[0:nr, vt, :, 0:64], ps[0:nr, b, :].rearrange("p (h d) -> p h d", h=8), AF.Copy),
                 reads=[bank_buf[b]], writes=[V_buf[cur]])
            if last:
                store_rows(vtail_d[c0:c0 + nr, :], b, nr, 512)
            if sample:
                store_rows(vso_d[si_, 496:512, :], b, 16, 512)
        if not sample:
            P.op("dve", f_copy(uT[:, :, 0:30], uT[:, :, 512:542]), reads=[uT_buf], writes=[uT_buf])
        glu(kind, bi, src=(H1, H1b))
        P.op("act", f_act(actT[A][:, :, 0:N], H1[:, :, 0:N], AF.Copy), reads=H1b, writes=[actT_buf[A]])
        if sample or last:
            conv_tail_out(kind, bi)
        queue_conv(kind, bi)

    def halo_a():
        sample, tiles, N, cur, xpar = geom("halo", 0)
        xr, xb = xres[xpar], xres_buf[xpar]
        norm_tiles([(xr[0:nr, i, :], xb[i], nr, r0) for i, (r0, nr) in enumerate(tiles)], "g_mix_pre", actT[B], actT_buf[B])
        P.op("dve", f_copy(Vext[:, 4 * cur:4 * cur + 4, :, 64:65].rearrange("p t h o -> p (t h o)"),
                           vflag[:, 0:1].to_broadcast([128, 32])), reads=[vflag_buf], writes=[V_buf[cur]])
        glu("halo", 0, src=(actT[B], actT_buf[B]))

    def halo_b():
        sample, tiles, N, cur, xpar = geom("halo", 0)
        hT, hTb = actT[B], actT_buf[B]
        slab, sbf = use_slab("w_in", 1)
        kc0 = cur * 512
        proj_fm(slab, sbf, 4, hT, hTb, N, lambda f, b: P.op(
            "act", f_act(kT[:, f, kc0:kc0 + N], ps[:, b, 0:N], AF.Copy), reads=[bank_buf[b]], writes=[kT_buf[cur]]))
        slab, sbf = use_slab("w_in", 2)
        for i, (r0, nr) in enumerate(tiles):
            b = bank()
            proj_tm(slab, sbf, hT, hTb, r0, nr, b)
            P.op("act", f_act(Vext[0:nr, 4 * cur + i, :, 0:64], ps[0:nr, b, :].rearrange("p (h d) -> p h d", h=8), AF.Copy),
                 reads=[bank_buf[b]], writes=[V_buf[cur]])

    def main_stage(kind, bi, next_pre=None, post_b_hook=None, mid_hook=None, next_begin=None, next_mid=None):
        sample, tiles, N, cur, xpar = geom(kind, bi)
        last = (kind == "prompt" and bi == NBLK - 1)
        prev = 1 - cur
        xr, xb = xres[xpar], xres_buf[xpar]
        drain(10 ** 6)
        mixT, mixTb = actT[B], actT_buf[B]
        slab, sbf = use_slab("w_in", 0)
        proj_fm(slab, sbf, 4, actT[A], actT_buf[A], N, lambda f, b: P.op(
            "act", f_act(qT[:, f, 0:N], ps[:, b, 0:N], AF.Copy, scale=0.125), reads=[bank_buf[b]], writes=[qT_buf]))
        lninfo = []
        for i, (r0, nr) in enumerate(tiles):
            b = bank()
            for c in range(4):
                P.op("pe", f_tr(ps[0:nr, b, c * 128:(c + 1) * 128], cacc[:, c, r0:r0 + nr], identf[:, :]),
                     reads=[cacc_buf[c], const_buf], writes=[bank_buf[b]], signal=(c == 3))
            cols, smb, _ = sm_cols()
            P.op("dve", lambda e, nr=nr, b=b, i=i: e.bn_stats(out=bnst[0:nr, i, 0:6], in_=ps[0:nr, b, :]),
                 reads=[bank_buf[b]], writes=[smb])
            P.op("dve", lambda e, nr=nr, cols=cols, i=i: e.bn_aggr(out=cols[0:nr, 4:6], in_=bnst[0:nr, i, 0:6]),
                 reads=[smb], writes=[smb])
            lninfo.append((b, cols, smb))
        for i, (r0, nr) in enumerate(tiles):
            b, cols, smb = lninfo[i]
            P.op("pool", f_ts1(cols[0:nr, 3:4], cols[0:nr, 5:6], EPS, ALU.add), reads=[smb], writes=[smb])
            P.op("pool", f_tt(cols[0:nr, 2:3], cols[0:nr, 3:4], cst[0:nr, 0:1], ALU.pow), reads=[smb, const_buf], writes=[smb])
        for i, (r0, nr) in enumerate(tiles):
            b, cols, smb = lninfo[i]
            P.op("dve", f_stt(cols[0:nr, 6:7], cols[0:nr, 4:5], -1.0, cols[0:nr, 2:3], ALU.mult, ALU.mult), reads=[smb], writes=[smb])
            P.op("act", f_act(zc[i][0:nr, :], ps[0:nr, b, :], AF.Identity, scale=cols[0:nr, 2:3], bias=cols[0:nr, 6:7]),
                 reads=[bank_buf[b], smb], writes=[zc_buf[i]])

        def c2b():
            cb = bank(2)
            st["reserved"].add(cb[0])
            st["reserved"].add(cb[1])
            for i, (r0, nr) in enumerate(tiles):
                for c in range(4):
                    pvc = ps[:, cb[c // 2], :].bitcast(BF16)[:, (c % 2) * 512:(c % 2) * 512 + 512]
                    P.op("pe", f_tr(pvc[:, r0:r0 + nr], zc[i][0:nr, c * 128:(c + 1) * 128], ident[0:nr, 0:nr]),
                         reads=[zc_buf[i], const_buf], writes=[bank_buf[cb[c // 2]]], signal=(c == 3))
            for c in range(4):
                pvc = ps[:, cb[c // 2], :].bitcast(BF16)[:, (c % 2) * 512:(c % 2) * 512 + 512]
                P.op("act", f_act(mixT[:, 4 + c, 0:N], pvc[:, 0:N], AF.Silu, scale=cvec[:, c, 32:33], bias=cvec[:, c, 33:34]),
                     reads=[bank_buf[cb[c // 2]], cvec_buf], writes=[mixTb])
            st["reserved"].discard(cb[0])
            st["reserved"].discard(cb[1])

        c2b()
        if sample:
            for s in range(2):
                for t in range(4):
                    sap, sbufs_, ssem = hslot()
                    P.dma("sp", ssem, sap, ck_d[s, t * 128:(t + 1) * 128, :], writes=sbufs_)
                    hi = rot("hbf", 2)
                    P.op("dve", f_copy(hbf[hi][:, 0:512], sap), reads=sbufs_, writes=[hbf_buf[hi]])
                    b = bank()
                    pv = bk_bf(b, 4, 128)
                    for f in range(4):
                        P.op("pe", f_tr(pv[:, f, :], hbf[hi][:, f * 128:(f + 1) * 128], ident[:, :]),
                             reads=[hbf_buf[hi], const_buf], writes=[bank_buf[b]], signal=(f == 3))
                    P.op("act", f_act(kT[:, :, prev * 512 + t * 128: prev * 512 + (t + 1) * 128], pv[:, :, :], AF.Copy),
                         reads=[bank_buf[b]], writes=[kT_buf[prev]])
                    sap, sbufs_, ssem = hslot()
                    P.dma("sp", ssem, sap, cv_d[s, t * 128:(t + 1) * 128, :], writes=sbufs_)
                    P.op("dve", f_copy(Vext[:, 4 * prev + t, :, 0:64], sap.rearrange("p (h d) -> p h d", h=8)),
                         reads=sbufs_, writes=[V_buf[prev]])
                kt_l = [(prev * 512 + t * 128, 128, 4 * prev + t, kT_buf[prev], V_buf[prev]) for t in range(4)]
                kt_l.append((cur * 512 + s * 16, 16, 4 * cur + s, kT_buf[cur], V_buf[cur]))
                attention_block([(s * 16, 16, kt_l)], mixT, mixTb)
        else:
            segs = []
            for j in range(4):
                kt_l = []
                for t in range(5):
                    gt = j + t
                    half = prev if gt < 4 else cur
                    kt_l.append((half * 512 + (gt % 4) * 128, 128, 4 * half + gt % 4, kT_buf[half], V_buf[half]))
                segs.append((j * 128, 128, kt_l))
            attention_block(segs, mixT, mixTb)

        nstate = next_begin() if next_begin is not None else None
        s0, s0b = use_slab("w_out", 0)
        s1, s1b = use_slab("w_out", 1)
        pn = PostNorm("g_mix_post")
        for i, (r0, nr) in enumerate(tiles):
            b0 = bank()
            proj_tm(s0, s0b, mixT, mixTb, r0, nr, b0)
            b1 = bank()
            proj_tm(s1, s1b, mixT, mixTb, r0, nr, b1)
            pn.add(b0, b1, nr, xr[0:nr, i, :], xb[i])
        pn.finish()
        if next_mid is not None:
            next_mid(nstate)

        if mid_hook is not None:
            mid_hook()
        norm_tiles([(xr[0:nr, i, :], xb[i], nr, r0) for i, (r0, nr) in enumerate(tiles)], "g_mem_pre", actT[A], actT_buf[A], wide=not sample)
        q2T, q2Tb = actT[B], actT_buf[B]
        for n in range(2):
            slab, sbf = use_slab("w_mq", n)
            proj_fm(slab, sbf, 4, actT[A], actT_buf[A], N, lambda f, b, n=n: P.op(
                "act", f_act(q2T[:, 4 * n + f, 0:N], ps[:, b, 0:N], AF.Copy, scale=1.0 / 16.0), reads=[bank_buf[b]], writes=[q2Tb]))
        oT, oTb = actT[A], actT_buf[A]
        if sample:
            for s in range(2):
                for m in range(2):
                    for src_d, is_k in ((cmk_d, True), (cmv_d, False)):
                        for hf in range(2):
                            sap, sbufs_, ssem = hslot()
                            P.dma("sp", ssem, sap, src_d[s, m * 128:(m + 1) * 128, hf * 512:(hf + 1) * 512], writes=sbufs_)
                            if is_k:
                                hi = rot("hbf", 2)
                                P.op("dve", f_copy(hbf[hi][:, 0:512], sap), reads=sbufs_, writes=[hbf_buf[hi]])
                                b = bank()
                                pv = bk_bf(b, 4, 128)
                                for f in range(4):
                                    P.op("pe", f_tr(pv[:, f, :], hbf[hi][:, f * 128:(f + 1) * 128], ident[:, :]),
                                         reads=[hbf_buf[hi], const_buf], writes=[bank_buf[b]], signal=(f == 3))
                                P.op("act", f_act(mkT[:, 4 * hf:4 * hf + 4, m * 128:(m + 1) * 128], pv[:, :, :], AF.Copy),
                                     reads=[bank_buf[b]], writes=[mkT_buf])
                            else:
                                P.op("dve", f_copy(mv[:, m, hf * 512:(hf + 1) * 512], sap),
                                     reads=sbufs_, writes=[mv_buf])
                mem_attention(q2T, q2Tb, s * 16, 16, oT, oTb)
        else:
            mem_attention(q2T, q2Tb, 0, N, oT, oTb)
        s0, s0b = use_slab("w_mo", 0)
        s1, s1b = use_slab("w_mo", 1)
        pn = PostNorm("g_mem_post")
        mob = {}
        for i in range(len(tiles)):
            for n in range(2):
                b = bank()
                mob[(i, n)] = b
                st["reserved"].add(b)
        for khalf in range(2):
            for i, (r0, nr) in enumerate(tiles):
                for n, (sl, slb) in enumerate(((s0, s0b), (s1, s1b))):
                    b = mob[(i, n)]
                    for k in range(4 * khalf, 4 * khalf + 4):
                        P.op("pe", f_mm(ps[0:nr, b, :], oT[:, k, r0:r0 + nr], sl[:, k, :], k == 0, k == 7),
                             reads=[slb, oT_kbuf[k]], touch=[oTb], writes=[bank_buf[b]], signal=(k % 4 == 3))
        for i, (r0, nr) in enumerate(tiles):
            pn.add(mob[(i, 0)], mob[(i, 1)], nr, xr[0:nr, i, :], xb[i])
            st["reserved"].discard(mob[(i, 0)])
            st["reserved"].discard(mob[(i, 1)])
        pn.finish()

        if next_pre is not None:
            next_pre()

        norm_tiles([(xr[0:nr, i, :], xb[i], nr, r0) for i, (r0, nr) in enumerate(tiles)], "g_ffn_pre", actT[B], actT_buf[B], wide=not sample)
        h3, h3b = actT[B], actT_buf[B]
        for s in range(6):
            sg_, sgb = use_slab("w_gate", s)
            su_, sub = use_slab("w_up", s)
            nf = 4 if s < 5 else 2
            for ff in range(nf):
                f = s * 4 + ff
                bg = bank()
                for k in range(8):
                    P.op("pe", f_mm(ps[:, bg, 0:N], sg_[:, k, ff * 128:(ff + 1) * 128], h3[:, k, 0:N], k == 0, k == 7),
                         reads=[sgb, h3b], writes=[bank_buf[bg]], signal=(k == 7))
                bu = bank()
                for k in range(8):
                    P.op("pe", f_mm(ps[:, bu, 0:N], su_[:, k, ff * 128:(ff + 1) * 128], h3[:, k, 0:N], k == 0, k == 7),
                         reads=[sub, h3b], writes=[bank_buf[bu]], signal=(k == 7))
                gi = rot("sg", 2)
                P.op("act", f_act(sg[gi][:, 0:N], ps[:, bg, 0:N], AF.Silu), reads=[bank_buf[bg]], writes=[sg_buf[gi]])
                P.op("dve", f_tt(hid[:, f, 0:N], sg[gi][:, 0:N], ps[:, bu, 0:N], ALU.mult),
                     reads=[sg_buf[gi], bank_buf[bu]], writes=[hid_buf[f]])
                drain(4)
        dbanks = {}
        for n in range(2):
            for i in range(len(tiles)):
                b = bank()
                dbanks[(i, n)] = b
                st["reserved"].add(b)
        for n in range(2):
            for ks in range(3):
                slab, sbf = use_slab("w_down", n * 3 + ks)
                kt = (8, 8, 6)[ks]
                for i, (r0, nr) in enumerate(tiles):
                    b = dbanks[(i, n)]
                    for k in range(kt):
                        fidx = ks * 8 + k
                        last_mm = (ks == 2 and k == kt - 1)
                        P.op("pe", f_mm(ps[0:nr, b, :], hid[:, fidx, r0:r0 + nr], slab[:, k, :], ks == 0 and k == 0, last_mm),
                             reads=[sbf, hid_buf[fidx]], writes=[bank_buf[b]], signal=(k == kt - 1))
                drain(8)
        drain(10 ** 6)
        pn = PostNorm("g_ffn_post")
        for i, (r0, nr) in enumerate(tiles):
            if sample:
                dsty = ys_d[r0:r0 + nr, :]
            else:
                dsty = y_d[bi * NB + r0: bi * NB + r0 + nr, :]
            pn.add(dbanks[(i, 0)], dbanks[(i, 1)], nr, xr[0:nr, i, :], xb[i],
                   after=lambda i=i, nr=nr, dsty=dsty: P.dma("sp", f"x{xpar}{i}", dsty, xr[0:nr, i, :], reads=[xb[i]]))
            st["reserved"].discard(dbanks[(i, 0)])
            st["reserved"].discard(dbanks[(i, 1)])
        pn.finish()

    def mem_kv():
        mT, mTb = actT[1], actT_buf[1]
        srcs = []
        for m in range(2):
            si = rot("tmp", 2)
            P.dma("sp", f"tmpl{si}", tmp[si][:], memp_d[m * 128:(m + 1) * 128, :], writes=[tmp_buf[si]])
            srcs.append((tmp[si][:, :], tmp_buf[si], 128, m * 128))
        norm_tiles(srcs, "g_mem_kv", mT, mTb)
        for n in range(2):
            slab, sbf = use_slab("w_mk", n)
            proj_fm(slab, sbf, 4, mT, mTb, 256, lambda f, b, n=n: P.op(
                "act", f_act(mkT[:, 4 * n + f, :], ps[:, b, 0:256], AF.Copy), reads=[bank_buf[b]], writes=[mkT_buf]))
            for m in range(2):
                b = bank()
                proj_tm(slab, sbf, mT, mTb, m * 128, 128, b)
                store_rows(mko_d[m * 128:(m + 1) * 128, n * 512:(n + 1) * 512], b, 128, 512)
        for n in range(2):
            slab, sbf = use_slab("w_mv", n)
            for m in range(2):
                b = bank()
                proj_tm(slab, sbf, mT, mTb, m * 128, 128, b)
                P.op("dve", f_copy(mv[:, m, n * 512:(n + 1) * 512], ps[:, b, :]), reads=[bank_buf[b]], writes=[mv_buf])
                store_rows(mvo_d[m * 128:(m + 1) * 128, n * 512:(n + 1) * 512], b, 128, 512, extra=[mv_buf])

    def sample_prep():
        for s in range(2):
            si = rot("stage", 1)
            P.dma("sp", f"stg{si}", stage[si][0:30, 0:512], cc_d[s], writes=[stage_buf[si]])
            b = bank()
            for c in range(4):
                P.op("pe", f_tr(ps[:, b, c * 32:c * 32 + 30], stage[si][0:30, c * 128:(c + 1) * 128], identf[0:30, 0:30]),
                     reads=[stage_buf[si], const_buf], writes=[bank_buf[b]], signal=(c == 3))
            P.op("act", f_act(uTs[:, :, s, 0:30], ps[:, b, 0:128].rearrange("p (c j) -> p c j", c=4)[:, :, 0:30], AF.Copy, scale=2.0),
                 reads=[bank_buf[b]], writes=[uTs_buf])
            P.dma("sp", "copy", kso_d[s, 0:496, :], ck_d[s, 16:512, :])
            P.dma("sp", "copy", vso_d[s, 0:496, :], cv_d[s, 16:512, :])
            P.dma("sp", "copy", cso_d[s, 0:14, :], cc_d[s, 16:30, :])
        P.op("pool", f_memset(Vext[:, :, :, 64:65], 1.0), writes=[V_buf[0], V_buf[1]])
        emit_xload("sample", 0)

    import os
    kstop = int(os.environ.get("KSTOP", "99"))
    steps = []
    gate_dummy = Buf("gate_dummy")

    def gated_casts(names, gate_bufs):
        P.op("pool", f_memset(cst[:, 2:3], 0.0), reads=list(gate_bufs), writes=[gate_dummy])
        cast_group(names)
    def queue_late_casts():
        cast_pending.extend([("w_mq", 0), ("w_mq", 1), ("w_mo", 0), ("w_mo", 1)])
        for i in range(6):
            cast_pending.extend([("w_gate", i), ("w_up", i)])
        cast_pending.extend([("w_down", i) for i in range(6)])
    steps.append(halo_a)
    steps.append(lambda: pre1("prompt", 0, pre_begin("prompt", 0)))
    steps.append(lambda: (pre2("prompt", 0), drain(10 ** 6)))
    steps.append(lambda: (gated_casts(["w_out", "w_mk", "w_mv"], [uT_buf]), halo_b(),
                          build_btab(), queue_late_casts()))
    for bi in range(NBLK):
        if bi + 1 < NBLK:
            nbeg = (lambda b=bi: (emit_xload("prompt", b + 1), pre_begin("prompt", b + 1))[1])
            nmid = (lambda stt, b=bi: pre1("prompt", b + 1, stt))
            nxt = (lambda b=bi: pre2("prompt", b + 1))
        else:
            nbeg = (lambda: (sample_prep(), pre_begin("sample", 0))[1])
            nmid = (lambda stt: pre1("sample", 0, stt))
            nxt = (lambda: pre2("sample", 0))
        if bi == 0:
            steps.append(lambda nxt=nxt, nbeg=nbeg, nmid=nmid: main_stage("prompt", 0, nxt, mid_hook=mem_kv, next_begin=nbeg, next_mid=nmid))
        else:
            steps.append(lambda bi=bi, nxt=nxt, nbeg=nbeg, nmid=nmid: main_stage("prompt", bi, nxt, next_begin=nbeg, next_mid=nmid))
    steps.append(lambda: main_stage("sample", 0, None))
    for i, stp in enumerate(steps):
        if i >= kstop:
            break
        stp()
    if kstop >= len(steps):
        assert st["slab_pos"] == len(sched), (st["slab_pos"], len(sched))
        assert not pending

    sem_names = ["act", "dve", "pool", "pe"] + sorted(P.dval.keys())
    sems = {n: es.enter_context(nc.semaphore("s_" + n)) for n in sem_names}
    for n, v in P.dval.items():
        P.q["sp"].append(("wait", n, v))
    for e in ("act", "dve", "pool", "pe"):
        P.q["sp"].append(("wait", e, P.cnt[e]))

    def replay(engname, eobj):
        for it in P.q[engname]:
            if it[0] == "wait":
                eobj.wait_ge(sems[it[1]], it[2])
            elif it[0] == "op":
                ins = it[1](eobj)
                if it[2]:
                    ins.then_inc(sems[engname], 1)
            else:
                eobj.dma_start(out=it[1], in_=it[2]).then_inc(sems[it[3]], 16)

    with nc.Block() as block:
        @block.sync
        def _(e):
            replay("sp", e)

        @block.scalar
        def _(e):
            replay("act", e)

        @block.vector
        def _(e):
            replay("dve", e)

        @block.gpsimd
        def _(e):
            replay("pool", e)

        @block.tensor
        def _(e):
            replay("pe", e)
    es.close()
    nc._prog_stats = {e: len(P.q[e]) for e in P.ENG}
    return nc


_NC_CACHE = {}


def kernel(x_prompt, x_sample, cache_att_k, cache_att_v, cache_conv, cache_mem_k, cache_mem_v, mem_prompt,
           g_mix_pre, g_mix_post, w_in, rel_bias, conv_w, conv_b, cln_g, cln_b, w_out,
           g_mem_pre, g_mem_post, g_mem_kv, w_mq, w_mk, w_mv, w_mo, g_ffn_pre, g_ffn_post, w_gate, w_up, w_down):
    f = lambda a: np.ascontiguousarray(np.asarray(a, dtype=np.float32))
    x_prompt, x_sample = f(x_prompt), f(x_sample)
    if "nc" not in _NC_CACHE:
        _NC_CACHE["nc"] = build_nc()
    nc = _NC_CACHE["nc"]
    cvec = np.concatenate([f(conv_w)[0], f(conv_b)[0][None], f(cln_g)[0][None], f(cln_b)[0][None]], axis=0)
    shared = {
        "g_mix_pre": f(g_mix_pre), "g_mix_post": f(g_mix_post), "g_mem_pre": f(g_mem_pre), "g_mem_post": f(g_mem_post),
        "g_mem_kv": f(g_mem_kv), "g_ffn_pre": f(g_ffn_pre), "g_ffn_post": f(g_ffn_post),
        "rel_bias": f(rel_bias)[0], "cvec": f(cvec),
        "w_in": f(w_in)[0], "w_out": f(w_out)[0], "w_mq": f(w_mq)[0], "w_mk": f(w_mk)[0], "w_mv": f(w_mv)[0],
        "w_mo": f(w_mo)[0], "w_gate": f(w_gate)[0], "w_up": f(w_up)[0], "w_down": f(w_down)[0],
    }
    ck, cv = f(cache_att_k)[0], f(cache_att_v)[0]
    cc, cmk, cmv = f(cache_conv)[0], f(cache_mem_k)[0], f(cache_mem_v)[0]
    memp = f(mem_prompt)
    in_maps = []
    for c in range(8):
        b, qt = c // 4, c % 4
        m = dict(shared)
        m["xp"] = x_prompt[b, qt * 4096:(qt + 1) * 4096]
        m["xh"] = x_prompt[b, qt * 4096 - 512:qt * 4096] if qt > 0 else np.zeros((512, D), np.float32)
        m["vflag"] = np.full((128, 1), 1.0 if qt > 0 else 0.0, np.float32)
        m["xs"] = x_sample[2 * c:2 * c + 2].reshape(32, D)
        m["ck"] = ck[2 * c:2 * c + 2].reshape(2, 512, 512)
        m["cv"] = cv[2 * c:2 * c + 2].reshape(2, 512, 512)
        m["cc"] = cc[2 * c:2 * c + 2]
        m["cmk"] = cmk[2 * c:2 * c + 2].reshape(2, 256, D)
        m["cmv"] = cmv[2 * c:2 * c + 2].reshape(2, 256, D)
        m["memp"] = memp[b]
        in_maps.append({k: np.ascontiguousarray(v) for k, v in m.items()})
    res = run_bass_kernel_spmd(nc, in_maps, core_ids=list(range(8)))
    R = res.results
    y_prompt = np.stack([np.concatenate([R[4 * b + q]["y"] for q in range(4)], axis=0) for b in range(2)], axis=0)
    y_sample = np.concatenate([R[c]["ys"].reshape(2, 16, D) for c in range(8)], axis=0)
    nk = np.stack([R[4 * b + 3]["ktail"].reshape(512, 8, 64) for b in range(2)], axis=0)[None]
    nv = np.stack([R[4 * b + 3]["vtail"].reshape(512, 8, 64) for b in range(2)], axis=0)[None]
    ncv = np.stack([R[4 * b + 3]["ctail"] for b in range(2)], axis=0)[None]
    nmk = np.stack([R[4 * b]["mko"].reshape(256, 4, 256) for b in range(2)], axis=0)[None]
    nmv = np.stack([R[4 * b]["mvo"].reshape(256, 4, 256) for b in range(2)], axis=0)[None]
    ks = np.concatenate([R[c]["kso"].reshape(2, 512, 8, 64) for c in range(8)], axis=0)[None]
    vs = np.concatenate([R[c]["vso"].reshape(2, 512, 8, 64) for c in range(8)], axis=0)[None]
    cs = np.concatenate([R[c]["cso"] for c in range(8)], axis=0)[None]
    out = (y_prompt, y_sample, nk, nv, ncv, nmk, nmv, ks, vs, cs)
    return tuple(np.ascontiguousarray(o, dtype=np.float32) for o in out)
```

```python
import contextlib
import numpy as np
import concourse.bass as bass
import concourse.mybir as mybir
from concourse.bass_utils import run_bass_kernel_spmd

F32 = mybir.dt.float32
BF16 = mybir.dt.bfloat16
AF = mybir.ActivationFunctionType
ALU = mybir.AluOpType

D = 1024
NBLK = 8
NB = 512
D_FF = 2816
EPS = 1e-6
NEG = -1e30
DEBUG = False


class Buf:
    __slots__ = ("name", "w", "r")

    def __init__(self, name):
        self.name = name
        self.w = None
        self.r = {}


class Prog:
    ENG = ("sp", "act", "dve", "pool", "pe")
    SAME_WAIT = {"pool", "act", "dve"}

    def __init__(self):
        self.q = {e: [] for e in self.ENG}
        self.cnt = {e: 0 for e in self.ENG}
        self.seen = {e: {} for e in self.ENG}
        self.dval = {}

    def _waits(self, eng, reads, writes):
        need = {}
        for b in reads:
            if b.w is not None:
                need[b.w[0]] = max(need.get(b.w[0], 0), b.w[1])
        for b in writes:
            if b.w is not None:
                need[b.w[0]] = max(need.get(b.w[0], 0), b.w[1])
            for k, v in b.r.items():
                need[k] = max(need.get(k, 0), v)
        for k, v in need.items():
            if k == eng and eng not in self.SAME_WAIT:
                continue
            if self.seen[eng].get(k, 0) >= v:
                continue
            self.seen[eng][k] = v
            self.q[eng].append(("wait", k, v))

    def op(self, eng, fn, reads=(), writes=(), signal=True, touch=()):
        self._waits(eng, reads, writes)
        if signal:
            self.cnt[eng] += 1
            tok = (eng, self.cnt[eng])
        else:
            tok = (eng, self.cnt[eng] + 1)
        self.q[eng].append(("op", fn, signal))
        for b in list(reads) + list(touch):
            b.r[eng] = max(b.r.get(eng, 0), tok[1])
        for b in writes:
            b.w = tok
            b.r = {}
        return tok

    def dma(self, qeng, sem, out, in_, reads=(), writes=()):
        self._waits(qeng, reads, writes)
        self.dval[sem] = self.dval.get(sem, 0) + 16
        tok = (sem, self.dval[sem])
        self.q[qeng].append(("dma", out, in_, sem))
        for b in reads:
            b.r[sem] = max(b.r.get(sem, 0), tok[1])
        for b in writes:
            b.w = tok
            b.r = {}
        return tok


def f_mm(out, lhsT, rhs, start, stop):
    return lambda e: e.matmul(out, lhsT=lhsT, rhs=rhs, start=start, stop=stop)


def f_tr(out, in_, ident):
    return lambda e: e.transpose(out=out, in_=in_, identity=ident)


def f_act(out, in_, func, scale=None, bias=None, accum=None):
    def f(e):
        kw = {}
        if scale is not None:
            kw["scale"] = scale
        if bias is not None:
            kw["bias"] = bias
        if accum is not None:
            kw["accum_out"] = accum
        return e.activation(out=out, in_=in_, func=func, **kw)
    return f


def f_tt(out, a, b, op):
    return lambda e: e.tensor_tensor(out=out, in0=a, in1=b, op=op)


def f_stt(out, a, s, b, op0, op1):
    return lambda e: e.scalar_tensor_tensor(out=out, in0=a, scalar=s, in1=b, op0=op0, op1=op1)


def f_ts(out, a, s1, s2, op0, op1):
    return lambda e: e.tensor_scalar(out=out, in0=a, scalar1=s1, scalar2=s2, op0=op0, op1=op1)


def f_ts1(out, a, s1, op0):
    return lambda e: e.tensor_scalar(out=out, in0=a, scalar1=s1, scalar2=None, op0=op0)


def f_copy(out, a):
    return lambda e: e.tensor_copy(out=out, in_=a)


def f_memset(ap, v):
    return lambda e: e.memset(ap, v)


def f_recip(out, a):
    return lambda e: e.reciprocal(out=out, in_=a)


def build_nc():
    nc = bass.Bass("TRN2", target_bir_lowering=False)
    P = Prog()
    es = contextlib.ExitStack()

    def din(name, shape):
        return nc.dram_tensor(name, list(shape), F32, kind="ExternalInput")

    def dout(name, shape):
        return nc.dram_tensor(name, list(shape), F32, kind="ExternalOutput")

    xp = din("xp", [4096, D]).ap()
    xh = din("xh", [512, D]).ap()
    vflag_d = din("vflag", [128, 1]).ap()
    xs_d = din("xs", [32, D]).ap()
    ck_d = din("ck", [2, 512, 512]).ap()
    cv_d = din("cv", [2, 512, 512]).ap()
    cc_d = din("cc", [2, 30, 512]).ap()
    cmk_d = din("cmk", [2, 256, D]).ap()
    cmv_d = din("cmv", [2, 256, D]).ap()
    memp_d = din("memp", [256, D]).ap()
    gv = {n: din(n, [1, D]).ap() for n in
          ("g_mix_pre", "g_mix_post", "g_mem_pre", "g_mem_post", "g_mem_kv", "g_ffn_pre", "g_ffn_post")}
    rb_d = din("rel_bias", [8, 257]).ap()
    cvec_d = din("cvec", [34, 512]).ap()
    W = {"w_in": din("w_in", [D, 2560]), "w_out": din("w_out", [D, D]), "w_mq": din("w_mq", [D, D]),
         "w_mk": din("w_mk", [D, D]), "w_mv": din("w_mv", [D, D]), "w_mo": din("w_mo", [D, D]),
         "w_gate": din("w_gate", [D, D_FF]), "w_up": din("w_up", [D, D_FF]), "w_down": din("w_down", [D_FF, D])}

    y_d = dout("y", [4096, D]).ap()
    ys_d = dout("ys", [32, D]).ap()
    ktail_d = dout("ktail", [512, 512]).ap()
    vtail_d = dout("vtail", [512, 512]).ap()
    ctail_d = dout("ctail", [30, 512]).ap()
    mko_d = dout("mko", [256, D]).ap()
    mvo_d = dout("mvo", [256, D]).ap()
    kso_d = dout("kso", [2, 512, 512]).ap()
    vso_d = dout("vso", [2, 512, 512]).ap()
    cso_d = dout("cso", [2, 30, 512]).ap()

    nsl = {"w_in": 5, "w_out": 2, "w_mq": 2, "w_mk": 2, "w_mv": 2, "w_mo": 2, "w_gate": 6, "w_up": 6, "w_down": 6}
    wsc = {n: nc.dram_tensor("ws_" + n, [k, 128, 8, 512], BF16) for n, k in nsl.items()}
    wsc_buf = {n: Buf("ws_" + n) for n in nsl}
    per = nc.dram_tensor("per", [8, 129, 768], F32)
    per_buf = Buf("per")

    def slab_geom(name, idx):
        if name == "w_down":
            n, ks = idx // 3, idx % 3
            return ks * 1024, (8, 8, 6)[ks], n * 512, 512
        if name in ("w_gate", "w_up"):
            return 0, 8, idx * 512, min(512, D_FF - idx * 512)
        return 0, 8, idx * 512, 512

    def sb(name, shape, dtype):
        return es.enter_context(nc.sbuf_tensor("t_" + name, list(shape), dtype))

    NSLOT = 4
    wring = [sb(f"wring{i}", [128, 8, 512], BF16) for i in range(NSLOT)]
    wring_buf = [Buf(f"wring{i}") for i in range(NSLOT)]
    xres = [sb(f"xres{i}", [128, 4, D], F32) for i in range(2)]
    xres_buf = [[Buf(f"xres{p}_{i}") for i in range(4)] for p in range(2)]
    actT = [sb(f"actT{i}", [128, 8, 512], BF16) for i in range(2)]
    actT_buf = [Buf(f"actT{i}") for i in range(2)]
    oT_kbuf = [Buf(f"oTk{k}") for k in range(8)]
    qT = sb("qT", [128, 4, 512], BF16)
    qT_buf = Buf("qT")
    kT = sb("kT", [128, 4, 1024], BF16)
    kT_buf = [Buf("kT0"), Buf("kT1")]
    Vext = sb("Vext", [128, 8, 8, 65], BF16)
    V_buf = [Buf("V0"), Buf("V1")]
    uT = sb("uT", [128, 4, 30 + 512], F32)
    uT_buf = Buf("uT")
    uTs = sb("uTs", [128, 4, 2, 46], F32)
    uTs_buf = Buf("uTs")
    cacc = sb("cacc", [128, 4, 512], F32)
    cacc_buf = [Buf(f"cacc{c}") for c in range(4)]
    sg = [sb(f"sg{i}", [128, 512], F32) for i in range(2)]
    sg_buf = [Buf(f"sg{i}") for i in range(2)]
    att = sb("att", [128, 512], BF16)
    att_buf = Buf("att")
    pT = [sb(f"pT{i}", [128, 5, 128], BF16) for i in range(3)]
    pT_buf = [Buf(f"pT{i}") for i in range(3)]
    Btab = sb("Btab", [128, 3, 8, 128], F32)
    Btab_buf = Buf("Btab")
    hid = sb("hid", [128, 22, 512], BF16)
    hid_buf = [Buf(f"hid{f}") for f in range(22)]
    pm = [sb(f"pm{i}", [128, 2, 512], BF16) for i in range(2)]
    pm_buf = [Buf(f"pm{i}") for i in range(2)]
    rs = [sb(f"rs{i}", [128, 512], F32) for i in range(3)]
    rs_buf = [Buf(f"rs{i}") for i in range(3)]
    tb = [rs[i][:, 0:384].rearrange("p (s n) -> p s n", s=3) for i in range(3)]
    tb_buf = rs_buf
    mkT = sb("mkT", [128, 8, 256], BF16)
    mkT_buf = Buf("mkT")
    mv = sb("mv", [128, 2, D], BF16)
    mv_buf = Buf("mv")
    gbuf = [sb(f"gbuf{i}", [128, D], F32) for i in range(2)]
    gbuf_buf = [Buf(f"gbuf{i}") for i in range(2)]
    hbf = [sb(f"hbf{i}", [128, D], BF16) for i in range(2)]
    hbf_buf = [Buf(f"hbf{i}") for i in range(2)]
    zc = [hbf[i // 2][:, (i % 2) * 512:(i % 2 + 1) * 512] for i in range(4)]
    zc_buf = [hbf_buf[i // 2] for i in range(4)]
    tmp = [sb(f"tmp{i}", [128, D], F32) for i in range(2)]
    tmp_buf = [Buf(f"tmp{i}") for i in range(2)]
    stage = [sb(f"stage{i}", [128, 512], F32) for i in range(1)]
    stage_buf = [Buf(f"stage{i}") for i in range(1)]
    ident = sb("ident", [128, 128], BF16)
    identf = sb("identf", [128, 128], F32)
    ones = sb("ones", [128, 128], BF16)
    const_buf = Buf("const")
    cvec_s = tmp[1]
    cvec = sb("cvecT", [128, 4, 34], F32)
    cvec_buf = Buf("cvec")
    cvs_buf = tmp_buf[1]
    frow = tmp[0]
    frow_buf = tmp_buf[0]
    vflag = sb("vflag_s", [128, 1], F32)
    vflag_buf = Buf("vflag")
    sm = sb("sm", [128, 64], F32)
    sm_buf = [Buf(f"sm{i}") for i in range(8)]
    cst = sb("cst", [128, 4], F32)
    bnst = sb("bnst", [128, 4, 8], F32)
    ps = es.enter_context(nc.psum_tensor("ps", [128, 8, 512], F32))
    bank_buf = [Buf(f"bank{i}") for i in range(8)]

    st = {"rr": 0, "reserved": set(), "slab_pos": 0, "sm": 0, "g": 0, "hbf": 0, "tmp": 0, "stage": 0, "sg": 0,
          "tb": 0, "pm": 0, "rs": 0, "hslot": 0}

    def bank(n=1):
        for _ in range(16):
            p = st["rr"]
            if n == 2 and p % 2 == 1:
                p = (p + 1) % 8
            cand = [(p + i) % 8 for i in range(n)]
            st["rr"] = (p + n) % 8
            if not any(c in st["reserved"] for c in cand):
                return cand if n > 1 else cand[0]
        raise RuntimeError(f"no psum bank n={n} reserved={sorted(st['reserved'])} rr={st['rr']}")

    def rot(key, n):
        v = st[key]
        st[key] = (v + 1) % n
        return v

    def bk(b):
        return ps[:, b, :]

    def bk_bf(b, k, n):
        return ps[:, b, :].bitcast(BF16).rearrange("p (k n) -> p k n", k=k)[:, :, 0:n]

    def blk_seq(with_next, first):
        seq = [("w_in", 0), ("w_out", 0), ("w_out", 1)]
        if with_next:
            seq += [("w_in", 1)]
        if first:
            seq += [("w_mk", 0), ("w_mk", 1), ("w_mv", 0), ("w_mv", 1)]
        seq += [("w_mq", 0), ("w_mq", 1), ("w_mo", 0), ("w_mo", 1)]
        if with_next:
            seq += [("w_in", 2), ("w_in", 3), ("w_in", 4)]
        seq += [x for s in range(6) for x in (("w_gate", s), ("w_up", s))]
        seq += [("w_down", i) for i in range(6)]
        return seq
    sched = [("w_in", 3), ("w_in", 4), ("w_in", 1), ("w_in", 2), ("w_in", 3), ("w_in", 4), ("w_in", 1), ("w_in", 2)]
    for _b in range(NBLK + 1):
        sched += blk_seq(_b < NBLK, _b == 0)
    DEPTH = 3
    loaded = {"n": 0}

    def issue_loads(upto):
        while loaded["n"] < min(upto, len(sched)):
            p = loaded["n"]
            name, idx = sched[p]
            r0, kt, c0, ncols = slab_geom(name, idx)
            slot = p % NSLOT
            ensure_cast(name, idx)
            P.dma("sp", f"wr{slot}", wring[slot][:, 0:kt, 0:ncols], wsc[name].ap()[idx][:, 0:kt, 0:ncols],
                  reads=[slab_cast_buf[(name, idx)]], writes=[wring_buf[slot]])
            loaded["n"] += 1

    def use_slab(name, idx):
        p = st["slab_pos"]
        assert sched[p] == (name, idx), (p, sched[p], name, idx)
        issue_loads(p + DEPTH)
        emit_casts(2)
        st["slab_pos"] = p + 1
        slot = p % NSLOT
        return wring[slot], wring_buf[slot]

    slab_cast_buf = {}
    cast_pending = []

    def cast_one(name, idx):
        r0, kt, c0, ncols = slab_geom(name, idx)
        src = W[name].ap()[r0:r0 + kt * 128, c0:c0 + ncols].rearrange("(ko p) c -> p ko c", p=128)
        bf = Buf(f"ws_{name}_{idx}")
        P.dma("pool", f"cast_{name}_{idx}", wsc[name].ap()[idx][:, 0:kt, 0:ncols], src, writes=[bf])
        slab_cast_buf[(name, idx)] = bf

    def emit_casts(n):
        k = 0
        while cast_pending and k < n:
            cast_one(*cast_pending.pop(0))
            k += 1

    def ensure_cast(name, idx):
        while (name, idx) not in slab_cast_buf:
            assert cast_pending, ("no cast scheduled for", name, idx)
            cast_one(*cast_pending.pop(0))

    def cast_group(names):
        for name in names:
            order = [3, 4, 1, 2, 0] if name == "w_in" else list(range(nsl[name]))
            for idx in order:
                r0, kt, c0, ncols = slab_geom(name, idx)
                src = W[name].ap()[r0:r0 + kt * 128, c0:c0 + ncols].rearrange("(ko p) c -> p ko c", p=128)
                bf = Buf(f"ws_{name}_{idx}")
                P.dma("pool", f"cast_{name}_{idx}", wsc[name].ap()[idx][:, 0:kt, 0:ncols], src, writes=[bf])
                slab_cast_buf[(name, idx)] = bf

    P.op("pool", f_memset(ident[:], 0.0), writes=[const_buf])
    P.op("pool", lambda e: e.affine_select(out=ident[:], in_=ident[:], pattern=[[-1, 128]], compare_op=ALU.not_equal,
                                           fill=1.0, base=0, channel_multiplier=1), writes=[const_buf])
    P.op("pool", f_memset(identf[:], 0.0), writes=[const_buf])
    P.op("pool", lambda e: e.affine_select(out=identf[:], in_=identf[:], pattern=[[-1, 128]], compare_op=ALU.not_equal,
                                           fill=1.0, base=0, channel_multiplier=1), writes=[const_buf])
    P.op("pool", f_memset(ones[:], 1.0), writes=[const_buf])
    P.op("pool", f_memset(cst[:, 0:1], -0.5), writes=[const_buf])
    P.op("pool", f_memset(cst[:, 1:2], EPS), writes=[const_buf])
    P.op("pool", f_memset(Vext[:], 1.0), writes=[V_buf[0], V_buf[1]])
    P.op("pool", f_memset(uT[:], 0.0), writes=[uT_buf])

    cast_group(["w_in"])
    for i in range(4):
        P.dma("sp", f"x1{i}", xres[1][:, i, :], xh[i * 128:(i + 1) * 128, :], writes=[xres_buf[1][i]])
    for i in range(4):
        P.dma("sp", f"x0{i}", xres[0][:, i, :], xp[i * 128:(i + 1) * 128, :], writes=[xres_buf[0][i]])
    P.dma("sp", "setup", vflag[:], vflag_d, writes=[vflag_buf])
    P.dma("sp", "setup", cvec_s[0:34, 0:512], cvec_d, writes=[cvs_buf])
    P.dma("sp", "setup", frow[0:8, 0:256], rb_d[:, 1:257], writes=[frow_buf])
    setup_tok = ("setup", P.dval["setup"])
    vflag_buf.w = cvs_buf.w = frow_buf.w = setup_tok
    P.op("dve", f_copy(frow[0:8, 256:768], frow[0:8, 255:256].to_broadcast([8, 512])), reads=[frow_buf], writes=[frow_buf])
    for h in range(8):
        P.dma("pool", "per", per.ap()[h:h + 1], frow[h:h + 1, 0:768].unsqueeze(1).to_broadcast([1, 129, 768]),
              reads=[frow_buf], writes=[])
    per_buf.w = ("per", P.dval["per"])

    b = bank()
    for c in range(4):
        P.op("pe", f_tr(ps[:, b, c * 34:(c + 1) * 34], cvec_s[0:34, c * 128:(c + 1) * 128], identf[0:34, 0:34]),
             reads=[cvs_buf, const_buf], writes=[bank_buf[b]], signal=(c == 3))
    P.op("dve", f_copy(cvec[:].rearrange("p c j -> p (c j)"), ps[:, b, 0:136]), reads=[bank_buf[b]], writes=[cvec_buf])
    P.op("dve", f_ts1(cvec[:, :, 0:31], cvec[:, :, 0:31], 0.5, ALU.mult), reads=[cvec_buf], writes=[cvec_buf])

    def build_btab():
        for si, t in enumerate((0, 3, 4)):
            src = bass.AP(per, 639 - 128 * t, [[767, 128], [129 * 768, 8], [1, 128]])
            P.dma("sp", "btab", Btab[:, si], src, reads=[per_buf], writes=[])
        Btab_buf.w = ("btab", P.dval["btab"])
        P.op("pool", f_memset(Btab[0:64, 0, :, 64:128], NEG), reads=[], writes=[Btab_buf])
        P.op("pool", f_memset(Btab[64:128, 2, :, 0:64], NEG), reads=[], writes=[Btab_buf])

    def aslist(b):
        return list(b) if isinstance(b, (list, tuple)) else [b]

    def sm_cols(n=8):
        g = rot("sm", 8)
        return sm[:, g * 8:g * 8 + n], sm_buf[g], g

    def load_g(name):
        i = rot("g", 2)
        P.dma("sp", f"g{i}", gbuf[i][:], gv[name].to_broadcast([128, D]), writes=[gbuf_buf[i]])
        return gbuf[i], gbuf_buf[i]

    def rstd_from_ss(col_ap, nr, smb, n_terms):
        if n_terms == 2:
            P.op("pool", f_tt(col_ap[0:nr, 0:1], col_ap[0:nr, 0:1], col_ap[0:nr, 1:2], ALU.add), reads=[smb], writes=[smb])
        P.op("pool", f_ts(col_ap[0:nr, 3:4], col_ap[0:nr, 0:1], 1.0 / D, EPS, ALU.mult, ALU.add), reads=[smb], writes=[smb])
        P.op("pool", f_tt(col_ap[0:nr, 2:3], col_ap[0:nr, 3:4], cst[0:nr, 0:1], ALU.pow), reads=[smb, const_buf], writes=[smb])

    def norm_begin(srcs, g_name, n_early=2, bufs=None):
        g_t, g_b = load_g(g_name)
        info = []
        for (src_ap, src_buf, nr, col0) in srcs:
            cols, smb, _ = sm_cols()
            if bufs is None:
                hi = rot("hbf", 2)
                hb_ap, hb_buf = hbf[hi], hbf_buf[hi]
            else:
                hb_ap, hb_buf = bufs[len(info) % len(bufs)]
            P.op("act", f_act(hb_ap[0:nr, :], src_ap, AF.Square, accum=cols[0:nr, 0:1]), reads=[src_buf], writes=[smb, hb_buf])
            info.append([cols, smb, (hb_ap, hb_buf), False])
        for (src_ap, src_buf, nr, col0), (cols, smb, hi, _) in zip(srcs, info):
            rstd_from_ss(cols, nr, smb, 1)
        stt = {"srcs": srcs, "info": info, "g": (g_t, g_b)}
        for i in range(min(n_early, len(srcs))):
            norm_h(stt, i)
        return stt

    def norm_h(stt, i):
        (src_ap, src_buf, nr, col0) = stt["srcs"][i]
        cols, smb, hi, done = stt["info"][i]
        if done:
            return
        g_t, g_b = stt["g"]
        hb_ap, hb_buf = hi
        P.op("dve", f_stt(hb_ap[0:nr, :], src_ap, cols[0:nr, 2:3], g_t[0:nr, :], ALU.mult, ALU.mult),
             reads=[src_buf, smb, g_b], writes=[hb_buf])
        stt["info"][i][3] = True

    def norm_finish(stt, dstT, dstT_buf):
        for i, (src_ap, src_buf, nr, col0) in enumerate(stt["srcs"]):
            norm_h(stt, i)
            cols, smb, (hb_ap, hb_buf), _ = stt["info"][i]
            b = bank()
            pv = bk_bf(b, 8, 128)
            for k in range(8):
                P.op("pe", f_tr(pv[:, k, 0:nr], hb_ap[0:nr, k * 128:(k + 1) * 128], ident[0:nr, 0:nr]),
                     reads=[hb_buf, const_buf], writes=[bank_buf[b]], signal=(k == 7))
            P.op("act", f_act(dstT[:, :, col0:col0 + nr], pv[:, :, 0:nr], AF.Copy), reads=[bank_buf[b]], writes=aslist(dstT_buf))

    def row_buffers():
        return [(hbf[0], hbf_buf[0]), (hbf[1], hbf_buf[1]),
                (pm[0][:].rearrange("p a b -> p (a b)"), pm_buf[0]), (pm[1][:].rearrange("p a b -> p (a b)"), pm_buf[1])]

    def norm_tiles(srcs, g_name, dstT, dstT_buf, wide=False):
        if wide:
            norm_finish(norm_begin(srcs, g_name, n_early=4, bufs=row_buffers()), dstT, dstT_buf)
        else:
            norm_finish(norm_begin(srcs, g_name, n_early=0), dstT, dstT_buf)

    def proj_fm(slab, slab_b, ncol_tiles, src, src_buf, N, evac):
        for f in range(ncol_tiles):
            b = bank()
            for k in range(8):
                P.op("pe", f_mm(ps[:, b, 0:N], slab[:, k, f * 128:(f + 1) * 128], src[:, k, 0:N], k == 0, k == 7),
                     reads=[slab_b] + aslist(src_buf), writes=[bank_buf[b]], signal=(k == 7))
            evac(f, b)

    def proj_tm(slab, slab_b, src, src_buf, c0, nr, b, kt=8, k0=0, start=True, stop=True, ncols=512):
        for k in range(kt):
            P.op("pe", f_mm(ps[0:nr, b, 0:ncols], src[:, k0 + k, c0:c0 + nr], slab[:, k, 0:ncols],
                            start and k == 0, stop and k == kt - 1),
                 reads=[slab_b] + aslist(src_buf), writes=[bank_buf[b]], signal=(stop and k == kt - 1))

    def store_rows(dst_ap, src_bank, nr, ncols, scale=None, col0=0, extra=()):
        si = rot("stage", 1)
        if scale is None:
            P.op("act", f_act(stage[si][0:nr, 0:ncols], ps[0:nr, src_bank, col0:col0 + ncols], AF.Copy),
                 reads=[bank_buf[src_bank]] + list(extra), writes=[stage_buf[si]])
        else:
            P.op("act", f_act(stage[si][0:nr, 0:ncols], ps[0:nr, src_bank, col0:col0 + ncols], AF.Copy, scale=scale),
                 reads=[bank_buf[src_bank]] + list(extra), writes=[stage_buf[si]])
        P.dma("sp", f"stg{si}", dst_ap, stage[si][0:nr, 0:ncols], reads=[stage_buf[si]])

    class PostNorm:
        def __init__(self, g_name):
            self.g_t, self.g_b = load_g(g_name)
            self.pend = None

        def add(self, b0, b1, nr, x_ap, x_buf, after=None):
            g_t, g_b = self.g_t, self.g_b
            cols, smb, _ = sm_cols()
            ti = rot("tmp", 2)
            for n, bb in enumerate((b0, b1)):
                P.op("act", f_act(tmp[ti][0:nr, n * 512:(n + 1) * 512], ps[0:nr, bb, :], AF.Square, accum=cols[0:nr, n:n + 1]),
                     reads=[bank_buf[bb]], writes=[smb, tmp_buf[ti]])
            for n, bb in enumerate((b0, b1)):
                P.op("dve", f_tt(tmp[ti][0:nr, n * 512:(n + 1) * 512], ps[0:nr, bb, :], g_t[0:nr, n * 512:(n + 1) * 512], ALU.mult),
                     reads=[bank_buf[bb], smb, g_b], writes=[tmp_buf[ti]])
            rstd_from_ss(cols, nr, smb, 2)

            def xupd():
                P.op("dve", f_stt(x_ap, tmp[ti][0:nr, :], cols[0:nr, 2:3], x_ap, ALU.mult, ALU.add),
                     reads=[tmp_buf[ti], smb, x_buf], writes=[x_buf])
                if after is not None:
                    after()
            if self.pend is not None:
                self.pend()
            self.pend = xupd

        def finish(self):
            if self.pend is not None:
                self.pend()
            self.pend = None

    pending = []

    def drain(n):
        k = 0
        while pending and k < n:
            pending.pop(0)()
            k += 1

    def hslot():
        j = rot("hslot", 11)
        ap = hid[:, 2 * j:2 * j + 2, :].rearrange("p a b -> p (a b)").bitcast(F32)
        return ap, [hid_buf[2 * j], hid_buf[2 * j + 1]], f"hst{j}"

    SLOT = {1: 0, 2: 1, 0: 2, 3: 3, 4: 4}

    def attention_block(segs, dstT, dstT_buf):
        obs = {}

        def s_stage(si, h):
            q0, nq, ktiles = segs[si]
            base = 64 * (h % 2)
            f = h // 2
            sbk = bank(2)
            assert sbk[1] == sbk[0] + 1
            spv = ps[:, sbk[0]:sbk[0] + 2, :].rearrange("p b n -> p (b n)")
            sbufs = [bank_buf[sbk[0]], bank_buf[sbk[1]]]
            for t in range(5):
                kcol, nk, vt, kb, vb = ktiles[t]
                s = SLOT[t]
                P.op("pe", f_mm(spv[0:nk, s * 128:s * 128 + nq], kT[base:base + 64, f, kcol:kcol + nk],
                                qT[base:base + 64, f, q0:q0 + nq], True, True),
                     reads=[kb, qT_buf], writes=sbufs, signal=(t == 4))
            ti = rot("tb", 3)
            s3 = spv[:, 256:640].rearrange("p (s n) -> p s n", s=3)[:, :, 0:nq]
            P.op("dve", f_tt(tb[ti][:, :, 0:nq], s3, Btab[:, :, h, 0:nq], ALU.add),
                 reads=sbufs + [Btab_buf], writes=[tb_buf[ti]])
            s2 = spv[:, 0:256].rearrange("p (s n) -> p s n", s=2)[:, :, 0:nq]
            P.op("act", f_act(pT[ti][:, 0:2, 0:nq], s2, AF.Exp, bias=Btab[:, 0, h, 0:1]),
                 reads=sbufs + [Btab_buf, tb_buf[ti]], writes=[pT_buf[ti]])
            P.op("act", f_act(pT[ti][:, 2:5, 0:nq], tb[ti][:, :, 0:nq], AF.Exp), reads=[tb_buf[ti]], writes=[pT_buf[ti]])
            return ti

        def pv_stage(si, h, ti):
            q0, nq, ktiles = segs[si]
            if h == 0:
                ob = bank(2)
                st["reserved"].add(ob[0])
                st["reserved"].add(ob[1])
                obs[si] = ob
            ob = obs[si]
            o = ob[h // 4]
            hc = (h % 4) * 65
            for t in range(5):
                kcol, nk, vt, kb, vb = ktiles[t]
                P.op("pe", f_mm(ps[0:nq, o, hc:hc + 65], pT[ti][0:nk, SLOT[t], 0:nq], Vext[0:nk, vt, h, :], t == 0, t == 4),
                     reads=[pT_buf[ti], vb], writes=[bank_buf[o]], signal=(t == 4))
            if h == 7:
                cols, smb, _ = sm_cols()
                for i2 in range(2):
                    ov = ps[0:nq, ob[i2], 0:260].rearrange("p (h d) -> p h d", h=4)
                    P.op("dve", f_recip(cols[0:nq, i2 * 4:i2 * 4 + 4], ov[:, :, 64]), reads=[bank_buf[ob[i2]]], writes=[smb])
                    P.op("dve", f_tt(att[0:nq, i2 * 256:(i2 + 1) * 256].rearrange("p (h d) -> p h d", h=4), ov[:, :, 0:64],
                                     cols[0:nq, i2 * 4:i2 * 4 + 4].unsqueeze(2).to_broadcast([nq, 4, 64]), ALU.mult),
                         reads=[bank_buf[ob[i2]], smb], writes=[att_buf])
                st["reserved"].discard(ob[0])
                st["reserved"].discard(ob[1])
                b = bank()
                pv = bk_bf(b, 4, 128)
                for c in range(4):
                    P.op("pe", f_tr(pv[:, c, 0:nq], att[0:nq, c * 128:(c + 1) * 128], ident[0:nq, 0:nq]),
                         reads=[att_buf, const_buf], writes=[bank_buf[b]], signal=(c == 3))
                P.op("act", f_act(dstT[:, 0:4, q0:q0 + nq], pv[:, :, 0:nq], AF.Copy), reads=[bank_buf[b]], writes=[dstT_buf])

        fifo = []
        for si in range(len(segs)):
            for h in range(8):
                ti = s_stage(si, h)
                fifo.append((si, h, ti))
                if len(fifo) > 2:
                    pv_stage(*fifo.pop(0))
        while fifo:
            pv_stage(*fifo.pop(0))

    def mem_attention(q2T, q2T_buf, q0, nq, oT, oT_buf):
        def s_stage(h):
            pi = rot("pm", 2)
            for m in range(2):
                b = bank()
                for dk in range(2):
                    P.op("pe", f_mm(ps[:, b, 0:nq], mkT[:, 2 * h + dk, m * 128:(m + 1) * 128], q2T[:, 2 * h + dk, q0:q0 + nq],
                                    dk == 0, dk == 1), reads=[mkT_buf, q2T_buf], writes=[bank_buf[b]], signal=(dk == 1))
                P.op("act", f_act(pm[pi][:, m, 0:nq], ps[:, b, 0:nq], AF.Exp), reads=[bank_buf[b]], writes=[pm_buf[pi]])
            return pi

        def o_stage(h, pi):
            b = bank()
            for m in range(2):
                P.op("pe", f_mm(ps[:, b, 0:nq], ones[:], pm[pi][:, m, 0:nq], m == 0, m == 1),
                     reads=[pm_buf[pi], const_buf], writes=[bank_buf[b]], signal=(m == 1))
            ri = rot("rs", 3)
            P.op("dve", f_recip(rs[ri][:, 0:nq], ps[:, b, 0:nq]), reads=[bank_buf[b]], writes=[rs_buf[ri]])
            for dk in range(2):
                b = bank()
                for m in range(2):
                    c0 = 256 * h + 128 * dk
                    P.op("pe", f_mm(ps[:, b, 0:nq], mv[:, m, c0:c0 + 128], pm[pi][:, m, 0:nq], m == 0, m == 1),
                         reads=[pm_buf[pi], mv_buf], writes=[bank_buf[b]], signal=(m == 1))
                P.op("dve", f_tt(oT[:, 2 * h + dk, q0:q0 + nq], ps[:, b, 0:nq], rs[ri][:, 0:nq], ALU.mult),
                     reads=[bank_buf[b], rs_buf[ri]], writes=[oT_buf, oT_kbuf[2 * h + dk]])
        prev = None
        for h in range(4):
            pi = s_stage(h)
            if prev is not None:
                o_stage(*prev)
            prev = (h, pi)
        o_stage(*prev)

    A, B = 0, 1

    def geom(kind, bi):
        sample = kind == "sample"
        tiles = [(0, 32)] if sample else [(i * 128, 128) for i in range(4)]
        N = 32 if sample else 512
        if kind == "halo":
            cur, xpar = 1, 1
        elif sample:
            cur, xpar = 1, 0
        else:
            cur, xpar = bi % 2, bi % 2
        return sample, tiles, N, cur, xpar

    def emit_xload(kind, bi):
        if kind == "halo":
            for i in range(4):
                P.dma("sp", f"x1{i}", xres[1][:, i, :], xh[i * 128:(i + 1) * 128, :], writes=[xres_buf[1][i]])
        elif kind == "sample":
            P.dma("sp", "x00", xres[0][0:32, 0, :], xs_d[:, :], writes=[xres_buf[0][0]])
        else:
            xpar = bi % 2
            for i in range(4):
                P.dma("sp", f"x{xpar}{i}", xres[xpar][:, i, :], xp[bi * NB + i * 128: bi * NB + (i + 1) * 128, :],
                      writes=[xres_buf[xpar][i]])

    def glu(kind, bi, src=None):
        sample, tiles, N, cur, xpar = geom(kind, bi)
        hT, hTb = src if src is not None else (actT[A], actT_buf[A])
        slab_v, sbf_v = use_slab("w_in", 3)
        slab_g, sbf_g = use_slab("w_in", 4)
        for c in range(4):
            bv = bank()
            for k in range(8):
                P.op("pe", f_mm(ps[:, bv, 0:N], slab_v[:, k, c * 128:(c + 1) * 128], hT[:, k, 0:N], k == 0, k == 7),
                     reads=[sbf_v] + aslist(hTb), writes=[bank_buf[bv]], signal=(k == 7))
            bg = bank()
            for k in range(8):
                P.op("pe", f_mm(ps[:, bg, 0:N], slab_g[:, k, c * 128:(c + 1) * 128], hT[:, k, 0:N], k == 0, k == 7),
                     reads=[sbf_g] + aslist(hTb), writes=[bank_buf[bg]], signal=(k == 7))
            gi = rot("sg", 2)
            P.op("act", f_act(sg[gi][:, 0:N], ps[:, bg, 0:N], AF.Tanh, scale=0.5), reads=[bank_buf[bg]], writes=[sg_buf[gi]])
            if sample:
                dst = uTs[:, c, :, 30:46]
                srcs = sg[gi][:, 0:32].rearrange("p (s t) -> p s t", s=2)
                srcv = ps[:, bv, 0:32].rearrange("p (s t) -> p s t", s=2)
                P.op("dve", f_stt(dst, srcs, 1.0, srcv, ALU.add, ALU.mult), reads=[sg_buf[gi], bank_buf[bv]], writes=[uTs_buf])
            else:
                P.op("dve", f_stt(uT[:, c, 30:30 + N], sg[gi][:, 0:N], 1.0, ps[:, bv, 0:N], ALU.add, ALU.mult),
                     reads=[sg_buf[gi], bank_buf[bv]], writes=[uT_buf])

    def queue_conv(kind, bi):
        sample, tiles, N, cur, xpar = geom(kind, bi)
        for j in range(31):
            for c in range(4):
                if sample:
                    src = uTs[:, c, :, j:j + 16]
                    dstc = cacc[:, c, 0:32].rearrange("p (s t) -> p s t", s=2)
                    ub = uTs_buf
                else:
                    src = uT[:, c, j:j + N]
                    dstc = cacc[:, c, 0:N]
                    ub = uT_buf
                if j == 0:
                    pending.append(lambda src=src, dstc=dstc, ub=ub, c=c: P.op(
                        "dve", f_ts(dstc, src, cvec[:, c, 0:1], cvec[:, c, 31:32], ALU.mult, ALU.add),
                        reads=[ub, cvec_buf], writes=[cacc_buf[c]]))
                else:
                    pending.append(lambda src=src, dstc=dstc, ub=ub, c=c, j=j: P.op(
                        "dve", f_stt(dstc, src, cvec[:, c, j:j + 1], dstc, ALU.mult, ALU.add),
                        reads=[ub, cvec_buf], writes=[cacc_buf[c]]))

    def conv_tail_out(kind, bi):
        sample, tiles, N, cur, xpar = geom(kind, bi)
        nseg = 2 if sample else 1
        for s in range(nseg):
            b = bank()
            nrow = 16 if sample else 30
            for c in range(4):
                srcu = uTs[:, c, s, 30:46] if sample else uT[:, c, N:N + 30]
                P.op("pe", f_tr(ps[0:nrow, b, c * 128:(c + 1) * 128], srcu, identf[:, :]),
                     reads=[uTs_buf if sample else uT_buf, const_buf], writes=[bank_buf[b]], signal=(c == 3))
            store_rows(cso_d[s, 14:30, :] if sample else ctail_d[:, :], b, nrow, 512, scale=0.5)

    H1 = hid[:, 0:8, :]
    H1b = [hid_buf[f] for f in range(8)]

    def pre_begin(kind, bi):
        sample, tiles, N, cur, xpar = geom(kind, bi)
        xr, xb = xres[xpar], xres_buf[xpar]
        return norm_begin([(xr[0:nr, i, :], xb[i], nr, r0) for i, (r0, nr) in enumerate(tiles)], "g_mix_pre",
                          n_early=4, bufs=row_buffers())

    def pre1(kind, bi, stt):
        sample, tiles, N, cur, xpar = geom(kind, bi)
        last = (kind == "prompt" and bi == NBLK - 1)
        norm_finish(stt, H1, H1b)
        slab, sbf = use_slab("w_in", 1)
        kc0 = cur * 512
        proj_fm(slab, sbf, 4, H1, H1b, N, lambda f, b: P.op(
            "act", f_act(kT[:, f, kc0:kc0 + N], ps[:, b, 0:N], AF.Copy), reads=[bank_buf[b]], writes=[kT_buf[cur]]))
        if last:
            for i, (r0, nr) in enumerate(tiles):
                b = bank()
                proj_tm(slab, sbf, H1, H1b, r0, nr, b)
                store_rows(ktail_d[r0:r0 + nr, :], b, nr, 512)
        if sample:
            for s in range(2):
                b = bank()
                proj_tm(slab, sbf, H1, H1b, s * 16, 16, b)
                store_rows(kso_d[s, 496:512, :], b, 16, 512)

    def pre2(kind, bi):
        sample, tiles, N, cur, xpar = geom(kind, bi)
        last = (kind == "prompt" and bi == NBLK - 1)
        slab, sbf = use_slab("w_in", 2)
        if kind == "prompt" and bi == 1:
            P.op("pool", f_memset(Vext[:, 4 * cur:4 * cur + 4, :, 64:65], 1.0), writes=[V_buf[cur]])
        vsegs = [(s * 16, 16, 4 * cur + s) for s in range(2)] if sample else [(r0, nr, 4 * cur + i) for i, (r0, nr) in enumerate(tiles)]
        for si_, (c0, nr, vt) in enumerate(vsegs):
            b = bank()
            proj_tm(slab, sbf, H1, H1b, c0, nr, b)
            P.op("act", f_act(Vext[0:nr, vt, :, 0:64], ps[0:nr, b, :].rearrange("p (h d) -> p h d", h=8), AF.Copy),
                 reads=[bank_buf[b]], writes=[V_buf[cur]])
            if last:
                store_rows(vtail_d[c0:c0 + nr, :], b, nr, 512)
            if sample:
                store_rows(vso_d[si_, 496:512, :], b, 16, 512)
        if not sample:
            P.op("dve", f_copy(uT[:, :, 0:30], uT[:, :, 512:542]), reads=[uT_buf], writes=[uT_buf])
        glu(kind, bi, src=(H1, H1b))
        P.op("act", f_act(actT[A][:, :, 0:N], H1[:, :, 0:N], AF.Copy), reads=H1b, writes=[actT_buf[A]])
        if sample or last:
            conv_tail_out(kind, bi)
        queue_conv(kind, bi)

    def halo_a():
        sample, tiles, N, cur, xpar = geom("halo", 0)
        xr, xb = xres[xpar], xres_buf[xpar]
        norm_tiles([(xr[0:nr, i, :], xb[i], nr, r0) for i, (r0, nr) in enumerate(tiles)], "g_mix_pre", actT[B], actT_buf[B])
        P.op("dve", f_copy(Vext[:, 4 * cur:4 * cur + 4, :, 64:65].rearrange("p t h o -> p (t h o)"),
                           vflag[:, 0:1].to_broadcast([128, 32])), reads=[vflag_buf], writes=[V_buf[cur]])
        glu("halo", 0, src=(actT[B], actT_buf[B]))

    def halo_b():
        sample, tiles, N, cur, xpar = geom("halo", 0)
        hT, hTb = actT[B], actT_buf[B]
        slab, sbf = use_slab("w_in", 1)
        kc0 = cur * 512
        proj_fm(slab, sbf, 4, hT, hTb, N, lambda f, b: P.op(
            "act", f_act(kT[:, f, kc0:kc0 + N], ps[:, b, 0:N], AF.Copy), reads=[bank_buf[b]], writes=[kT_buf[cur]]))
        slab, sbf = use_slab("w_in", 2)
        for i, (r0, nr) in enumerate(tiles):
            b = bank()
            proj_tm(slab, sbf, hT, hTb, r0, nr, b)
            P.op("act", f_act(Vext[0:nr, 4 * cur + i, :, 0:64], ps[0:nr, b, :].rearrange("p (h d) -> p h d", h=8), AF.Copy),
                 reads=[bank_buf[b]], writes=[V_buf[cur]])

    def main_stage(kind, bi, next_pre=None, post_b_hook=None, mid_hook=None, next_begin=None, next_mid=None):
        sample, tiles, N, cur, xpar = geom(kind, bi)
        last = (kind == "prompt" and bi == NBLK - 1)
        prev = 1 - cur
        xr, xb = xres[xpar], xres_buf[xpar]
        drain(10 ** 6)
        mixT, mixTb = actT[B], actT_buf[B]
        slab, sbf = use_slab("w_in", 0)
        proj_fm(slab, sbf, 4, actT[A], actT_buf[A], N, lambda f, b: P.op(
            "act", f_act(qT[:, f, 0:N], ps[:, b, 0:N], AF.Copy, scale=0.125), reads=[bank_buf[b]], writes=[qT_buf]))
        lninfo = []
        for i, (r0, nr) in enumerate(tiles):
            b = bank()
            for c in range(4):
                P.op("pe", f_tr(ps[0:nr, b, c * 128:(c + 1) * 128], cacc[:, c, r0:r0 + nr], identf[:, :]),
                     reads=[cacc_buf[c], const_buf], writes=[bank_buf[b]], signal=(c == 3))
            cols, smb, _ = sm_cols()
            P.op("dve", lambda e, nr=nr, b=b, i=i: e.bn_stats(out=bnst[0:nr, i, 0:6], in_=ps[0:nr, b, :]),
                 reads=[bank_buf[b]], writes=[smb])
            P.op("dve", lambda e, nr=nr, cols=cols, i=i: e.bn_aggr(out=cols[0:nr, 4:6], in_=bnst[0:nr, i, 0:6]),
                 reads=[smb], writes=[smb])
            lninfo.append((b, cols, smb))
        for i, (r0, nr) in enumerate(tiles):
            b, cols, smb = lninfo[i]
            P.op("pool", f_ts1(cols[0:nr, 3:4], cols[0:nr, 5:6], EPS, ALU.add), reads=[smb], writes=[smb])
            P.op("pool", f_tt(cols[0:nr, 2:3], cols[0:nr, 3:4], cst[0:nr, 0:1], ALU.pow), reads=[smb, const_buf], writes=[smb])
        for i, (r0, nr) in enumerate(tiles):
            b, cols, smb = lninfo[i]
            P.op("dve", f_stt(cols[0:nr, 6:7], cols[0:nr, 4:5], -1.0, cols[0:nr, 2:3], ALU.mult, ALU.mult), reads=[smb], writes=[smb])
            P.op("act", f_act(zc[i][0:nr, :], ps[0:nr, b, :], AF.Identity, scale=cols[0:nr, 2:3], bias=cols[0:nr, 6:7]),
                 reads=[bank_buf[b], smb], writes=[zc_buf[i]])

        def c2b():
            cb = bank(2)
            st["reserved"].add(cb[0])
            st["reserved"].add(cb[1])
            for i, (r0, nr) in enumerate(tiles):
                for c in range(4):
                    pvc = ps[:, cb[c // 2], :].bitcast(BF16)[:, (c % 2) * 512:(c % 2) * 512 + 512]
                    P.op("pe", f_tr(pvc[:, r0:r0 + nr], zc[i][0:nr, c * 128:(c + 1) * 128], ident[0:nr, 0:nr]),
                         reads=[zc_buf[i], const_buf], writes=[bank_buf[cb[c // 2]]], signal=(c == 3))
            for c in range(4):
                pvc = ps[:, cb[c // 2], :].bitcast(BF16)[:, (c % 2) * 512:(c % 2) * 512 + 512]
                P.op("act", f_act(mixT[:, 4 + c, 0:N], pvc[:, 0:N], AF.Silu, scale=cvec[:, c, 32:33], bias=cvec[:, c, 33:34]),
                     reads=[bank_buf[cb[c // 2]], cvec_buf], writes=[mixTb])
            st["reserved"].discard(cb[0])
            st["reserved"].discard(cb[1])

        c2b()
        if sample:
            for s in range(2):
                for t in range(4):
                    sap, sbufs_, ssem = hslot()
                    P.dma("sp", ssem, sap, ck_d[s, t * 128:(t + 1) * 128, :], writes=sbufs_)
                    hi = rot("hbf", 2)
                    P.op("dve", f_copy(hbf[hi][:, 0:512], sap), reads=sbufs_, writes=[hbf_buf[hi]])
                    b = bank()
                    pv = bk_bf(b, 4, 128)
                    for f in range(4):
                        P.op("pe", f_tr(pv[:, f, :], hbf[hi][:, f * 128:(f + 1) * 128], ident[:, :]),
                             reads=[hbf_buf[hi], const_buf], writes=[bank_buf[b]], signal=(f == 3))
                    P.op("act", f_act(kT[:, :, prev * 512 + t * 128: prev * 512 + (t + 1) * 128], pv[:, :, :], AF.Copy),
                         reads=[bank_buf[b]], writes=[kT_buf[prev]])
                    sap, sbufs_, ssem = hslot()
                    P.dma("sp", ssem, sap, cv_d[s, t * 128:(t + 1) * 128, :], writes=sbufs_)
                    P.op("dve", f_copy(Vext[:, 4 * prev + t, :, 0:64], sap.rearrange("p (h d) -> p h d", h=8)),
                         reads=sbufs_, writes=[V_buf[prev]])
                kt_l = [(prev * 512 + t * 128, 128, 4 * prev + t, kT_buf[prev], V_buf[prev]) for t in range(4)]
                kt_l.append((cur * 512 + s * 16, 16, 4 * cur + s, kT_buf[cur], V_buf[cur]))
                attention_block([(s * 16, 16, kt_l)], mixT, mixTb)
        else:
            segs = []
            for j in range(4):
                kt_l = []
                for t in range(5):
                    gt = j + t
                    half = prev if gt < 4 else cur
                    kt_l.append((half * 512 + (gt % 4) * 128, 128, 4 * half + gt % 4, kT_buf[half], V_buf[half]))
                segs.append((j * 128, 128, kt_l))
            attention_block(segs, mixT, mixTb)

        nstate = next_begin() if next_begin is not None else None
        s0, s0b = use_slab("w_out", 0)
        s1, s1b = use_slab("w_out", 1)
        pn = PostNorm("g_mix_post")
        for i, (r0, nr) in enumerate(tiles):
            b0 = bank()
            proj_tm(s0, s0b, mixT, mixTb, r0, nr, b0)
            b1 = bank()
            proj_tm(s1, s1b, mixT, mixTb, r0, nr, b1)
            pn.add(b0, b1, nr, xr[0:nr, i, :], xb[i])
        pn.finish()
        if next_mid is not None:
            next_mid(nstate)

        if mid_hook is not None:
            mid_hook()
        norm_tiles([(xr[0:nr, i, :], xb[i], nr, r0) for i, (r0, nr) in enumerate(tiles)], "g_mem_pre", actT[A], actT_buf[A], wide=not sample)
        q2T, q2Tb = actT[B], actT_buf[B]
        for n in range(2):
            slab, sbf = use_slab("w_mq", n)
            proj_fm(slab, sbf, 4, actT[A], actT_buf[A], N, lambda f, b, n=n: P.op(
                "act", f_act(q2T[:, 4 * n + f, 0:N], ps[:, b, 0:N], AF.Copy, scale=1.0 / 16.0), reads=[bank_buf[b]], writes=[q2Tb]))
        oT, oTb = actT[A], actT_buf[A]
        if sample:
            for s in range(2):
                for m in range(2):
                    for src_d, is_k in ((cmk_d, True), (cmv_d, False)):
                        for hf in range(2):
                            sap, sbufs_, ssem = hslot()
                            P.dma("sp", ssem, sap, src_d[s, m * 128:(m + 1) * 128, hf * 512:(hf + 1) * 512], writes=sbufs_)
                            if is_k:
                                hi = rot("hbf", 2)
                                P.op("dve", f_copy(hbf[hi][:, 0:512], sap), reads=sbufs_, writes=[hbf_buf[hi]])
                                b = bank()
                                pv = bk_bf(b, 4, 128)
                                for f in range(4):
                                    P.op("pe", f_tr(pv[:, f, :], hbf[hi][:, f * 128:(f + 1) * 128], ident[:, :]),
                                         reads=[hbf_buf[hi], const_buf], writes=[bank_buf[b]], signal=(f == 3))
                                P.op("act", f_act(mkT[:, 4 * hf:4 * hf + 4, m * 128:(m + 1) * 128], pv[:, :, :], AF.Copy),
                                     reads=[bank_buf[b]], writes=[mkT_buf])
                            else:
                                P.op("dve", f_copy(mv[:, m, hf * 512:(hf + 1) * 512], sap),
                                     reads=sbufs_, writes=[mv_buf])
                mem_attention(q2T, q2Tb, s * 16, 16, oT, oTb)
        else:
            mem_attention(q2T, q2Tb, 0, N, oT, oTb)
        s0, s0b = use_slab("w_mo", 0)
        s1, s1b = use_slab("w_mo", 1)
        pn = PostNorm("g_mem_post")
        mob = {}
        for i in range(len(tiles)):
            for n in range(2):
                b = bank()
                mob[(i, n)] = b
                st["reserved"].add(b)
        for khalf in range(2):
            for i, (r0, nr) in enumerate(tiles):
                for n, (sl, slb) in enumerate(((s0, s0b), (s1, s1b))):
                    b = mob[(i, n)]
                    for k in range(4 * khalf, 4 * khalf + 4):
                        P.op("pe", f_mm(ps[0:nr, b, :], oT[:, k, r0:r0 + nr], sl[:, k, :], k == 0, k == 7),
                             reads=[slb, oT_kbuf[k]], touch=[oTb], writes=[bank_buf[b]], signal=(k % 4 == 3))
        for i, (r0, nr) in enumerate(tiles):
            pn.add(mob[(i, 0)], mob[(i, 1)], nr, xr[0:nr, i, :], xb[i])
            st["reserved"].discard(mob[(i, 0)])
            st["reserved"].discard(mob[(i, 1)])
        pn.finish()

        gsrcs = [(xr[0:nr, i, :], xb[i], nr, r0) for i, (r0, nr) in enumerate(tiles)]
        gstate = norm_begin(gsrcs, "g_ffn_pre", n_early=4, bufs=row_buffers()) if not sample else norm_begin(gsrcs, "g_ffn_pre", n_early=0)
        if next_pre is not None:
            next_pre()

        norm_finish(gstate, actT[B], actT_buf[B])
        h3, h3b = actT[B], actT_buf[B]
        for s in range(6):
            sg_, sgb = use_slab("w_gate", s)
            su_, sub = use_slab("w_up", s)
            nf = 4 if s < 5 else 2
            for ff in range(nf):
                f = s * 4 + ff
                bg = bank()
                for k in range(8):
                    P.op("pe", f_mm(ps[:, bg, 0:N], sg_[:, k, ff * 128:(ff + 1) * 128], h3[:, k, 0:N], k == 0, k == 7),
                         reads=[sgb, h3b], writes=[bank_buf[bg]], signal=(k == 7))
                bu = bank()
                for k in range(8):
                    P.op("pe", f_mm(ps[:, bu, 0:N], su_[:, k, ff * 128:(ff + 1) * 128], h3[:, k, 0:N], k == 0, k == 7),
                         reads=[sub, h3b], writes=[bank_buf[bu]], signal=(k == 7))
                gi = rot("sg", 2)
                P.op("act", f_act(sg[gi][:, 0:N], ps[:, bg, 0:N], AF.Silu), reads=[bank_buf[bg]], writes=[sg_buf[gi]])
                P.op("dve", f_tt(hid[:, f, 0:N], sg[gi][:, 0:N], ps[:, bu, 0:N], ALU.mult),
                     reads=[sg_buf[gi], bank_buf[bu]], writes=[hid_buf[f]])
                drain(4)
        dbanks = {}
        for n in range(2):
            for i in range(len(tiles)):
                b = bank()
                dbanks[(i, n)] = b
                st["reserved"].add(b)
        for n in range(2):
            for ks in range(3):
                slab, sbf = use_slab("w_down", n * 3 + ks)
                kt = (8, 8, 6)[ks]
                for i, (r0, nr) in enumerate(tiles):
                    b = dbanks[(i, n)]
                    for k in range(kt):
                        fidx = ks * 8 + k
                        last_mm = (ks == 2 and k == kt - 1)
                        P.op("pe", f_mm(ps[0:nr, b, :], hid[:, fidx, r0:r0 + nr], slab[:, k, :], ks == 0 and k == 0, last_mm),
                             reads=[sbf, hid_buf[fidx]], writes=[bank_buf[b]], signal=(k == kt - 1))
                drain(8)
        drain(10 ** 6)
        pn = PostNorm("g_ffn_post")
        for i, (r0, nr) in enumerate(tiles):
            if sample:
                dsty = ys_d[r0:r0 + nr, :]
            else:
                dsty = y_d[bi * NB + r0: bi * NB + r0 + nr, :]
            pn.add(dbanks[(i, 0)], dbanks[(i, 1)], nr, xr[0:nr, i, :], xb[i],
                   after=lambda i=i, nr=nr, dsty=dsty: P.dma("sp", f"x{xpar}{i}", dsty, xr[0:nr, i, :], reads=[xb[i]]))
            st["reserved"].discard(dbanks[(i, 0)])
            st["reserved"].discard(dbanks[(i, 1)])
        pn.finish()

    def mem_kv():
        mT, mTb = actT[1], actT_buf[1]
        srcs = []
        for m in range(2):
            si = rot("tmp", 2)
            P.dma("sp", f"tmpl{si}", tmp[si][:], memp_d[m * 128:(m + 1) * 128, :], writes=[tmp_buf[si]])
            srcs.append((tmp[si][:, :], tmp_buf[si], 128, m * 128))
        norm_tiles(srcs, "g_mem_kv", mT, mTb)
        for n in range(2):
            slab, sbf = use_slab("w_mk", n)
            proj_fm(slab, sbf, 4, mT, mTb, 256, lambda f, b, n=n: P.op(
                "act", f_act(mkT[:, 4 * n + f, :], ps[:, b, 0:256], AF.Copy), reads=[bank_buf[b]], writes=[mkT_buf]))
            for m in range(2):
                b = bank()
                proj_tm(slab, sbf, mT, mTb, m * 128, 128, b)
                store_rows(mko_d[m * 128:(m + 1) * 128, n * 512:(n + 1) * 512], b, 128, 512)
        for n in range(2):
            slab, sbf = use_slab("w_mv", n)
            for m in range(2):
                b = bank()
                proj_tm(slab, sbf, mT, mTb, m * 128, 128, b)
                P.op("dve", f_copy(mv[:, m, n * 512:(n + 1) * 512], ps[:, b, :]), reads=[bank_buf[b]], writes=[mv_buf])
                store_rows(mvo_d[m * 128:(m + 1) * 128, n * 512:(n + 1) * 512], b, 128, 512, extra=[mv_buf])

    def sample_prep():
        for s in range(2):
            si = rot("stage", 1)
            P.dma("sp", f"stg{si}", stage[si][0:30, 0:512], cc_d[s], writes=[stage_buf[si]])
            b = bank()
            for c in range(4):
                P.op("pe", f_tr(ps[:, b, c * 32:c * 32 + 30], stage[si][0:30, c * 128:(c + 1) * 128], identf[0:30, 0:30]),
                     reads=[stage_buf[si], const_buf], writes=[bank_buf[b]], signal=(c == 3))
            P.op("act", f_act(uTs[:, :, s, 0:30], ps[:, b, 0:128].rearrange("p (c j) -> p c j", c=4)[:, :, 0:30], AF.Copy, scale=2.0),
                 reads=[bank_buf[b]], writes=[uTs_buf])
            P.dma("sp", "copy", kso_d[s, 0:496, :], ck_d[s, 16:512, :])
            P.dma("sp", "copy", vso_d[s, 0:496, :], cv_d[s, 16:512, :])
            P.dma("sp", "copy", cso_d[s, 0:14, :], cc_d[s, 16:30, :])
        P.op("pool", f_memset(Vext[:, :, :, 64:65], 1.0), writes=[V_buf[0], V_buf[1]])
        emit_xload("sample", 0)

    import os
    kstop = int(os.environ.get("KSTOP", "99"))
    steps = []
    gate_dummy = Buf("gate_dummy")

    def gated_casts(names, gate_bufs):
        P.op("pool", f_memset(cst[:, 2:3], 0.0), reads=list(gate_bufs), writes=[gate_dummy])
        cast_group(names)
    def queue_late_casts():
        cast_pending.extend([("w_mq", 0), ("w_mq", 1), ("w_mo", 0), ("w_mo", 1)])
        for i in range(6):
            cast_pending.extend([("w_gate", i), ("w_up", i)])
        cast_pending.extend([("w_down", i) for i in range(6)])
    steps.append(halo_a)
    steps.append(lambda: pre1("prompt", 0, pre_begin("prompt", 0)))
    steps.append(lambda: (pre2("prompt", 0), drain(10 ** 6)))
    steps.append(lambda: (gated_casts(["w_out", "w_mk", "w_mv"], [uT_buf]), halo_b(),
                          build_btab(), queue_late_casts()))
    for bi in range(NBLK):
        if bi + 1 < NBLK:
            nbeg = (lambda b=bi: (emit_xload("prompt", b + 1), pre_begin("prompt", b + 1))[1])
            nmid = (lambda stt, b=bi: pre1("prompt", b + 1, stt))
            nxt = (lambda b=bi: pre2("prompt", b + 1))
        else:
            nbeg = (lambda: (sample_prep(), pre_begin("sample", 0))[1])
            nmid = (lambda stt: pre1("sample", 0, stt))
            nxt = (lambda: pre2("sample", 0))
        if bi == 0:
            steps.append(lambda nxt=nxt, nbeg=nbeg, nmid=nmid: main_stage("prompt", 0, nxt, mid_hook=mem_kv, next_begin=nbeg, next_mid=nmid))
        else:
            steps.append(lambda bi=bi, nxt=nxt, nbeg=nbeg, nmid=nmid: main_stage("prompt", bi, nxt, next_begin=nbeg, next_mid=nmid))
    steps.append(lambda: main_stage("sample", 0, None))
    for i, stp in enumerate(steps):
        if i >= kstop:
            break
        stp()
    if kstop >= len(steps):
        assert st["slab_pos"] == len(sched), (st["slab_pos"], len(sched))
        assert not pending

    sem_names = ["act", "dve", "pool", "pe"] + sorted(P.dval.keys())
    sems = {n: es.enter_context(nc.semaphore("s_" + n)) for n in sem_names}
    for n, v in P.dval.items():
        P.q["sp"].append(("wait", n, v))
    for e in ("act", "dve", "pool", "pe"):
        P.q["sp"].append(("wait", e, P.cnt[e]))

    def replay(engname, eobj):
        for it in P.q[engname]:
            if it[0] == "wait":
                eobj.wait_ge(sems[it[1]], it[2])
            elif it[0] == "op":
                ins = it[1](eobj)
                if it[2]:
                    ins.then_inc(sems[engname], 1)
            else:
                eobj.dma_start(out=it[1], in_=it[2]).then_inc(sems[it[3]], 16)

    with nc.Block() as block:
        @block.sync
        def _(e):
            replay("sp", e)

        @block.scalar
        def _(e):
            replay("act", e)

        @block.vector
        def _(e):
            replay("dve", e)

        @block.gpsimd
        def _(e):
            replay("pool", e)

        @block.tensor
        def _(e):
            replay("pe", e)
    es.close()
    nc._prog_stats = {e: len(P.q[e]) for e in P.ENG}
    return nc


_NC_CACHE = {}


def kernel(x_prompt, x_sample, cache_att_k, cache_att_v, cache_conv, cache_mem_k, cache_mem_v, mem_prompt,
           g_mix_pre, g_mix_post, w_in, rel_bias, conv_w, conv_b, cln_g, cln_b, w_out,
           g_mem_pre, g_mem_post, g_mem_kv, w_mq, w_mk, w_mv, w_mo, g_ffn_pre, g_ffn_post, w_gate, w_up, w_down):
    f = lambda a: np.ascontiguousarray(np.asarray(a, dtype=np.float32))
    x_prompt, x_sample = f(x_prompt), f(x_sample)
    if "nc" not in _NC_CACHE:
        _NC_CACHE["nc"] = build_nc()
    nc = _NC_CACHE["nc"]
    cvec = np.concatenate([f(conv_w)[0], f(conv_b)[0][None], f(cln_g)[0][None], f(cln_b)[0][None]], axis=0)
    shared = {
        "g_mix_pre": f(g_mix_pre), "g_mix_post": f(g_mix_post), "g_mem_pre": f(g_mem_pre), "g_mem_post": f(g_mem_post),
        "g_mem_kv": f(g_mem_kv), "g_ffn_pre": f(g_ffn_pre), "g_ffn_post": f(g_ffn_post),
        "rel_bias": f(rel_bias)[0], "cvec": f(cvec),
        "w_in": f(w_in)[0], "w_out": f(w_out)[0], "w_mq": f(w_mq)[0], "w_mk": f(w_mk)[0], "w_mv": f(w_mv)[0],
        "w_mo": f(w_mo)[0], "w_gate": f(w_gate)[0], "w_up": f(w_up)[0], "w_down": f(w_down)[0],
    }
    ck, cv = f(cache_att_k)[0], f(cache_att_v)[0]
    cc, cmk, cmv = f(cache_conv)[0], f(cache_mem_k)[0], f(cache_mem_v)[0]
    memp = f(mem_prompt)
    in_maps = []
    for c in range(8):
        b, qt = c // 4, c % 4
        m = dict(shared)
        m["xp"] = x_prompt[b, qt * 4096:(qt + 1) * 4096]
        m["xh"] = x_prompt[b, qt * 4096 - 512:qt * 4096] if qt > 0 else np.zeros((512, D), np.float32)
        m["vflag"] = np.full((128, 1), 1.0 if qt > 0 else 0.0, np.float32)
        m["xs"] = x_sample[2 * c:2 * c + 2].reshape(32, D)
        m["ck"] = ck[2 * c:2 * c + 2].reshape(2, 512, 512)
        m["cv"] = cv[2 * c:2 * c + 2].reshape(2, 512, 512)
        m["cc"] = cc[2 * c:2 * c + 2]
        m["cmk"] = cmk[2 * c:2 * c + 2].reshape(2, 256, D)
        m["cmv"] = cmv[2 * c:2 * c + 2].reshape(2, 256, D)
        m["memp"] = memp[b]
        in_maps.append({k: np.ascontiguousarray(v) for k, v in m.items()})
    res = run_bass_kernel_spmd(nc, in_maps, core_ids=list(range(8)))
    R = res.results
    y_prompt = np.stack([np.concatenate([R[4 * b + q]["y"] for q in range(4)], axis=0) for b in range(2)], axis=0)
    y_sample = np.concatenate([R[c]["ys"].reshape(2, 16, D) for c in range(8)], axis=0)
    nk = np.stack([R[4 * b + 3]["ktail"].reshape(512, 8, 64) for b in range(2)], axis=0)[None]
    nv = np.stack([R[4 * b + 3]["vtail"].reshape(512, 8, 64) for b in range(2)], axis=0)[None]
    ncv = np.stack([R[4 * b + 3]["ctail"] for b in range(2)], axis=0)[None]
    nmk = np.stack([R[4 * b]["mko"].reshape(256, 4, 256) for b in range(2)], axis=0)[None]
    nmv = np.stack([R[4 * b]["mvo"].reshape(256, 4, 256) for b in range(2)], axis=0)[None]
    ks = np.concatenate([R[c]["kso"].reshape(2, 512, 8, 64) for c in range(8)], axis=0)[None]
    vs = np.concatenate([R[c]["vso"].reshape(2, 512, 8, 64) for c in range(8)], axis=0)[None]
    cs = np.concatenate([R[c]["cso"] for c in range(8)], axis=0)[None]
    out = (y_prompt, y_sample, nk, nv, ncv, nmk, nmv, ks, vs, cs)
    return tuple(np.ascontiguousarray(o, dtype=np.float32) for o in out)
```

```python
import contextlib
import numpy as np
import concourse.bass as bass
import concourse.mybir as mybir
from concourse.bass_utils import run_bass_kernel_spmd

F32 = mybir.dt.float32
BF16 = mybir.dt.bfloat16
AF = mybir.ActivationFunctionType
ALU = mybir.AluOpType

D = 1024
NBLK = 8
NB = 512
D_FF = 2816
EPS = 1e-6
NEG = -1e30
DEBUG = False


class Buf:
    __slots__ = ("name", "w", "r")

    def __init__(self, name):
        self.name = name
        self.w = None
        self.r = {}


class Prog:
    ENG = ("sp", "act", "dve", "pool", "pe")
    SAME_WAIT = {"pool", "act", "dve"}

    def __init__(self):
        self.q = {e: [] for e in self.ENG}
        self.cnt = {e: 0 for e in self.ENG}
        self.seen = {e: {} for e in self.ENG}
        self.dval = {}

    def _waits(self, eng, reads, writes):
        need = {}
        for b in reads:
            if b.w is not None:
                need[b.w[0]] = max(need.get(b.w[0], 0), b.w[1])
        for b in writes:
            if b.w is not None:
                need[b.w[0]] = max(need.get(b.w[0], 0), b.w[1])
            for k, v in b.r.items():
                need[k] = max(need.get(k, 0), v)
        for k, v in need.items():
            if k == eng and eng not in self.SAME_WAIT:
                continue
            if self.seen[eng].get(k, 0) >= v:
                continue
            self.seen[eng][k] = v
            self.q[eng].append(("wait", k, v))

    def op(self, eng, fn, reads=(), writes=(), signal=True, touch=()):
        self._waits(eng, reads, writes)
        if signal:
            self.cnt[eng] += 1
            tok = (eng, self.cnt[eng])
        else:
            tok = (eng, self.cnt[eng] + 1)
        self.q[eng].append(("op", fn, signal))
        for b in list(reads) + list(touch):
            b.r[eng] = max(b.r.get(eng, 0), tok[1])
        for b in writes:
            b.w = tok
            b.r = {}
        return tok

    def dma(self, qeng, sem, out, in_, reads=(), writes=()):
        self._waits(qeng, reads, writes)
        self.dval[sem] = self.dval.get(sem, 0) + 16
        tok = (sem, self.dval[sem])
        self.q[qeng].append(("dma", out, in_, sem))
        for b in reads:
            b.r[sem] = max(b.r.get(sem, 0), tok[1])
        for b in writes:
            b.w = tok
            b.r = {}
        return tok


def f_mm(out, lhsT, rhs, start, stop):
    return lambda e: e.matmul(out, lhsT=lhsT, rhs=rhs, start=start, stop=stop)


def f_tr(out, in_, ident):
    return lambda e: e.transpose(out=out, in_=in_, identity=ident)


def f_act(out, in_, func, scale=None, bias=None, accum=None):
    def f(e):
        kw = {}
        if scale is not None:
            kw["scale"] = scale
        if bias is not None:
            kw["bias"] = bias
        if accum is not None:
            kw["accum_out"] = accum
        return e.activation(out=out, in_=in_, func=func, **kw)
    return f


def f_tt(out, a, b, op):
    return lambda e: e.tensor_tensor(out=out, in0=a, in1=b, op=op)


def f_stt(out, a, s, b, op0, op1):
    return lambda e: e.scalar_tensor_tensor(out=out, in0=a, scalar=s, in1=b, op0=op0, op1=op1)


def f_ts(out, a, s1, s2, op0, op1):
    return lambda e: e.tensor_scalar(out=out, in0=a, scalar1=s1, scalar2=s2, op0=op0, op1=op1)


def f_ts1(out, a, s1, op0):
    return lambda e: e.tensor_scalar(out=out, in0=a, scalar1=s1, scalar2=None, op0=op0)


def f_copy(out, a):
    return lambda e: e.tensor_copy(out=out, in_=a)


def f_memset(ap, v):
    return lambda e: e.memset(ap, v)


def f_recip(out, a):
    return lambda e: e.reciprocal(out=out, in_=a)


def build_nc():
    nc = bass.Bass("TRN2", target_bir_lowering=False)
    P = Prog()
    es = contextlib.ExitStack()

    def din(name, shape):
        return nc.dram_tensor(name, list(shape), F32, kind="ExternalInput")

    def dout(name, shape):
        return nc.dram_tensor(name, list(shape), F32, kind="ExternalOutput")

    xp = din("xp", [4096, D]).ap()
    xh = din("xh", [512, D]).ap()
    vflag_d = din("vflag", [128, 1]).ap()
    xs_d = din("xs", [32, D]).ap()
    ck_d = din("ck", [2, 512, 512]).ap()
    cv_d = din("cv", [2, 512, 512]).ap()
    cc_d = din("cc", [2, 30, 512]).ap()
    cmk_d = din("cmk", [2, 256, D]).ap()
    cmv_d = din("cmv", [2, 256, D]).ap()
    memp_d = din("memp", [256, D]).ap()
    gv = {n: din(n, [1, D]).ap() for n in
          ("g_mix_pre", "g_mix_post", "g_mem_pre", "g_mem_post", "g_mem_kv", "g_ffn_pre", "g_ffn_post")}
    rb_d = din("rel_bias", [8, 257]).ap()
    cvec_d = din("cvec", [34, 512]).ap()
    W = {"w_in": din("w_in", [D, 2560]), "w_out": din("w_out", [D, D]), "w_mq": din("w_mq", [D, D]),
         "w_mk": din("w_mk", [D, D]), "w_mv": din("w_mv", [D, D]), "w_mo": din("w_mo", [D, D]),
         "w_gate": din("w_gate", [D, D_FF]), "w_up": din("w_up", [D, D_FF]), "w_down": din("w_down", [D_FF, D])}

    y_d = dout("y", [4096, D]).ap()
    ys_d = dout("ys", [32, D]).ap()
    ktail_d = dout("ktail", [512, 512]).ap()
    vtail_d = dout("vtail", [512, 512]).ap()
    ctail_d = dout("ctail", [30, 512]).ap()
    mko_d = dout("mko", [256, D]).ap()
    mvo_d = dout("mvo", [256, D]).ap()
    kso_d = dout("kso", [2, 512, 512]).ap()
    vso_d = dout("vso", [2, 512, 512]).ap()
    cso_d = dout("cso", [2, 30, 512]).ap()

    nsl = {"w_in": 5, "w_out": 2, "w_mq": 2, "w_mk": 2, "w_mv": 2, "w_mo": 2, "w_gate": 6, "w_up": 6, "w_down": 6}
    wsc = {n: nc.dram_tensor("ws_" + n, [k, 128, 8, 512], BF16) for n, k in nsl.items()}
    wsc_buf = {n: Buf("ws_" + n) for n in nsl}
    per = nc.dram_tensor("per", [8, 129, 768], F32)
    per_buf = Buf("per")

    def slab_geom(name, idx):
        if name == "w_down":
            n, ks = idx // 3, idx % 3
            return ks * 1024, (8, 8, 6)[ks], n * 512, 512
        if name in ("w_gate", "w_up"):
            return 0, 8, idx * 512, min(512, D_FF - idx * 512)
        return 0, 8, idx * 512, 512

    def sb(name, shape, dtype):
        return es.enter_context(nc.sbuf_tensor("t_" + name, list(shape), dtype))

    NSLOT = 4
    wring = [sb(f"wring{i}", [128, 8, 512], BF16) for i in range(NSLOT)]
    wring_buf = [Buf(f"wring{i}") for i in range(NSLOT)]
    xres = [sb(f"xres{i}", [128, 4, D], F32) for i in range(2)]
    xres_buf = [[Buf(f"xres{p}_{i}") for i in range(4)] for p in range(2)]
    actT = [sb(f"actT{i}", [128, 8, 512], BF16) for i in range(2)]
    actT_buf = [Buf(f"actT{i}") for i in range(2)]
    oT_kbuf = [Buf(f"oTk{k}") for k in range(8)]
    qT = sb("qT", [128, 4, 512], BF16)
    qT_buf = Buf("qT")
    kT = sb("kT", [128, 4, 1024], BF16)
    kT_buf = [Buf("kT0"), Buf("kT1")]
    Vext = sb("Vext", [128, 8, 8, 65], BF16)
    V_buf = [Buf("V0"), Buf("V1")]
    uT = sb("uT", [128, 4, 30 + 512], F32)
    uT_buf = Buf("uT")
    uTs = sb("uTs", [128, 4, 2, 46], F32)
    uTs_buf = Buf("uTs")
    cacc = sb("cacc", [128, 4, 512], F32)
    cacc_buf = [Buf(f"cacc{c}") for c in range(4)]
    sg = [sb(f"sg{i}", [128, 512], F32) for i in range(2)]
    sg_buf = [Buf(f"sg{i}") for i in range(2)]
    att = sb("att", [128, 512], BF16)
    att_buf = Buf("att")
    pT = [sb(f"pT{i}", [128, 5, 128], BF16) for i in range(3)]
    pT_buf = [Buf(f"pT{i}") for i in range(3)]
    Btab = sb("Btab", [128, 3, 8, 128], F32)
    Btab_buf = Buf("Btab")
    hid = sb("hid", [128, 22, 512], BF16)
    hid_buf = [Buf(f"hid{f}") for f in range(22)]
    pm = [sb(f"pm{i}", [128, 2, 512], BF16) for i in range(2)]
    pm_buf = [Buf(f"pm{i}") for i in range(2)]
    rs = [sb(f"rs{i}", [128, 512], F32) for i in range(3)]
    rs_buf = [Buf(f"rs{i}") for i in range(3)]
    tb = [rs[i][:, 0:384].rearrange("p (s n) -> p s n", s=3) for i in range(3)]
    tb_buf = rs_buf
    mkT = sb("mkT", [128, 8, 256], BF16)
    mkT_buf = Buf("mkT")
    mv = sb("mv", [128, 2, D], BF16)
    mv_buf = Buf("mv")
    gbuf = [sb(f"gbuf{i}", [128, D], F32) for i in range(2)]
    gbuf_buf = [Buf(f"gbuf{i}") for i in range(2)]
    hbf = [sb(f"hbf{i}", [128, D], BF16) for i in range(2)]
    hbf_buf = [Buf(f"hbf{i}") for i in range(2)]
    zc = [hbf[i // 2][:, (i % 2) * 512:(i % 2 + 1) * 512] for i in range(4)]
    zc_buf = [hbf_buf[i // 2] for i in range(4)]
    tmp = [sb(f"tmp{i}", [128, D], F32) for i in range(2)]
    tmp_buf = [Buf(f"tmp{i}") for i in range(2)]
    stage = [sb(f"stage{i}", [128, 512], F32) for i in range(1)]
    stage_buf = [Buf(f"stage{i}") for i in range(1)]
    ident = sb("ident", [128, 128], BF16)
    identf = sb("identf", [128, 128], F32)
    ones = sb("ones", [128, 128], BF16)
    const_buf = Buf("const")
    cvec_s = tmp[1]
    cvec = sb("cvecT", [128, 4, 34], F32)
    cvec_buf = Buf("cvec")
    cvs_buf = tmp_buf[1]
    frow = tmp[0]
    frow_buf = tmp_buf[0]
    vflag = sb("vflag_s", [128, 1], F32)
    vflag_buf = Buf("vflag")
    sm = sb("sm", [128, 64], F32)
    sm_buf = [Buf(f"sm{i}") for i in range(8)]
    cst = sb("cst", [128, 4], F32)
    bnst = sb("bnst", [128, 4, 8], F32)
    ps = es.enter_context(nc.psum_tensor("ps", [128, 8, 512], F32))
    bank_buf = [Buf(f"bank{i}") for i in range(8)]

    st = {"rr": 0, "reserved": set(), "slab_pos": 0, "sm": 0, "g": 0, "hbf": 0, "tmp": 0, "stage": 0, "sg": 0,
          "tb": 0, "pm": 0, "rs": 0, "hslot": 0}

    def bank(n=1):
        for _ in range(16):
            p = st["rr"]
            if n == 2 and p % 2 == 1:
                p = (p + 1) % 8
            cand = [(p + i) % 8 for i in range(n)]
            st["rr"] = (p + n) % 8
            if not any(c in st["reserved"] for c in cand):
                return cand if n > 1 else cand[0]
        raise RuntimeError(f"no psum bank n={n} reserved={sorted(st['reserved'])} rr={st['rr']}")

    def rot(key, n):
        v = st[key]
        st[key] = (v + 1) % n
        return v

    def bk(b):
        return ps[:, b, :]

    def bk_bf(b, k, n):
        return ps[:, b, :].bitcast(BF16).rearrange("p (k n) -> p k n", k=k)[:, :, 0:n]

    def blk_seq(with_next):
        seq = [("w_in", 0), ("w_in", 1), ("w_in", 2), ("w_out", 0), ("w_out", 1), ("w_mq", 0), ("w_mq", 1),
               ("w_mo", 0), ("w_mo", 1)]
        if with_next:
            seq += [("w_in", 3), ("w_in", 4)]
        seq += [x for s in range(6) for x in (("w_gate", s), ("w_up", s))]
        seq += [("w_down", i) for i in range(6)]
        return seq
    sched = [("w_in", 3), ("w_in", 4), ("w_in", 3), ("w_in", 4), ("w_in", 1), ("w_in", 2)]
    for _b in range(NBLK + 1):
        seq = blk_seq(_b < NBLK)
        if _b == 0:
            i0 = seq.index(("w_mq", 0))
            seq = seq[:i0] + [("w_mk", 0), ("w_mk", 1), ("w_mv", 0), ("w_mv", 1)] + seq[i0:]
        sched += seq
    DEPTH = 3
    loaded = {"n": 0}

    def issue_loads(upto):
        while loaded["n"] < min(upto, len(sched)):
            p = loaded["n"]
            name, idx = sched[p]
            r0, kt, c0, ncols = slab_geom(name, idx)
            slot = p % NSLOT
            ensure_cast(name, idx)
            P.dma("sp", f"wr{slot}", wring[slot][:, 0:kt, 0:ncols], wsc[name].ap()[idx][:, 0:kt, 0:ncols],
                  reads=[slab_cast_buf[(name, idx)]], writes=[wring_buf[slot]])
            loaded["n"] += 1

    def use_slab(name, idx):
        p = st["slab_pos"]
        assert sched[p] == (name, idx), (p, sched[p], name, idx)
        issue_loads(p + DEPTH)
        emit_casts(2)
        st["slab_pos"] = p + 1
        slot = p % NSLOT
        return wring[slot], wring_buf[slot]

    slab_cast_buf = {}
    cast_pending = []

    def cast_one(name, idx):
        r0, kt, c0, ncols = slab_geom(name, idx)
        src = W[name].ap()[r0:r0 + kt * 128, c0:c0 + ncols].rearrange("(ko p) c -> p ko c", p=128)
        bf = Buf(f"ws_{name}_{idx}")
        P.dma("pool", f"cast_{name}_{idx}", wsc[name].ap()[idx][:, 0:kt, 0:ncols], src, writes=[bf])
        slab_cast_buf[(name, idx)] = bf

    def emit_casts(n):
        k = 0
        while cast_pending and k < n:
            cast_one(*cast_pending.pop(0))
            k += 1

    def ensure_cast(name, idx):
        while (name, idx) not in slab_cast_buf:
            assert cast_pending, ("no cast scheduled for", name, idx)
            cast_one(*cast_pending.pop(0))

    def cast_group(names):
        for name in names:
            order = [3, 4, 1, 2, 0] if name == "w_in" else list(range(nsl[name]))
            for idx in order:
                r0, kt, c0, ncols = slab_geom(name, idx)
                src = W[name].ap()[r0:r0 + kt * 128, c0:c0 + ncols].rearrange("(ko p) c -> p ko c", p=128)
                bf = Buf(f"ws_{name}_{idx}")
                P.dma("pool", f"cast_{name}_{idx}", wsc[name].ap()[idx][:, 0:kt, 0:ncols], src, writes=[bf])
                slab_cast_buf[(name, idx)] = bf

    P.op("pool", f_memset(ident[:], 0.0), writes=[const_buf])
    P.op("pool", lambda e: e.affine_select(out=ident[:], in_=ident[:], pattern=[[-1, 128]], compare_op=ALU.not_equal,
                                           fill=1.0, base=0, channel_multiplier=1), writes=[const_buf])
    P.op("pool", f_memset(identf[:], 0.0), writes=[const_buf])
    P.op("pool", lambda e: e.affine_select(out=identf[:], in_=identf[:], pattern=[[-1, 128]], compare_op=ALU.not_equal,
                                           fill=1.0, base=0, channel_multiplier=1), writes=[const_buf])
    P.op("pool", f_memset(ones[:], 1.0), writes=[const_buf])
    P.op("pool", f_memset(cst[:, 0:1], -0.5), writes=[const_buf])
    P.op("pool", f_memset(cst[:, 1:2], EPS), writes=[const_buf])
    P.op("pool", f_memset(Vext[:], 1.0), writes=[V_buf[0], V_buf[1]])
    P.op("pool", f_memset(uT[:], 0.0), writes=[uT_buf])

    cast_group(["w_in"])
    for i in range(4):
        P.dma("sp", f"x1{i}", xres[1][:, i, :], xh[i * 128:(i + 1) * 128, :], writes=[xres_buf[1][i]])
    for i in range(4):
        P.dma("sp", f"x0{i}", xres[0][:, i, :], xp[i * 128:(i + 1) * 128, :], writes=[xres_buf[0][i]])
    P.dma("sp", "setup", vflag[:], vflag_d, writes=[vflag_buf])
    P.dma("sp", "setup", cvec_s[0:34, 0:512], cvec_d, writes=[cvs_buf])
    P.dma("sp", "setup", frow[0:8, 0:256], rb_d[:, 1:257], writes=[frow_buf])
    setup_tok = ("setup", P.dval["setup"])
    vflag_buf.w = cvs_buf.w = frow_buf.w = setup_tok
    P.op("dve", f_copy(frow[0:8, 256:768], frow[0:8, 255:256].to_broadcast([8, 512])), reads=[frow_buf], writes=[frow_buf])
    for h in range(8):
        P.dma("pool", "per", per.ap()[h:h + 1], frow[h:h + 1, 0:768].unsqueeze(1).to_broadcast([1, 129, 768]),
              reads=[frow_buf], writes=[])
    per_buf.w = ("per", P.dval["per"])

    b = bank()
    for c in range(4):
        P.op("pe", f_tr(ps[:, b, c * 34:(c + 1) * 34], cvec_s[0:34, c * 128:(c + 1) * 128], identf[0:34, 0:34]),
             reads=[cvs_buf, const_buf], writes=[bank_buf[b]], signal=(c == 3))
    P.op("dve", f_copy(cvec[:].rearrange("p c j -> p (c j)"), ps[:, b, 0:136]), reads=[bank_buf[b]], writes=[cvec_buf])
    P.op("dve", f_ts1(cvec[:, :, 0:31], cvec[:, :, 0:31], 0.5, ALU.mult), reads=[cvec_buf], writes=[cvec_buf])

    def build_btab():
        for si, t in enumerate((0, 3, 4)):
            src = bass.AP(per, 639 - 128 * t, [[767, 128], [129 * 768, 8], [1, 128]])
            P.dma("sp", "btab", Btab[:, si], src, reads=[per_buf], writes=[])
        Btab_buf.w = ("btab", P.dval["btab"])
        P.op("pool", f_memset(Btab[0:64, 0, :, 64:128], NEG), reads=[], writes=[Btab_buf])
        P.op("pool", f_memset(Btab[64:128, 2, :, 0:64], NEG), reads=[], writes=[Btab_buf])

    def sm_cols(n=8):
        g = rot("sm", 8)
        return sm[:, g * 8:g * 8 + n], sm_buf[g], g

    def load_g(name):
        i = rot("g", 2)
        P.dma("sp", f"g{i}", gbuf[i][:], gv[name].to_broadcast([128, D]), writes=[gbuf_buf[i]])
        return gbuf[i], gbuf_buf[i]

    def rstd_from_ss(col_ap, nr, smb, n_terms):
        if n_terms == 2:
            P.op("pool", f_tt(col_ap[0:nr, 0:1], col_ap[0:nr, 0:1], col_ap[0:nr, 1:2], ALU.add), reads=[smb], writes=[smb])
        P.op("pool", f_ts(col_ap[0:nr, 3:4], col_ap[0:nr, 0:1], 1.0 / D, EPS, ALU.mult, ALU.add), reads=[smb], writes=[smb])
        P.op("pool", f_tt(col_ap[0:nr, 2:3], col_ap[0:nr, 3:4], cst[0:nr, 0:1], ALU.pow), reads=[smb, const_buf], writes=[smb])

    def norm_begin(srcs, g_name, n_early=2, bufs=None):
        g_t, g_b = load_g(g_name)
        info = []
        for (src_ap, src_buf, nr, col0) in srcs:
            cols, smb, _ = sm_cols()
            if bufs is None:
                hi = rot("hbf", 2)
                hb_ap, hb_buf = hbf[hi], hbf_buf[hi]
            else:
                hb_ap, hb_buf = bufs[len(info) % len(bufs)]
            P.op("act", f_act(hb_ap[0:nr, :], src_ap, AF.Square, accum=cols[0:nr, 0:1]), reads=[src_buf], writes=[smb, hb_buf])
            info.append([cols, smb, (hb_ap, hb_buf), False])
        for (src_ap, src_buf, nr, col0), (cols, smb, hi, _) in zip(srcs, info):
            rstd_from_ss(cols, nr, smb, 1)
        stt = {"srcs": srcs, "info": info, "g": (g_t, g_b)}
        for i in range(min(n_early, len(srcs))):
            norm_h(stt, i)
        return stt

    def norm_h(stt, i):
        (src_ap, src_buf, nr, col0) = stt["srcs"][i]
        cols, smb, hi, done = stt["info"][i]
        if done:
            return
        g_t, g_b = stt["g"]
        hb_ap, hb_buf = hi
        P.op("dve", f_stt(hb_ap[0:nr, :], src_ap, cols[0:nr, 2:3], g_t[0:nr, :], ALU.mult, ALU.mult),
             reads=[src_buf, smb, g_b], writes=[hb_buf])
        stt["info"][i][3] = True

    def norm_finish(stt, dstT, dstT_buf):
        for i, (src_ap, src_buf, nr, col0) in enumerate(stt["srcs"]):
            norm_h(stt, i)
            cols, smb, (hb_ap, hb_buf), _ = stt["info"][i]
            b = bank()
            pv = bk_bf(b, 8, 128)
            for k in range(8):
                P.op("pe", f_tr(pv[:, k, 0:nr], hb_ap[0:nr, k * 128:(k + 1) * 128], ident[0:nr, 0:nr]),
                     reads=[hb_buf, const_buf], writes=[bank_buf[b]], signal=(k == 7))
            P.op("act", f_act(dstT[:, :, col0:col0 + nr], pv[:, :, 0:nr], AF.Copy), reads=[bank_buf[b]], writes=[dstT_buf])

    def row_buffers():
        return [(hbf[0], hbf_buf[0]), (hbf[1], hbf_buf[1]),
                (pm[0][:].rearrange("p a b -> p (a b)"), pm_buf[0]), (pm[1][:].rearrange("p a b -> p (a b)"), pm_buf[1])]

    def norm_tiles(srcs, g_name, dstT, dstT_buf, wide=False):
        if wide:
            norm_finish(norm_begin(srcs, g_name, n_early=4, bufs=row_buffers()), dstT, dstT_buf)
        else:
            norm_finish(norm_begin(srcs, g_name, n_early=0), dstT, dstT_buf)

    def proj_fm(slab, slab_b, ncol_tiles, src, src_buf, N, evac):
        for f in range(ncol_tiles):
            b = bank()
            for k in range(8):
                P.op("pe", f_mm(ps[:, b, 0:N], slab[:, k, f * 128:(f + 1) * 128], src[:, k, 0:N], k == 0, k == 7),
                     reads=[slab_b, src_buf], writes=[bank_buf[b]], signal=(k == 7))
            evac(f, b)

    def proj_tm(slab, slab_b, src, src_buf, c0, nr, b, kt=8, k0=0, start=True, stop=True, ncols=512):
        for k in range(kt):
            P.op("pe", f_mm(ps[0:nr, b, 0:ncols], src[:, k0 + k, c0:c0 + nr], slab[:, k, 0:ncols],
                            start and k == 0, stop and k == kt - 1),
                 reads=[slab_b, src_buf], writes=[bank_buf[b]], signal=(stop and k == kt - 1))

    def store_rows(dst_ap, src_bank, nr, ncols, scale=None, col0=0, extra=()):
        si = rot("stage", 1)
        if scale is None:
            P.op("act", f_act(stage[si][0:nr, 0:ncols], ps[0:nr, src_bank, col0:col0 + ncols], AF.Copy),
                 reads=[bank_buf[src_bank]] + list(extra), writes=[stage_buf[si]])
        else:
            P.op("act", f_act(stage[si][0:nr, 0:ncols], ps[0:nr, src_bank, col0:col0 + ncols], AF.Copy, scale=scale),
                 reads=[bank_buf[src_bank]] + list(extra), writes=[stage_buf[si]])
        P.dma("sp", f"stg{si}", dst_ap, stage[si][0:nr, 0:ncols], reads=[stage_buf[si]])

    class PostNorm:
        def __init__(self, g_name):
            self.g_t, self.g_b = load_g(g_name)
            self.pend = None

        def add(self, b0, b1, nr, x_ap, x_buf, after=None):
            g_t, g_b = self.g_t, self.g_b
            cols, smb, _ = sm_cols()
            ti = rot("tmp", 2)
            for n, bb in enumerate((b0, b1)):
                P.op("act", f_act(tmp[ti][0:nr, n * 512:(n + 1) * 512], ps[0:nr, bb, :], AF.Square, accum=cols[0:nr, n:n + 1]),
                     reads=[bank_buf[bb]], writes=[smb, tmp_buf[ti]])
            for n, bb in enumerate((b0, b1)):
                P.op("dve", f_tt(tmp[ti][0:nr, n * 512:(n + 1) * 512], ps[0:nr, bb, :], g_t[0:nr, n * 512:(n + 1) * 512], ALU.mult),
                     reads=[bank_buf[bb], smb, g_b], writes=[tmp_buf[ti]])
            rstd_from_ss(cols, nr, smb, 2)

            def xupd():
                P.op("dve", f_stt(x_ap, tmp[ti][0:nr, :], cols[0:nr, 2:3], x_ap, ALU.mult, ALU.add),
                     reads=[tmp_buf[ti], smb, x_buf], writes=[x_buf])
                if after is not None:
                    after()
            if self.pend is not None:
                self.pend()
            self.pend = xupd

        def finish(self):
            if self.pend is not None:
                self.pend()
            self.pend = None

    pending = []

    def drain(n):
        k = 0
        while pending and k < n:
            pending.pop(0)()
            k += 1

    def hslot():
        j = rot("hslot", 11)
        ap = hid[:, 2 * j:2 * j + 2, :].rearrange("p a b -> p (a b)").bitcast(F32)
        return ap, [hid_buf[2 * j], hid_buf[2 * j + 1]], f"hst{j}"

    SLOT = {1: 0, 2: 1, 0: 2, 3: 3, 4: 4}

    def attention_block(segs, dstT, dstT_buf):
        obs = {}

        def s_stage(si, h):
            q0, nq, ktiles = segs[si]
            base = 64 * (h % 2)
            f = h // 2
            sbk = bank(2)
            assert sbk[1] == sbk[0] + 1
            spv = ps[:, sbk[0]:sbk[0] + 2, :].rearrange("p b n -> p (b n)")
            sbufs = [bank_buf[sbk[0]], bank_buf[sbk[1]]]
            for t in range(5):
                kcol, nk, vt, kb, vb = ktiles[t]
                s = SLOT[t]
                P.op("pe", f_mm(spv[0:nk, s * 128:s * 128 + nq], kT[base:base + 64, f, kcol:kcol + nk],
                                qT[base:base + 64, f, q0:q0 + nq], True, True),
                     reads=[kb, qT_buf], writes=sbufs, signal=(t == 4))
            ti = rot("tb", 3)
            s3 = spv[:, 256:640].rearrange("p (s n) -> p s n", s=3)[:, :, 0:nq]
            P.op("dve", f_tt(tb[ti][:, :, 0:nq], s3, Btab[:, :, h, 0:nq], ALU.add),
                 reads=sbufs + [Btab_buf], writes=[tb_buf[ti]])
            s2 = spv[:, 0:256].rearrange("p (s n) -> p s n", s=2)[:, :, 0:nq]
            P.op("act", f_act(pT[ti][:, 0:2, 0:nq], s2, AF.Exp, bias=Btab[:, 0, h, 0:1]),
                 reads=sbufs + [Btab_buf, tb_buf[ti]], writes=[pT_buf[ti]])
            P.op("act", f_act(pT[ti][:, 2:5, 0:nq], tb[ti][:, :, 0:nq], AF.Exp), reads=[tb_buf[ti]], writes=[pT_buf[ti]])
            return ti

        def pv_stage(si, h, ti):
            q0, nq, ktiles = segs[si]
            if h == 0:
                ob = bank(2)
                st["reserved"].add(ob[0])
                st["reserved"].add(ob[1])
                obs[si] = ob
            ob = obs[si]
            o = ob[h // 4]
            hc = (h % 4) * 65
            for t in range(5):
                kcol, nk, vt, kb, vb = ktiles[t]
                P.op("pe", f_mm(ps[0:nq, o, hc:hc + 65], pT[ti][0:nk, SLOT[t], 0:nq], Vext[0:nk, vt, h, :], t == 0, t == 4),
                     reads=[pT_buf[ti], vb], writes=[bank_buf[o]], signal=(t == 4))
            if h == 7:
                cols, smb, _ = sm_cols()
                for i2 in range(2):
                    ov = ps[0:nq, ob[i2], 0:260].rearrange("p (h d) -> p h d", h=4)
                    P.op("dve", f_recip(cols[0:nq, i2 * 4:i2 * 4 + 4], ov[:, :, 64]), reads=[bank_buf[ob[i2]]], writes=[smb])
                    P.op("dve", f_tt(att[0:nq, i2 * 256:(i2 + 1) * 256].rearrange("p (h d) -> p h d", h=4), ov[:, :, 0:64],
                                     cols[0:nq, i2 * 4:i2 * 4 + 4].unsqueeze(2).to_broadcast([nq, 4, 64]), ALU.mult),
                         reads=[bank_buf[ob[i2]], smb], writes=[att_buf])
                st["reserved"].discard(ob[0])
                st["reserved"].discard(ob[1])
                b = bank()
                pv = bk_bf(b, 4, 128)
                for c in range(4):
                    P.op("pe", f_tr(pv[:, c, 0:nq], att[0:nq, c * 128:(c + 1) * 128], ident[0:nq, 0:nq]),
                         reads=[att_buf, const_buf], writes=[bank_buf[b]], signal=(c == 3))
                P.op("act", f_act(dstT[:, 0:4, q0:q0 + nq], pv[:, :, 0:nq], AF.Copy), reads=[bank_buf[b]], writes=[dstT_buf])

        fifo = []
        for si in range(len(segs)):
            for h in range(8):
                ti = s_stage(si, h)
                fifo.append((si, h, ti))
                if len(fifo) > 2:
                    pv_stage(*fifo.pop(0))
        while fifo:
            pv_stage(*fifo.pop(0))

    def mem_attention(q2T, q2T_buf, q0, nq, oT, oT_buf):
        def s_stage(h):
            pi = rot("pm", 2)
            for m in range(2):
                b = bank()
                for dk in range(2):
                    P.op("pe", f_mm(ps[:, b, 0:nq], mkT[:, 2 * h + dk, m * 128:(m + 1) * 128], q2T[:, 2 * h + dk, q0:q0 + nq],
                                    dk == 0, dk == 1), reads=[mkT_buf, q2T_buf], writes=[bank_buf[b]], signal=(dk == 1))
                P.op("act", f_act(pm[pi][:, m, 0:nq], ps[:, b, 0:nq], AF.Exp), reads=[bank_buf[b]], writes=[pm_buf[pi]])
            return pi

        def o_stage(h, pi):
            b = bank()
            for m in range(2):
                P.op("pe", f_mm(ps[:, b, 0:nq], ones[:], pm[pi][:, m, 0:nq], m == 0, m == 1),
                     reads=[pm_buf[pi], const_buf], writes=[bank_buf[b]], signal=(m == 1))
            ri = rot("rs", 3)
            P.op("dve", f_recip(rs[ri][:, 0:nq], ps[:, b, 0:nq]), reads=[bank_buf[b]], writes=[rs_buf[ri]])
            for dk in range(2):
                b = bank()
                for m in range(2):
                    c0 = 256 * h + 128 * dk
                    P.op("pe", f_mm(ps[:, b, 0:nq], mv[:, m, c0:c0 + 128], pm[pi][:, m, 0:nq], m == 0, m == 1),
                         reads=[pm_buf[pi], mv_buf], writes=[bank_buf[b]], signal=(m == 1))
                P.op("dve", f_tt(oT[:, 2 * h + dk, q0:q0 + nq], ps[:, b, 0:nq], rs[ri][:, 0:nq], ALU.mult),
                     reads=[bank_buf[b], rs_buf[ri]], writes=[oT_buf, oT_kbuf[2 * h + dk]])
        prev = None
        for h in range(4):
            pi = s_stage(h)
            if prev is not None:
                o_stage(*prev)
            prev = (h, pi)
        o_stage(*prev)

    A, B = 0, 1

    def geom(kind, bi):
        sample = kind == "sample"
        tiles = [(0, 32)] if sample else [(i * 128, 128) for i in range(4)]
        N = 32 if sample else 512
        if kind == "halo":
            cur, xpar = 1, 1
        elif sample:
            cur, xpar = 1, 0
        else:
            cur, xpar = bi % 2, bi % 2
        return sample, tiles, N, cur, xpar

    def emit_xload(kind, bi):
        if kind == "halo":
            for i in range(4):
                P.dma("sp", f"x1{i}", xres[1][:, i, :], xh[i * 128:(i + 1) * 128, :], writes=[xres_buf[1][i]])
        elif kind == "sample":
            P.dma("sp", "x00", xres[0][0:32, 0, :], xs_d[:, :], writes=[xres_buf[0][0]])
        else:
            xpar = bi % 2
            for i in range(4):
                P.dma("sp", f"x{xpar}{i}", xres[xpar][:, i, :], xp[bi * NB + i * 128: bi * NB + (i + 1) * 128, :],
                      writes=[xres_buf[xpar][i]])

    def glu(kind, bi, src=None):
        sample, tiles, N, cur, xpar = geom(kind, bi)
        hT, hTb = src if src is not None else (actT[A], actT_buf[A])
        slab_v, sbf_v = use_slab("w_in", 3)
        slab_g, sbf_g = use_slab("w_in", 4)
        for c in range(4):
            bv = bank()
            for k in range(8):
                P.op("pe", f_mm(ps[:, bv, 0:N], slab_v[:, k, c * 128:(c + 1) * 128], hT[:, k, 0:N], k == 0, k == 7),
                     reads=[sbf_v, hTb], writes=[bank_buf[bv]], signal=(k == 7))
            bg = bank()
            for k in range(8):
                P.op("pe", f_mm(ps[:, bg, 0:N], slab_g[:, k, c * 128:(c + 1) * 128], hT[:, k, 0:N], k == 0, k == 7),
                     reads=[sbf_g, hTb], writes=[bank_buf[bg]], signal=(k == 7))
            gi = rot("sg", 2)
            P.op("act", f_act(sg[gi][:, 0:N], ps[:, bg, 0:N], AF.Tanh, scale=0.5), reads=[bank_buf[bg]], writes=[sg_buf[gi]])
            if sample:
                dst = uTs[:, c, :, 30:46]
                srcs = sg[gi][:, 0:32].rearrange("p (s t) -> p s t", s=2)
                srcv = ps[:, bv, 0:32].rearrange("p (s t) -> p s t", s=2)
                P.op("dve", f_stt(dst, srcs, 1.0, srcv, ALU.add, ALU.mult), reads=[sg_buf[gi], bank_buf[bv]], writes=[uTs_buf])
            else:
                P.op("dve", f_stt(uT[:, c, 30:30 + N], sg[gi][:, 0:N], 1.0, ps[:, bv, 0:N], ALU.add, ALU.mult),
                     reads=[sg_buf[gi], bank_buf[bv]], writes=[uT_buf])

    def queue_conv(kind, bi):
        sample, tiles, N, cur, xpar = geom(kind, bi)
        for j in range(31):
            for c in range(4):
                if sample:
                    src = uTs[:, c, :, j:j + 16]
                    dstc = cacc[:, c, 0:32].rearrange("p (s t) -> p s t", s=2)
                    ub = uTs_buf
                else:
                    src = uT[:, c, j:j + N]
                    dstc = cacc[:, c, 0:N]
                    ub = uT_buf
                if j == 0:
                    pending.append(lambda src=src, dstc=dstc, ub=ub, c=c: P.op(
                        "dve", f_ts(dstc, src, cvec[:, c, 0:1], cvec[:, c, 31:32], ALU.mult, ALU.add),
                        reads=[ub, cvec_buf], writes=[cacc_buf[c]]))
                else:
                    pending.append(lambda src=src, dstc=dstc, ub=ub, c=c, j=j: P.op(
                        "dve", f_stt(dstc, src, cvec[:, c, j:j + 1], dstc, ALU.mult, ALU.add),
                        reads=[ub, cvec_buf], writes=[cacc_buf[c]]))

    def conv_tail_out(kind, bi):
        sample, tiles, N, cur, xpar = geom(kind, bi)
        nseg = 2 if sample else 1
        for s in range(nseg):
            b = bank()
            nrow = 16 if sample else 30
            for c in range(4):
                srcu = uTs[:, c, s, 30:46] if sample else uT[:, c, N:N + 30]
                P.op("pe", f_tr(ps[0:nrow, b, c * 128:(c + 1) * 128], srcu, identf[:, :]),
                     reads=[uTs_buf if sample else uT_buf, const_buf], writes=[bank_buf[b]], signal=(c == 3))
            store_rows(cso_d[s, 14:30, :] if sample else ctail_d[:, :], b, nrow, 512, scale=0.5)

    def pre_begin(kind, bi):
        sample, tiles, N, cur, xpar = geom(kind, bi)
        xr, xb = xres[xpar], xres_buf[xpar]
        rowbufs = row_buffers()
        return norm_begin([(xr[0:nr, i, :], xb[i], nr, r0) for i, (r0, nr) in enumerate(tiles)], "g_mix_pre",
                          n_early=4, bufs=rowbufs)

    def pre_mid(kind, bi, stt):
        norm_finish(stt, actT[A], actT_buf[A])

    def pre_end(kind, bi):
        sample, tiles, N, cur, xpar = geom(kind, bi)
        if not sample:
            P.op("dve", f_copy(uT[:, :, 0:30], uT[:, :, 512:542]), reads=[uT_buf], writes=[uT_buf])
        glu(kind, bi)
        if sample or (kind == "prompt" and bi == NBLK - 1):
            conv_tail_out(kind, bi)
        queue_conv(kind, bi)

    def pre_finish(kind, bi, stt):
        pre_mid(kind, bi, stt)
        pre_end(kind, bi)

    def pre_stage(kind, bi):
        pre_finish(kind, bi, pre_begin(kind, bi))

    def halo_a():
        sample, tiles, N, cur, xpar = geom("halo", 0)
        xr, xb = xres[xpar], xres_buf[xpar]
        norm_tiles([(xr[0:nr, i, :], xb[i], nr, r0) for i, (r0, nr) in enumerate(tiles)], "g_mix_pre", actT[B], actT_buf[B])
        P.op("dve", f_copy(Vext[:, 4 * cur:4 * cur + 4, :, 64:65].rearrange("p t h o -> p (t h o)"),
                           vflag[:, 0:1].to_broadcast([128, 32])), reads=[vflag_buf], writes=[V_buf[cur]])
        glu("halo", 0, src=(actT[B], actT_buf[B]))

    def halo_b():
        sample, tiles, N, cur, xpar = geom("halo", 0)
        hT, hTb = actT[B], actT_buf[B]
        slab, sbf = use_slab("w_in", 1)
        kc0 = cur * 512
        proj_fm(slab, sbf, 4, hT, hTb, N, lambda f, b: P.op(
            "act", f_act(kT[:, f, kc0:kc0 + N], ps[:, b, 0:N], AF.Copy), reads=[bank_buf[b]], writes=[kT_buf[cur]]))
        slab, sbf = use_slab("w_in", 2)
        for i, (r0, nr) in enumerate(tiles):
            b = bank()
            proj_tm(slab, sbf, hT, hTb, r0, nr, b)
            P.op("act", f_act(Vext[0:nr, 4 * cur + i, :, 0:64], ps[0:nr, b, :].rearrange("p (h d) -> p h d", h=8), AF.Copy),
                 reads=[bank_buf[b]], writes=[V_buf[cur]])

    def main_stage(kind, bi, next_pre=None, post_b_hook=None, mid_hook=None, next_begin=None, next_mid=None):
        sample, tiles, N, cur, xpar = geom(kind, bi)
        last = (kind == "prompt" and bi == NBLK - 1)
        prev = 1 - cur
        xr, xb = xres[xpar], xres_buf[xpar]
        hT, hTb = actT[A], actT_buf[A]
        drain(10 ** 6)
        mixT, mixTb = actT[B], actT_buf[B]
        slab, sbf = use_slab("w_in", 0)
        proj_fm(slab, sbf, 4, hT, hTb, N, lambda f, b: P.op(
            "act", f_act(qT[:, f, 0:N], ps[:, b, 0:N], AF.Copy, scale=0.125), reads=[bank_buf[b]], writes=[qT_buf]))
        lninfo = []
        for i, (r0, nr) in enumerate(tiles):
            b = bank()
            for c in range(4):
                P.op("pe", f_tr(ps[0:nr, b, c * 128:(c + 1) * 128], cacc[:, c, r0:r0 + nr], identf[:, :]),
                     reads=[cacc_buf[c], const_buf], writes=[bank_buf[b]], signal=(c == 3))
            cols, smb, _ = sm_cols()
            P.op("dve", lambda e, nr=nr, b=b, i=i: e.bn_stats(out=bnst[0:nr, i, 0:6], in_=ps[0:nr, b, :]),
                 reads=[bank_buf[b]], writes=[smb])
            P.op("dve", lambda e, nr=nr, cols=cols, i=i: e.bn_aggr(out=cols[0:nr, 4:6], in_=bnst[0:nr, i, 0:6]),
                 reads=[smb], writes=[smb])
            lninfo.append((b, cols, smb))
        for i, (r0, nr) in enumerate(tiles):
            b, cols, smb = lninfo[i]
            P.op("pool", f_ts1(cols[0:nr, 3:4], cols[0:nr, 5:6], EPS, ALU.add), reads=[smb], writes=[smb])
            P.op("pool", f_tt(cols[0:nr, 2:3], cols[0:nr, 3:4], cst[0:nr, 0:1], ALU.pow), reads=[smb, const_buf], writes=[smb])
        for i, (r0, nr) in enumerate(tiles):
            b, cols, smb = lninfo[i]
            P.op("dve", f_stt(cols[0:nr, 6:7], cols[0:nr, 4:5], -1.0, cols[0:nr, 2:3], ALU.mult, ALU.mult), reads=[smb], writes=[smb])
            P.op("act", f_act(zc[i][0:nr, :], ps[0:nr, b, :], AF.Identity, scale=cols[0:nr, 2:3], bias=cols[0:nr, 6:7]),
                 reads=[bank_buf[b], smb], writes=[zc_buf[i]])

        slab, sbf = use_slab("w_in", 1)
        kc0 = cur * 512
        proj_fm(slab, sbf, 4, hT, hTb, N, lambda f, b: P.op(
            "dve", f_copy(kT[:, f, kc0:kc0 + N], ps[:, b, 0:N]), reads=[bank_buf[b]], writes=[kT_buf[cur]]))
        if last:
            for i, (r0, nr) in enumerate(tiles):
                b = bank()
                proj_tm(slab, sbf, hT, hTb, r0, nr, b)
                store_rows(ktail_d[r0:r0 + nr, :], b, nr, 512)
        if sample:
            for s in range(2):
                b = bank()
                proj_tm(slab, sbf, hT, hTb, s * 16, 16, b)
                store_rows(kso_d[s, 496:512, :], b, 16, 512)
        def c2b():
            cb = bank(2)
            st["reserved"].add(cb[0])
            st["reserved"].add(cb[1])
            for i, (r0, nr) in enumerate(tiles):
                for c in range(4):
                    pvc = ps[:, cb[c // 2], :].bitcast(BF16)[:, (c % 2) * 512:(c % 2) * 512 + 512]
                    P.op("pe", f_tr(pvc[:, r0:r0 + nr], zc[i][0:nr, c * 128:(c + 1) * 128], ident[0:nr, 0:nr]),
                         reads=[zc_buf[i], const_buf], writes=[bank_buf[cb[c // 2]]], signal=(c == 3))
            for c in range(4):
                pvc = ps[:, cb[c // 2], :].bitcast(BF16)[:, (c % 2) * 512:(c % 2) * 512 + 512]
                P.op("act", f_act(mixT[:, 4 + c, 0:N], pvc[:, 0:N], AF.Silu, scale=cvec[:, c, 32:33], bias=cvec[:, c, 33:34]),
                     reads=[bank_buf[cb[c // 2]], cvec_buf], writes=[mixTb])
            st["reserved"].discard(cb[0])
            st["reserved"].discard(cb[1])

        c2b()
        slab, sbf = use_slab("w_in", 2)
        if kind == "prompt" and bi == 1:
            P.op("pool", f_memset(Vext[:, 4 * cur:4 * cur + 4, :, 64:65], 1.0), writes=[V_buf[cur]])
        vsegs = [(s * 16, 16, 4 * cur + s) for s in range(2)] if sample else [(r0, nr, 4 * cur + i) for i, (r0, nr) in enumerate(tiles)]
        for si_, (c0, nr, vt) in enumerate(vsegs):
            b = bank()
            proj_tm(slab, sbf, hT, hTb, c0, nr, b)
            P.op("act", f_act(Vext[0:nr, vt, :, 0:64], ps[0:nr, b, :].rearrange("p (h d) -> p h d", h=8), AF.Copy),
                 reads=[bank_buf[b]], writes=[V_buf[cur]])
            if last:
                store_rows(vtail_d[c0:c0 + nr, :], b, nr, 512)
            if sample:
                store_rows(vso_d[si_, 496:512, :], b, 16, 512)
        if post_b_hook is not None:
            post_b_hook()

        if sample:
            for s in range(2):
                for t in range(4):
                    sap, sbufs_, ssem = hslot()
                    P.dma("sp", ssem, sap, ck_d[s, t * 128:(t + 1) * 128, :], writes=sbufs_)
                    hi = rot("hbf", 2)
                    P.op("dve", f_copy(hbf[hi][:, 0:512], sap), reads=sbufs_, writes=[hbf_buf[hi]])
                    b = bank()
                    pv = bk_bf(b, 4, 128)
                    for f in range(4):
                        P.op("pe", f_tr(pv[:, f, :], hbf[hi][:, f * 128:(f + 1) * 128], ident[:, :]),
                             reads=[hbf_buf[hi], const_buf], writes=[bank_buf[b]], signal=(f == 3))
                    P.op("act", f_act(kT[:, :, prev * 512 + t * 128: prev * 512 + (t + 1) * 128], pv[:, :, :], AF.Copy),
                         reads=[bank_buf[b]], writes=[kT_buf[prev]])
                    sap, sbufs_, ssem = hslot()
                    P.dma("sp", ssem, sap, cv_d[s, t * 128:(t + 1) * 128, :], writes=sbufs_)
                    P.op("dve", f_copy(Vext[:, 4 * prev + t, :, 0:64], sap.rearrange("p (h d) -> p h d", h=8)),
                         reads=sbufs_, writes=[V_buf[prev]])
                kt_l = [(prev * 512 + t * 128, 128, 4 * prev + t, kT_buf[prev], V_buf[prev]) for t in range(4)]
                kt_l.append((cur * 512 + s * 16, 16, 4 * cur + s, kT_buf[cur], V_buf[cur]))
                attention_block([(s * 16, 16, kt_l)], mixT, mixTb)
        else:
            segs = []
            for j in range(4):
                kt_l = []
                for t in range(5):
                    gt = j + t
                    half = prev if gt < 4 else cur
                    kt_l.append((half * 512 + (gt % 4) * 128, 128, 4 * half + gt % 4, kT_buf[half], V_buf[half]))
                segs.append((j * 128, 128, kt_l))
            attention_block(segs, mixT, mixTb)

        s0, s0b = use_slab("w_out", 0)
        s1, s1b = use_slab("w_out", 1)
        pn = PostNorm("g_mix_post")
        for i, (r0, nr) in enumerate(tiles):
            b0 = bank()
            proj_tm(s0, s0b, mixT, mixTb, r0, nr, b0)
            b1 = bank()
            proj_tm(s1, s1b, mixT, mixTb, r0, nr, b1)
            pn.add(b0, b1, nr, xr[0:nr, i, :], xb[i])
        pn.finish()

        if mid_hook is not None:
            mid_hook()
        norm_tiles([(xr[0:nr, i, :], xb[i], nr, r0) for i, (r0, nr) in enumerate(tiles)], "g_mem_pre", actT[A], actT_buf[A], wide=not sample)
        q2T, q2Tb = actT[B], actT_buf[B]
        for n in range(2):
            slab, sbf = use_slab("w_mq", n)
            proj_fm(slab, sbf, 4, actT[A], actT_buf[A], N, lambda f, b, n=n: P.op(
                "act", f_act(q2T[:, 4 * n + f, 0:N], ps[:, b, 0:N], AF.Copy, scale=1.0 / 16.0), reads=[bank_buf[b]], writes=[q2Tb]))
        oT, oTb = actT[A], actT_buf[A]
        if sample:
            for s in range(2):
                for m in range(2):
                    for src_d, is_k in ((cmk_d, True), (cmv_d, False)):
                        for hf in range(2):
                            sap, sbufs_, ssem = hslot()
                            P.dma("sp", ssem, sap, src_d[s, m * 128:(m + 1) * 128, hf * 512:(hf + 1) * 512], writes=sbufs_)
                            if is_k:
                                hi = rot("hbf", 2)
                                P.op("dve", f_copy(hbf[hi][:, 0:512], sap), reads=sbufs_, writes=[hbf_buf[hi]])
                                b = bank()
                                pv = bk_bf(b, 4, 128)
                                for f in range(4):
                                    P.op("pe", f_tr(pv[:, f, :], hbf[hi][:, f * 128:(f + 1) * 128], ident[:, :]),
                                         reads=[hbf_buf[hi], const_buf], writes=[bank_buf[b]], signal=(f == 3))
                                P.op("act", f_act(mkT[:, 4 * hf:4 * hf + 4, m * 128:(m + 1) * 128], pv[:, :, :], AF.Copy),
                                     reads=[bank_buf[b]], writes=[mkT_buf])
                            else:
                                P.op("dve", f_copy(mv[:, m, hf * 512:(hf + 1) * 512], sap),
                                     reads=sbufs_, writes=[mv_buf])
                mem_attention(q2T, q2Tb, s * 16, 16, oT, oTb)
        else:
            mem_attention(q2T, q2Tb, 0, N, oT, oTb)
        nstate = next_begin() if next_begin is not None else None
        s0, s0b = use_slab("w_mo", 0)
        s1, s1b = use_slab("w_mo", 1)
        pn = PostNorm("g_mem_post")
        mob = {}
        for i in range(len(tiles)):
            for n in range(2):
                b = bank()
                mob[(i, n)] = b
                st["reserved"].add(b)
        for khalf in range(2):
            for i, (r0, nr) in enumerate(tiles):
                for n, (sl, slb) in enumerate(((s0, s0b), (s1, s1b))):
                    b = mob[(i, n)]
                    for k in range(4 * khalf, 4 * khalf + 4):
                        P.op("pe", f_mm(ps[0:nr, b, :], oT[:, k, r0:r0 + nr], sl[:, k, :], k == 0, k == 7),
                             reads=[slb, oT_kbuf[k]], touch=[oTb], writes=[bank_buf[b]], signal=(k % 4 == 3))
        for i, (r0, nr) in enumerate(tiles):
            pn.add(mob[(i, 0)], mob[(i, 1)], nr, xr[0:nr, i, :], xb[i])
            st["reserved"].discard(mob[(i, 0)])
            st["reserved"].discard(mob[(i, 1)])
            if next_mid is not None and i == min(1, len(tiles) - 1):
                next_mid(nstate)
        pn.finish()

        gsrcs = [(xr[0:nr, i, :], xb[i], nr, r0) for i, (r0, nr) in enumerate(tiles)]
        gstate = norm_begin(gsrcs, "g_ffn_pre", n_early=4, bufs=row_buffers()) if not sample else norm_begin(gsrcs, "g_ffn_pre", n_early=0)
        if next_pre is not None:
            next_pre(nstate)

        norm_finish(gstate, actT[B], actT_buf[B])
        h3, h3b = actT[B], actT_buf[B]
        for s in range(6):
            sg_, sgb = use_slab("w_gate", s)
            su_, sub = use_slab("w_up", s)
            nf = 4 if s < 5 else 2
            for ff in range(nf):
                f = s * 4 + ff
                bg = bank()
                for k in range(8):
                    P.op("pe", f_mm(ps[:, bg, 0:N], sg_[:, k, ff * 128:(ff + 1) * 128], h3[:, k, 0:N], k == 0, k == 7),
                         reads=[sgb, h3b], writes=[bank_buf[bg]], signal=(k == 7))
                bu = bank()
                for k in range(8):
                    P.op("pe", f_mm(ps[:, bu, 0:N], su_[:, k, ff * 128:(ff + 1) * 128], h3[:, k, 0:N], k == 0, k == 7),
                         reads=[sub, h3b], writes=[bank_buf[bu]], signal=(k == 7))
                gi = rot("sg", 2)
                P.op("act", f_act(sg[gi][:, 0:N], ps[:, bg, 0:N], AF.Silu), reads=[bank_buf[bg]], writes=[sg_buf[gi]])
                P.op("dve", f_tt(hid[:, f, 0:N], sg[gi][:, 0:N], ps[:, bu, 0:N], ALU.mult),
                     reads=[sg_buf[gi], bank_buf[bu]], writes=[hid_buf[f]])
                drain(4)
        dbanks = {}
        for n in range(2):
            for i in range(len(tiles)):
                b = bank()
                dbanks[(i, n)] = b
                st["reserved"].add(b)
        for n in range(2):
            for ks in range(3):
                slab, sbf = use_slab("w_down", n * 3 + ks)
                kt = (8, 8, 6)[ks]
                for i, (r0, nr) in enumerate(tiles):
                    b = dbanks[(i, n)]
                    for k in range(kt):
                        fidx = ks * 8 + k
                        last_mm = (ks == 2 and k == kt - 1)
                        P.op("pe", f_mm(ps[0:nr, b, :], hid[:, fidx, r0:r0 + nr], slab[:, k, :], ks == 0 and k == 0, last_mm),
                             reads=[sbf, hid_buf[fidx]], writes=[bank_buf[b]], signal=(k == kt - 1))
                drain(8)
        drain(10 ** 6)
        pn = PostNorm("g_ffn_post")
        for i, (r0, nr) in enumerate(tiles):
            if sample:
                dsty = ys_d[r0:r0 + nr, :]
            else:
                dsty = y_d[bi * NB + r0: bi * NB + r0 + nr, :]
            pn.add(dbanks[(i, 0)], dbanks[(i, 1)], nr, xr[0:nr, i, :], xb[i],
                   after=lambda i=i, nr=nr, dsty=dsty: P.dma("sp", f"x{xpar}{i}", dsty, xr[0:nr, i, :], reads=[xb[i]]))
            st["reserved"].discard(dbanks[(i, 0)])
            st["reserved"].discard(dbanks[(i, 1)])
        pn.finish()

    def mem_kv():
        mT, mTb = actT[1], actT_buf[1]
        srcs = []
        for m in range(2):
            si = rot("tmp", 2)
            P.dma("sp", f"tmpl{si}", tmp[si][:], memp_d[m * 128:(m + 1) * 128, :], writes=[tmp_buf[si]])
            srcs.append((tmp[si][:, :], tmp_buf[si], 128, m * 128))
        norm_tiles(srcs, "g_mem_kv", mT, mTb)
        for n in range(2):
            slab, sbf = use_slab("w_mk", n)
            proj_fm(slab, sbf, 4, mT, mTb, 256, lambda f, b, n=n: P.op(
                "act", f_act(mkT[:, 4 * n + f, :], ps[:, b, 0:256], AF.Copy), reads=[bank_buf[b]], writes=[mkT_buf]))
            for m in range(2):
                b = bank()
                proj_tm(slab, sbf, mT, mTb, m * 128, 128, b)
                store_rows(mko_d[m * 128:(m + 1) * 128, n * 512:(n + 1) * 512], b, 128, 512)
        for n in range(2):
            slab, sbf = use_slab("w_mv", n)
            for m in range(2):
                b = bank()
                proj_tm(slab, sbf, mT, mTb, m * 128, 128, b)
                P.op("dve", f_copy(mv[:, m, n * 512:(n + 1) * 512], ps[:, b, :]), reads=[bank_buf[b]], writes=[mv_buf])
                store_rows(mvo_d[m * 128:(m + 1) * 128, n * 512:(n + 1) * 512], b, 128, 512, extra=[mv_buf])

    def sample_prep():
        for s in range(2):
            si = rot("stage", 1)
            P.dma("sp", f"stg{si}", stage[si][0:30, 0:512], cc_d[s], writes=[stage_buf[si]])
            b = bank()
            for c in range(4):
                P.op("pe", f_tr(ps[:, b, c * 32:c * 32 + 30], stage[si][0:30, c * 128:(c + 1) * 128], identf[0:30, 0:30]),
                     reads=[stage_buf[si], const_buf], writes=[bank_buf[b]], signal=(c == 3))
            P.op("act", f_act(uTs[:, :, s, 0:30], ps[:, b, 0:128].rearrange("p (c j) -> p c j", c=4)[:, :, 0:30], AF.Copy, scale=2.0),
                 reads=[bank_buf[b]], writes=[uTs_buf])
            P.dma("sp", "copy", kso_d[s, 0:496, :], ck_d[s, 16:512, :])
            P.dma("sp", "copy", vso_d[s, 0:496, :], cv_d[s, 16:512, :])
            P.dma("sp", "copy", cso_d[s, 0:14, :], cc_d[s, 16:30, :])
        P.op("pool", f_memset(Vext[:, :, :, 64:65], 1.0), writes=[V_buf[0], V_buf[1]])
        emit_xload("sample", 0)

    import os
    kstop = int(os.environ.get("KSTOP", "99"))
    steps = []
    gate_dummy = Buf("gate_dummy")

    def gated_casts(names, gate_bufs):
        P.op("pool", f_memset(cst[:, 2:3], 0.0), reads=list(gate_bufs), writes=[gate_dummy])
        cast_group(names)
    def queue_late_casts():
        cast_pending.extend([("w_mq", 0), ("w_mq", 1), ("w_mo", 0), ("w_mo", 1)])
        for i in range(6):
            cast_pending.extend([("w_gate", i), ("w_up", i)])
        cast_pending.extend([("w_down", i) for i in range(6)])
    steps.append(halo_a)
    steps.append(lambda: (pre_stage("prompt", 0), drain(10 ** 6)))
    steps.append(lambda: (halo_b(), gated_casts(["w_out", "w_mk", "w_mv"], [V_buf[1], kT_buf[1]]),
                          build_btab(), queue_late_casts()))
    for bi in range(NBLK):
        if bi + 1 < NBLK:
            nbeg = (lambda b=bi: (emit_xload("prompt", b + 1), pre_begin("prompt", b + 1))[1])
            nmid = (lambda stt, b=bi: pre_mid("prompt", b + 1, stt))
            nxt = (lambda stt, b=bi: pre_end("prompt", b + 1))
        else:
            nbeg = (lambda: (sample_prep(), pre_begin("sample", 0))[1])
            nmid = (lambda stt: pre_mid("sample", 0, stt))
            nxt = (lambda stt: pre_end("sample", 0))
        if bi == 0:
            steps.append(lambda nxt=nxt, nbeg=nbeg, nmid=nmid: main_stage("prompt", 0, nxt, mid_hook=mem_kv, next_begin=nbeg, next_mid=nmid))
        else:
            steps.append(lambda bi=bi, nxt=nxt, nbeg=nbeg, nmid=nmid: main_stage("prompt", bi, nxt, next_begin=nbeg, next_mid=nmid))
    steps.append(lambda: main_stage("sample", 0, None))
    for i, stp in enumerate(steps):
        if i >= kstop:
            break
        stp()
    if kstop >= len(steps):
        assert st["slab_pos"] == len(sched), (st["slab_pos"], len(sched))
        assert not pending

    sem_names = ["act", "dve", "pool", "pe"] + sorted(P.dval.keys())
    sems = {n: es.enter_context(nc.semaphore("s_" + n)) for n in sem_names}
    for n, v in P.dval.items():
        P.q["sp"].append(("wait", n, v))
    for e in ("act", "dve", "pool", "pe"):
        P.q["sp"].append(("wait", e, P.cnt[e]))

    def replay(engname, eobj):
        for it in P.q[engname]:
            if it[0] == "wait":
                eobj.wait_ge(sems[it[1]], it[2])
            elif it[0] == "op":
                ins = it[1](eobj)
                if it[2]:
                    ins.then_inc(sems[engname], 1)
            else:
                eobj.dma_start(out=it[1], in_=it[2]).then_inc(sems[it[3]], 16)

    with nc.Block() as block:
        @block.sync
        def _(e):
            replay("sp", e)

        @block.scalar
        def _(e):
            replay("act", e)

        @block.vector
        def _(e):
            replay("dve", e)

        @block.gpsimd
        def _(e):
            replay("pool", e)

        @block.tensor
        def _(e):
            replay("pe", e)
    es.close()
    nc._prog_stats = {e: len(P.q[e]) for e in P.ENG}
    return nc


_NC_CACHE = {}


def kernel(x_prompt, x_sample, cache_att_k, cache_att_v, cache_conv, cache_mem_k, cache_mem_v, mem_prompt,
           g_mix_pre, g_mix_post, w_in, rel_bias, conv_w, conv_b, cln_g, cln_b, w_out,
           g_mem_pre, g_mem_post, g_mem_kv, w_mq, w_mk, w_mv, w_mo, g_ffn_pre, g_ffn_post, w_gate, w_up, w_down):
    f = lambda a: np.ascontiguousarray(np.asarray(a, dtype=np.float32))
    x_prompt, x_sample = f(x_prompt), f(x_sample)
    if "nc" not in _NC_CACHE:
        _NC_CACHE["nc"] = build_nc()
    nc = _NC_CACHE["nc"]
    cvec = np.concatenate([f(conv_w)[0], f(conv_b)[0][None], f(cln_g)[0][None], f(cln_b)[0][None]], axis=0)
    shared = {
        "g_mix_pre": f(g_mix_pre), "g_mix_post": f(g_mix_post), "g_mem_pre": f(g_mem_pre), "g_mem_post": f(g_mem_post),
        "g_mem_kv": f(g_mem_kv), "g_ffn_pre": f(g_ffn_pre), "g_ffn_post": f(g_ffn_post),
        "rel_bias": f(rel_bias)[0], "cvec": f(cvec),
        "w_in": f(w_in)[0], "w_out": f(w_out)[0], "w_mq": f(w_mq)[0], "w_mk": f(w_mk)[0], "w_mv": f(w_mv)[0],
        "w_mo": f(w_mo)[0], "w_gate": f(w_gate)[0], "w_up": f(w_up)[0], "w_down": f(w_down)[0],
    }
    ck, cv = f(cache_att_k)[0], f(cache_att_v)[0]
    cc, cmk, cmv = f(cache_conv)[0], f(cache_mem_k)[0], f(cache_mem_v)[0]
    memp = f(mem_prompt)
    in_maps = []
    for c in range(8):
        b, qt = c // 4, c % 4
        m = dict(shared)
        m["xp"] = x_prompt[b, qt * 4096:(qt + 1) * 4096]
        m["xh"] = x_prompt[b, qt * 4096 - 512:qt * 4096] if qt > 0 else np.zeros((512, D), np.float32)
        m["vflag"] = np.full((128, 1), 1.0 if qt > 0 else 0.0, np.float32)
        m["xs"] = x_sample[2 * c:2 * c + 2].reshape(32, D)
        m["ck"] = ck[2 * c:2 * c + 2].reshape(2, 512, 512)
        m["cv"] = cv[2 * c:2 * c + 2].reshape(2, 512, 512)
        m["cc"] = cc[2 * c:2 * c + 2]
        m["cmk"] = cmk[2 * c:2 * c + 2].reshape(2, 256, D)
        m["cmv"] = cmv[2 * c:2 * c + 2].reshape(2, 256, D)
        m["memp"] = memp[b]
        in_maps.append({k: np.ascontiguousarray(v) for k, v in m.items()})
    res = run_bass_kernel_spmd(nc, in_maps, core_ids=list(range(8)))
    R = res.results
    y_prompt = np.stack([np.concatenate([R[4 * b + q]["y"] for q in range(4)], axis=0) for b in range(2)], axis=0)
    y_sample = np.concatenate([R[c]["ys"].reshape(2, 16, D) for c in range(8)], axis=0)
    nk = np.stack([R[4 * b + 3]["ktail"].reshape(512, 8, 64) for b in range(2)], axis=0)[None]
    nv = np.stack([R[4 * b + 3]["vtail"].reshape(512, 8, 64) for b in range(2)], axis=0)[None]
    ncv = np.stack([R[4 * b + 3]["ctail"] for b in range(2)], axis=0)[None]
    nmk = np.stack([R[4 * b]["mko"].reshape(256, 4, 256) for b in range(2)], axis=0)[None]
    nmv = np.stack([R[4 * b]["mvo"].reshape(256, 4, 256) for b in range(2)], axis=0)[None]
    ks = np.concatenate([R[c]["kso"].reshape(2, 512, 8, 64) for c in range(8)], axis=0)[None]
    vs = np.concatenate([R[c]["vso"].reshape(2, 512, 8, 64) for c in range(8)], axis=0)[None]
    cs = np.concatenate([R[c]["cso"] for c in range(8)], axis=0)[None]
    out = (y_prompt, y_sample, nk, nv, ncv, nmk, nmv, ks, vs, cs)
    return tuple(np.ascontiguousarray(o, dtype=np.float32) for o in out)
```

```python
import contextlib
import numpy as np
import concourse.bass as bass
import concourse.mybir as mybir
from concourse.bass_utils import run_bass_kernel_spmd

F32 = mybir.dt.float32
BF16 = mybir.dt.bfloat16
AF = mybir.ActivationFunctionType
ALU = mybir.AluOpType

D = 1024
NBLK = 8
NB = 512
D_FF = 2816
EPS = 1e-6
NEG = -1e30
DEBUG = False


class Buf:
    __slots__ = ("name", "w", "r")

    def __init__(self, name):
        self.name = name
        self.w = None
        self.r = {}


class Prog:
    ENG = ("sp", "act", "dve", "pool", "pe")
    SAME_WAIT = {"pool", "act", "dve"}

    def __init__(self):
        self.q = {e: [] for e in self.ENG}
        self.cnt = {e: 0 for e in self.ENG}
        self.seen = {e: {} for e in self.ENG}
        self.dval = {}

    def _waits(self, eng, reads, writes):
        need = {}
        for b in reads:
            if b.w is not None:
                need[b.w[0]] = max(need.get(b.w[0], 0), b.w[1])
        for b in writes:
            if b.w is not None:
                need[b.w[0]] = max(need.get(b.w[0], 0), b.w[1])
            for k, v in b.r.items():
                need[k] = max(need.get(k, 0), v)
        for k, v in need.items():
            if k == eng and eng not in self.SAME_WAIT:
                continue
            if self.seen[eng].get(k, 0) >= v:
                continue
            self.seen[eng][k] = v
            self.q[eng].append(("wait", k, v))

    def op(self, eng, fn, reads=(), writes=(), signal=True, touch=()):
        self._waits(eng, reads, writes)
        if signal:
            self.cnt[eng] += 1
            tok = (eng, self.cnt[eng])
        else:
            tok = (eng, self.cnt[eng] + 1)
        self.q[eng].append(("op", fn, signal))
        for b in list(reads) + list(touch):
            b.r[eng] = max(b.r.get(eng, 0), tok[1])
        for b in writes:
            b.w = tok
            b.r = {}
        return tok

    def dma(self, qeng, sem, out, in_, reads=(), writes=()):
        self._waits(qeng, reads, writes)
        self.dval[sem] = self.dval.get(sem, 0) + 16
        tok = (sem, self.dval[sem])
        self.q[qeng].append(("dma", out, in_, sem))
        for b in reads:
            b.r[sem] = max(b.r.get(sem, 0), tok[1])
        for b in writes:
            b.w = tok
            b.r = {}
        return tok


def f_mm(out, lhsT, rhs, start, stop):
    return lambda e: e.matmul(out, lhsT=lhsT, rhs=rhs, start=start, stop=stop)


def f_tr(out, in_, ident):
    return lambda e: e.transpose(out=out, in_=in_, identity=ident)


def f_act(out, in_, func, scale=None, bias=None, accum=None):
    def f(e):
        kw = {}
        if scale is not None:
            kw["scale"] = scale
        if bias is not None:
            kw["bias"] = bias
        if accum is not None:
            kw["accum_out"] = accum
        return e.activation(out=out, in_=in_, func=func, **kw)
    return f


def f_tt(out, a, b, op):
    return lambda e: e.tensor_tensor(out=out, in0=a, in1=b, op=op)


def f_stt(out, a, s, b, op0, op1):
    return lambda e: e.scalar_tensor_tensor(out=out, in0=a, scalar=s, in1=b, op0=op0, op1=op1)


def f_ts(out, a, s1, s2, op0, op1):
    return lambda e: e.tensor_scalar(out=out, in0=a, scalar1=s1, scalar2=s2, op0=op0, op1=op1)


def f_ts1(out, a, s1, op0):
    return lambda e: e.tensor_scalar(out=out, in0=a, scalar1=s1, scalar2=None, op0=op0)


def f_copy(out, a):
    return lambda e: e.tensor_copy(out=out, in_=a)


def f_memset(ap, v):
    return lambda e: e.memset(ap, v)


def f_recip(out, a):
    return lambda e: e.reciprocal(out=out, in_=a)


def build_nc():
    nc = bass.Bass("TRN2", target_bir_lowering=False)
    P = Prog()
    es = contextlib.ExitStack()

    def din(name, shape):
        return nc.dram_tensor(name, list(shape), F32, kind="ExternalInput")

    def dout(name, shape):
        return nc.dram_tensor(name, list(shape), F32, kind="ExternalOutput")

    xp = din("xp", [4096, D]).ap()
    xh = din("xh", [512, D]).ap()
    vflag_d = din("vflag", [128, 1]).ap()
    xs_d = din("xs", [32, D]).ap()
    ck_d = din("ck", [2, 512, 512]).ap()
    cv_d = din("cv", [2, 512, 512]).ap()
    cc_d = din("cc", [2, 30, 512]).ap()
    cmk_d = din("cmk", [2, 256, D]).ap()
    cmv_d = din("cmv", [2, 256, D]).ap()
    memp_d = din("memp", [256, D]).ap()
    gv = {n: din(n, [1, D]).ap() for n in
          ("g_mix_pre", "g_mix_post", "g_mem_pre", "g_mem_post", "g_mem_kv", "g_ffn_pre", "g_ffn_post")}
    rb_d = din("rel_bias", [8, 257]).ap()
    cvec_d = din("cvec", [34, 512]).ap()
    W = {"w_in": din("w_in", [D, 2560]), "w_out": din("w_out", [D, D]), "w_mq": din("w_mq", [D, D]),
         "w_mk": din("w_mk", [D, D]), "w_mv": din("w_mv", [D, D]), "w_mo": din("w_mo", [D, D]),
         "w_gate": din("w_gate", [D, D_FF]), "w_up": din("w_up", [D, D_FF]), "w_down": din("w_down", [D_FF, D])}

    y_d = dout("y", [4096, D]).ap()
    ys_d = dout("ys", [32, D]).ap()
    ktail_d = dout("ktail", [512, 512]).ap()
    vtail_d = dout("vtail", [512, 512]).ap()
    ctail_d = dout("ctail", [30, 512]).ap()
    mko_d = dout("mko", [256, D]).ap()
    mvo_d = dout("mvo", [256, D]).ap()
    kso_d = dout("kso", [2, 512, 512]).ap()
    vso_d = dout("vso", [2, 512, 512]).ap()
    cso_d = dout("cso", [2, 30, 512]).ap()

    nsl = {"w_in": 5, "w_out": 2, "w_mq": 2, "w_mk": 2, "w_mv": 2, "w_mo": 2, "w_gate": 6, "w_up": 6, "w_down": 6}
    wsc = {n: nc.dram_tensor("ws_" + n, [k, 128, 8, 512], BF16) for n, k in nsl.items()}
    wsc_buf = {n: Buf("ws_" + n) for n in nsl}
    per = nc.dram_tensor("per", [8, 129, 768], F32)
    per_buf = Buf("per")

    def slab_geom(name, idx):
        if name == "w_down":
            n, ks = idx // 3, idx % 3
            return ks * 1024, (8, 8, 6)[ks], n * 512, 512
        if name in ("w_gate", "w_up"):
            return 0, 8, idx * 512, min(512, D_FF - idx * 512)
        return 0, 8, idx * 512, 512

    def sb(name, shape, dtype):
        return es.enter_context(nc.sbuf_tensor("t_" + name, list(shape), dtype))

    NSLOT = 4
    wring = [sb(f"wring{i}", [128, 8, 512], BF16) for i in range(NSLOT)]
    wring_buf = [Buf(f"wring{i}") for i in range(NSLOT)]
    xres = [sb(f"xres{i}", [128, 4, D], F32) for i in range(2)]
    xres_buf = [[Buf(f"xres{p}_{i}") for i in range(4)] for p in range(2)]
    actT = [sb(f"actT{i}", [128, 8, 512], BF16) for i in range(2)]
    actT_buf = [Buf(f"actT{i}") for i in range(2)]
    oT_kbuf = [Buf(f"oTk{k}") for k in range(8)]
    qT = sb("qT", [128, 4, 512], BF16)
    qT_buf = Buf("qT")
    kT = sb("kT", [128, 4, 1024], BF16)
    kT_buf = [Buf("kT0"), Buf("kT1")]
    Vext = sb("Vext", [128, 8, 8, 65], BF16)
    V_buf = [Buf("V0"), Buf("V1")]
    uT = sb("uT", [128, 4, 30 + 512], F32)
    uT_buf = Buf("uT")
    uTs = sb("uTs", [128, 4, 2, 46], F32)
    uTs_buf = Buf("uTs")
    cacc = sb("cacc", [128, 4, 512], F32)
    cacc_buf = [Buf(f"cacc{c}") for c in range(4)]
    sg = [sb(f"sg{i}", [128, 512], F32) for i in range(2)]
    sg_buf = [Buf(f"sg{i}") for i in range(2)]
    att = sb("att", [128, 512], BF16)
    att_buf = Buf("att")
    pT = [sb(f"pT{i}", [128, 5, 128], BF16) for i in range(3)]
    pT_buf = [Buf(f"pT{i}") for i in range(3)]
    Btab = sb("Btab", [128, 3, 8, 128], F32)
    Btab_buf = Buf("Btab")
    hid = sb("hid", [128, 22, 512], BF16)
    hid_buf = [Buf(f"hid{f}") for f in range(22)]
    pm = [sb(f"pm{i}", [128, 2, 512], BF16) for i in range(2)]
    pm_buf = [Buf(f"pm{i}") for i in range(2)]
    rs = [sb(f"rs{i}", [128, 512], F32) for i in range(3)]
    rs_buf = [Buf(f"rs{i}") for i in range(3)]
    tb = [rs[i][:, 0:384].rearrange("p (s n) -> p s n", s=3) for i in range(3)]
    tb_buf = rs_buf
    mkT = sb("mkT", [128, 8, 256], BF16)
    mkT_buf = Buf("mkT")
    mv = sb("mv", [128, 2, D], BF16)
    mv_buf = Buf("mv")
    gbuf = [sb(f"gbuf{i}", [128, D], F32) for i in range(2)]
    gbuf_buf = [Buf(f"gbuf{i}") for i in range(2)]
    hbf = [sb(f"hbf{i}", [128, D], BF16) for i in range(2)]
    hbf_buf = [Buf(f"hbf{i}") for i in range(2)]
    zc = [hbf[i // 2][:, (i % 2) * 512:(i % 2 + 1) * 512] for i in range(4)]
    zc_buf = [hbf_buf[i // 2] for i in range(4)]
    tmp = [sb(f"tmp{i}", [128, D], F32) for i in range(2)]
    tmp_buf = [Buf(f"tmp{i}") for i in range(2)]
    stage = [sb(f"stage{i}", [128, 512], F32) for i in range(1)]
    stage_buf = [Buf(f"stage{i}") for i in range(1)]
    ident = sb("ident", [128, 128], BF16)
    identf = sb("identf", [128, 128], F32)
    ones = sb("ones", [128, 128], BF16)
    const_buf = Buf("const")
    cvec_s = tmp[1]
    cvec = sb("cvecT", [128, 4, 34], F32)
    cvec_buf = Buf("cvec")
    cvs_buf = tmp_buf[1]
    frow = tmp[0]
    frow_buf = tmp_buf[0]
    vflag = sb("vflag_s", [128, 1], F32)
    vflag_buf = Buf("vflag")
    sm = sb("sm", [128, 64], F32)
    sm_buf = [Buf(f"sm{i}") for i in range(8)]
    cst = sb("cst", [128, 4], F32)
    bnst = sb("bnst", [128, 4, 8], F32)
    ps = es.enter_context(nc.psum_tensor("ps", [128, 8, 512], F32))
    bank_buf = [Buf(f"bank{i}") for i in range(8)]

    st = {"rr": 0, "reserved": set(), "slab_pos": 0, "sm": 0, "g": 0, "hbf": 0, "tmp": 0, "stage": 0, "sg": 0,
          "tb": 0, "pm": 0, "rs": 0, "hslot": 0}

    def bank(n=1):
        for _ in range(16):
            p = st["rr"]
            if n == 2 and p % 2 == 1:
                p = (p + 1) % 8
            cand = [(p + i) % 8 for i in range(n)]
            st["rr"] = (p + n) % 8
            if not any(c in st["reserved"] for c in cand):
                return cand if n > 1 else cand[0]
        raise RuntimeError(f"no psum bank n={n} reserved={sorted(st['reserved'])} rr={st['rr']}")

    def rot(key, n):
        v = st[key]
        st[key] = (v + 1) % n
        return v

    def bk(b):
        return ps[:, b, :]

    def bk_bf(b, k, n):
        return ps[:, b, :].bitcast(BF16).rearrange("p (k n) -> p k n", k=k)[:, :, 0:n]

    def blk_seq(with_next):
        seq = [("w_in", 0), ("w_in", 1), ("w_in", 2), ("w_out", 0), ("w_out", 1), ("w_mq", 0), ("w_mq", 1),
               ("w_mo", 0), ("w_mo", 1)]
        if with_next:
            seq += [("w_in", 3), ("w_in", 4)]
        seq += [x for s in range(6) for x in (("w_gate", s), ("w_up", s))]
        seq += [("w_down", i) for i in range(6)]
        return seq
    sched = [("w_in", 3), ("w_in", 4), ("w_in", 3), ("w_in", 4), ("w_in", 1), ("w_in", 2)]
    for _b in range(NBLK + 1):
        seq = blk_seq(_b < NBLK)
        if _b == 0:
            i0 = seq.index(("w_mq", 0))
            seq = seq[:i0] + [("w_mk", 0), ("w_mk", 1), ("w_mv", 0), ("w_mv", 1)] + seq[i0:]
        sched += seq
    DEPTH = 3
    loaded = {"n": 0}

    def issue_loads(upto):
        while loaded["n"] < min(upto, len(sched)):
            p = loaded["n"]
            name, idx = sched[p]
            r0, kt, c0, ncols = slab_geom(name, idx)
            slot = p % NSLOT
            ensure_cast(name, idx)
            P.dma("sp", f"wr{slot}", wring[slot][:, 0:kt, 0:ncols], wsc[name].ap()[idx][:, 0:kt, 0:ncols],
                  reads=[slab_cast_buf[(name, idx)]], writes=[wring_buf[slot]])
            loaded["n"] += 1

    def use_slab(name, idx):
        p = st["slab_pos"]
        assert sched[p] == (name, idx), (p, sched[p], name, idx)
        issue_loads(p + DEPTH)
        emit_casts(2)
        st["slab_pos"] = p + 1
        slot = p % NSLOT
        return wring[slot], wring_buf[slot]

    slab_cast_buf = {}
    cast_pending = []

    def cast_one(name, idx):
        r0, kt, c0, ncols = slab_geom(name, idx)
        src = W[name].ap()[r0:r0 + kt * 128, c0:c0 + ncols].rearrange("(ko p) c -> p ko c", p=128)
        bf = Buf(f"ws_{name}_{idx}")
        P.dma("pool", f"cast_{name}_{idx}", wsc[name].ap()[idx][:, 0:kt, 0:ncols], src, writes=[bf])
        slab_cast_buf[(name, idx)] = bf

    def emit_casts(n):
        k = 0
        while cast_pending and k < n:
            cast_one(*cast_pending.pop(0))
            k += 1

    def ensure_cast(name, idx):
        while (name, idx) not in slab_cast_buf:
            assert cast_pending, ("no cast scheduled for", name, idx)
            cast_one(*cast_pending.pop(0))

    def cast_group(names):
        for name in names:
            order = [3, 4, 1, 2, 0] if name == "w_in" else list(range(nsl[name]))
            for idx in order:
                r0, kt, c0, ncols = slab_geom(name, idx)
                src = W[name].ap()[r0:r0 + kt * 128, c0:c0 + ncols].rearrange("(ko p) c -> p ko c", p=128)
                bf = Buf(f"ws_{name}_{idx}")
                P.dma("pool", f"cast_{name}_{idx}", wsc[name].ap()[idx][:, 0:kt, 0:ncols], src, writes=[bf])
                slab_cast_buf[(name, idx)] = bf

    P.op("pool", f_memset(ident[:], 0.0), writes=[const_buf])
    P.op("pool", lambda e: e.affine_select(out=ident[:], in_=ident[:], pattern=[[-1, 128]], compare_op=ALU.not_equal,
                                           fill=1.0, base=0, channel_multiplier=1), writes=[const_buf])
    P.op("pool", f_memset(identf[:], 0.0), writes=[const_buf])
    P.op("pool", lambda e: e.affine_select(out=identf[:], in_=identf[:], pattern=[[-1, 128]], compare_op=ALU.not_equal,
                                           fill=1.0, base=0, channel_multiplier=1), writes=[const_buf])
    P.op("pool", f_memset(ones[:], 1.0), writes=[const_buf])
    P.op("pool", f_memset(cst[:, 0:1], -0.5), writes=[const_buf])
    P.op("pool", f_memset(cst[:, 1:2], EPS), writes=[const_buf])
    P.op("pool", f_memset(Vext[:], 1.0), writes=[V_buf[0], V_buf[1]])
    P.op("pool", f_memset(uT[:], 0.0), writes=[uT_buf])

    cast_group(["w_in"])
    for i in range(4):
        P.dma("sp", f"x1{i}", xres[1][:, i, :], xh[i * 128:(i + 1) * 128, :], writes=[xres_buf[1][i]])
    for i in range(4):
        P.dma("sp", f"x0{i}", xres[0][:, i, :], xp[i * 128:(i + 1) * 128, :], writes=[xres_buf[0][i]])
    P.dma("sp", "setup", vflag[:], vflag_d, writes=[vflag_buf])
    P.dma("sp", "setup", cvec_s[0:34, 0:512], cvec_d, writes=[cvs_buf])
    P.dma("sp", "setup", frow[0:8, 0:256], rb_d[:, 1:257], writes=[frow_buf])
    setup_tok = ("setup", P.dval["setup"])
    vflag_buf.w = cvs_buf.w = frow_buf.w = setup_tok
    P.op("dve", f_copy(frow[0:8, 256:768], frow[0:8, 255:256].to_broadcast([8, 512])), reads=[frow_buf], writes=[frow_buf])
    for h in range(8):
        P.dma("pool", "per", per.ap()[h:h + 1], frow[h:h + 1, 0:768].unsqueeze(1).to_broadcast([1, 129, 768]),
              reads=[frow_buf], writes=[])
    per_buf.w = ("per", P.dval["per"])

    b = bank()
    for c in range(4):
        P.op("pe", f_tr(ps[:, b, c * 34:(c + 1) * 34], cvec_s[0:34, c * 128:(c + 1) * 128], identf[0:34, 0:34]),
             reads=[cvs_buf, const_buf], writes=[bank_buf[b]], signal=(c == 3))
    P.op("dve", f_copy(cvec[:].rearrange("p c j -> p (c j)"), ps[:, b, 0:136]), reads=[bank_buf[b]], writes=[cvec_buf])
    P.op("dve", f_ts1(cvec[:, :, 0:31], cvec[:, :, 0:31], 0.5, ALU.mult), reads=[cvec_buf], writes=[cvec_buf])

    def build_btab():
        for si, t in enumerate((0, 3, 4)):
            src = bass.AP(per, 639 - 128 * t, [[767, 128], [129 * 768, 8], [1, 128]])
            P.dma("sp", "btab", Btab[:, si], src, reads=[per_buf], writes=[])
        Btab_buf.w = ("btab", P.dval["btab"])
        P.op("pool", f_memset(Btab[0:64, 0, :, 64:128], NEG), reads=[], writes=[Btab_buf])
        P.op("pool", f_memset(Btab[64:128, 2, :, 0:64], NEG), reads=[], writes=[Btab_buf])

    def sm_cols(n=8):
        g = rot("sm", 8)
        return sm[:, g * 8:g * 8 + n], sm_buf[g], g

    def load_g(name):
        i = rot("g", 2)
        P.dma("sp", f"g{i}", gbuf[i][:], gv[name].to_broadcast([128, D]), writes=[gbuf_buf[i]])
        return gbuf[i], gbuf_buf[i]

    def rstd_from_ss(col_ap, nr, smb, n_terms):
        if n_terms == 2:
            P.op("pool", f_tt(col_ap[0:nr, 0:1], col_ap[0:nr, 0:1], col_ap[0:nr, 1:2], ALU.add), reads=[smb], writes=[smb])
        P.op("pool", f_ts(col_ap[0:nr, 3:4], col_ap[0:nr, 0:1], 1.0 / D, EPS, ALU.mult, ALU.add), reads=[smb], writes=[smb])
        P.op("pool", f_tt(col_ap[0:nr, 2:3], col_ap[0:nr, 3:4], cst[0:nr, 0:1], ALU.pow), reads=[smb, const_buf], writes=[smb])

    def norm_begin(srcs, g_name, n_early=2, bufs=None):
        g_t, g_b = load_g(g_name)
        info = []
        for (src_ap, src_buf, nr, col0) in srcs:
            cols, smb, _ = sm_cols()
            if bufs is None:
                hi = rot("hbf", 2)
                hb_ap, hb_buf = hbf[hi], hbf_buf[hi]
            else:
                hb_ap, hb_buf = bufs[len(info) % len(bufs)]
            P.op("act", f_act(hb_ap[0:nr, :], src_ap, AF.Square, accum=cols[0:nr, 0:1]), reads=[src_buf], writes=[smb, hb_buf])
            info.append([cols, smb, (hb_ap, hb_buf), False])
        for (src_ap, src_buf, nr, col0), (cols, smb, hi, _) in zip(srcs, info):
            rstd_from_ss(cols, nr, smb, 1)
        stt = {"srcs": srcs, "info": info, "g": (g_t, g_b)}
        for i in range(min(n_early, len(srcs))):
            norm_h(stt, i)
        return stt

    def norm_h(stt, i):
        (src_ap, src_buf, nr, col0) = stt["srcs"][i]
        cols, smb, hi, done = stt["info"][i]
        if done:
            return
        g_t, g_b = stt["g"]
        hb_ap, hb_buf = hi
        P.op("dve", f_stt(hb_ap[0:nr, :], src_ap, cols[0:nr, 2:3], g_t[0:nr, :], ALU.mult, ALU.mult),
             reads=[src_buf, smb, g_b], writes=[hb_buf])
        stt["info"][i][3] = True

    def norm_finish(stt, dstT, dstT_buf):
        for i, (src_ap, src_buf, nr, col0) in enumerate(stt["srcs"]):
            norm_h(stt, i)
            cols, smb, (hb_ap, hb_buf), _ = stt["info"][i]
            b = bank()
            pv = bk_bf(b, 8, 128)
            for k in range(8):
                P.op("pe", f_tr(pv[:, k, 0:nr], hb_ap[0:nr, k * 128:(k + 1) * 128], ident[0:nr, 0:nr]),
                     reads=[hb_buf, const_buf], writes=[bank_buf[b]], signal=(k == 7))
            P.op("act", f_act(dstT[:, :, col0:col0 + nr], pv[:, :, 0:nr], AF.Copy), reads=[bank_buf[b]], writes=[dstT_buf])

    def row_buffers():
        return [(hbf[0], hbf_buf[0]), (hbf[1], hbf_buf[1]),
                (pm[0][:].rearrange("p a b -> p (a b)"), pm_buf[0]), (pm[1][:].rearrange("p a b -> p (a b)"), pm_buf[1])]

    def norm_tiles(srcs, g_name, dstT, dstT_buf, wide=False):
        if wide:
            norm_finish(norm_begin(srcs, g_name, n_early=4, bufs=row_buffers()), dstT, dstT_buf)
        else:
            norm_finish(norm_begin(srcs, g_name, n_early=0), dstT, dstT_buf)

    def proj_fm(slab, slab_b, ncol_tiles, src, src_buf, N, evac):
        for f in range(ncol_tiles):
            b = bank()
            for k in range(8):
                P.op("pe", f_mm(ps[:, b, 0:N], slab[:, k, f * 128:(f + 1) * 128], src[:, k, 0:N], k == 0, k == 7),
                     reads=[slab_b, src_buf], writes=[bank_buf[b]], signal=(k == 7))
            evac(f, b)

    def proj_tm(slab, slab_b, src, src_buf, c0, nr, b, kt=8, k0=0, start=True, stop=True, ncols=512):
        for k in range(kt):
            P.op("pe", f_mm(ps[0:nr, b, 0:ncols], src[:, k0 + k, c0:c0 + nr], slab[:, k, 0:ncols],
                            start and k == 0, stop and k == kt - 1),
                 reads=[slab_b, src_buf], writes=[bank_buf[b]], signal=(stop and k == kt - 1))

    def store_rows(dst_ap, src_bank, nr, ncols, scale=None, col0=0, extra=()):
        si = rot("stage", 1)
        if scale is None:
            P.op("act", f_act(stage[si][0:nr, 0:ncols], ps[0:nr, src_bank, col0:col0 + ncols], AF.Copy),
                 reads=[bank_buf[src_bank]] + list(extra), writes=[stage_buf[si]])
        else:
            P.op("act", f_act(stage[si][0:nr, 0:ncols], ps[0:nr, src_bank, col0:col0 + ncols], AF.Copy, scale=scale),
                 reads=[bank_buf[src_bank]] + list(extra), writes=[stage_buf[si]])
        P.dma("sp", f"stg{si}", dst_ap, stage[si][0:nr, 0:ncols], reads=[stage_buf[si]])

    class PostNorm:
        def __init__(self, g_name):
            self.g_t, self.g_b = load_g(g_name)
            self.pend = None

        def add(self, b0, b1, nr, x_ap, x_buf, after=None):
            g_t, g_b = self.g_t, self.g_b
            cols, smb, _ = sm_cols()
            ti = rot("tmp", 2)
            for n, bb in enumerate((b0, b1)):
                P.op("act", f_act(tmp[ti][0:nr, n * 512:(n + 1) * 512], ps[0:nr, bb, :], AF.Square, accum=cols[0:nr, n:n + 1]),
                     reads=[bank_buf[bb]], writes=[smb, tmp_buf[ti]])
            for n, bb in enumerate((b0, b1)):
                P.op("dve", f_tt(tmp[ti][0:nr, n * 512:(n + 1) * 512], ps[0:nr, bb, :], g_t[0:nr, n * 512:(n + 1) * 512], ALU.mult),
                     reads=[bank_buf[bb], smb, g_b], writes=[tmp_buf[ti]])
            rstd_from_ss(cols, nr, smb, 2)

            def xupd():
                P.op("dve", f_stt(x_ap, tmp[ti][0:nr, :], cols[0:nr, 2:3], x_ap, ALU.mult, ALU.add),
                     reads=[tmp_buf[ti], smb, x_buf], writes=[x_buf])
                if after is not None:
                    after()
            if self.pend is not None:
                self.pend()
            self.pend = xupd

        def finish(self):
            if self.pend is not None:
                self.pend()
            self.pend = None

    pending = []

    def drain(n):
        k = 0
        while pending and k < n:
            pending.pop(0)()
            k += 1

    def hslot():
        j = rot("hslot", 11)
        ap = hid[:, 2 * j:2 * j + 2, :].rearrange("p a b -> p (a b)").bitcast(F32)
        return ap, [hid_buf[2 * j], hid_buf[2 * j + 1]], f"hst{j}"

    SLOT = {1: 0, 2: 1, 0: 2, 3: 3, 4: 4}

    def attention_block(segs, dstT, dstT_buf):
        obs = {}

        def s_stage(si, h):
            q0, nq, ktiles = segs[si]
            base = 64 * (h % 2)
            f = h // 2
            sbk = bank(2)
            assert sbk[1] == sbk[0] + 1
            spv = ps[:, sbk[0]:sbk[0] + 2, :].rearrange("p b n -> p (b n)")
            sbufs = [bank_buf[sbk[0]], bank_buf[sbk[1]]]
            for t in range(5):
                kcol, nk, vt, kb, vb = ktiles[t]
                s = SLOT[t]
                P.op("pe", f_mm(spv[0:nk, s * 128:s * 128 + nq], kT[base:base + 64, f, kcol:kcol + nk],
                                qT[base:base + 64, f, q0:q0 + nq], True, True),
                     reads=[kb, qT_buf], writes=sbufs, signal=(t == 4))
            ti = rot("tb", 3)
            s3 = spv[:, 256:640].rearrange("p (s n) -> p s n", s=3)[:, :, 0:nq]
            P.op("dve", f_tt(tb[ti][:, :, 0:nq], s3, Btab[:, :, h, 0:nq], ALU.add),
                 reads=sbufs + [Btab_buf], writes=[tb_buf[ti]])
            s2 = spv[:, 0:256].rearrange("p (s n) -> p s n", s=2)[:, :, 0:nq]
            P.op("act", f_act(pT[ti][:, 0:2, 0:nq], s2, AF.Exp, bias=Btab[:, 0, h, 0:1]),
                 reads=sbufs + [Btab_buf, tb_buf[ti]], writes=[pT_buf[ti]])
            P.op("act", f_act(pT[ti][:, 2:5, 0:nq], tb[ti][:, :, 0:nq], AF.Exp), reads=[tb_buf[ti]], writes=[pT_buf[ti]])
            return ti

        epi_pending = []

        def pv_stage(si, h, ti):
            q0, nq, ktiles = segs[si]
            while epi_pending and h >= 1:
                epi_pending.pop(0)()
            if h == 0:
                ob = bank(2)
                st["reserved"].add(ob[0])
                st["reserved"].add(ob[1])
                obs[si] = ob
            ob = obs[si]
            o = ob[h // 4]
            hc = (h % 4) * 65
            for t in range(5):
                kcol, nk, vt, kb, vb = ktiles[t]
                P.op("pe", f_mm(ps[0:nq, o, hc:hc + 65], pT[ti][0:nk, SLOT[t], 0:nq], Vext[0:nk, vt, h, :], t == 0, t == 4),
                     reads=[pT_buf[ti], vb], writes=[bank_buf[o]], signal=(t == 4))
            if h == 7:
                cols, smb, _ = sm_cols()
                for i2 in range(2):
                    ov = ps[0:nq, ob[i2], 0:260].rearrange("p (h d) -> p h d", h=4)
                    P.op("dve", f_recip(cols[0:nq, i2 * 4:i2 * 4 + 4], ov[:, :, 64]), reads=[bank_buf[ob[i2]]], writes=[smb])
                    P.op("dve", f_tt(att[0:nq, i2 * 256:(i2 + 1) * 256].rearrange("p (h d) -> p h d", h=4), ov[:, :, 0:64],
                                     cols[0:nq, i2 * 4:i2 * 4 + 4].unsqueeze(2).to_broadcast([nq, 4, 64]), ALU.mult),
                         reads=[bank_buf[ob[i2]], smb], writes=[att_buf])
                st["reserved"].discard(ob[0])
                st["reserved"].discard(ob[1])

                def epi(q0=q0, nq=nq):
                    b = bank()
                    pv = bk_bf(b, 4, 128)
                    for c in range(4):
                        P.op("pe", f_tr(pv[:, c, 0:nq], att[0:nq, c * 128:(c + 1) * 128], ident[0:nq, 0:nq]),
                             reads=[att_buf, const_buf], writes=[bank_buf[b]], signal=(c == 3))
                    P.op("act", f_act(dstT[:, 0:4, q0:q0 + nq], pv[:, :, 0:nq], AF.Copy), reads=[bank_buf[b]], writes=[dstT_buf])
                epi_pending.append(epi)

        fifo = []
        for si in range(len(segs)):
            for h in range(8):
                ti = s_stage(si, h)
                fifo.append((si, h, ti))
                if len(fifo) > 2:
                    pv_stage(*fifo.pop(0))
        while fifo:
            pv_stage(*fifo.pop(0))
        while epi_pending:
            epi_pending.pop(0)()

    def mem_attention(q2T, q2T_buf, q0, nq, oT, oT_buf):
        def s_stage(h):
            pi = rot("pm", 2)
            for m in range(2):
                b = bank()
                for dk in range(2):
                    P.op("pe", f_mm(ps[:, b, 0:nq], mkT[:, 2 * h + dk, m * 128:(m + 1) * 128], q2T[:, 2 * h + dk, q0:q0 + nq],
                                    dk == 0, dk == 1), reads=[mkT_buf, q2T_buf], writes=[bank_buf[b]], signal=(dk == 1))
                P.op("act", f_act(pm[pi][:, m, 0:nq], ps[:, b, 0:nq], AF.Exp), reads=[bank_buf[b]], writes=[pm_buf[pi]])
            return pi

        def o_stage(h, pi):
            b = bank()
            for m in range(2):
                P.op("pe", f_mm(ps[:, b, 0:nq], ones[:], pm[pi][:, m, 0:nq], m == 0, m == 1),
                     reads=[pm_buf[pi], const_buf], writes=[bank_buf[b]], signal=(m == 1))
            ri = rot("rs", 3)
            P.op("dve", f_recip(rs[ri][:, 0:nq], ps[:, b, 0:nq]), reads=[bank_buf[b]], writes=[rs_buf[ri]])
            for dk in range(2):
                b = bank()
                for m in range(2):
                    c0 = 256 * h + 128 * dk
                    P.op("pe", f_mm(ps[:, b, 0:nq], mv[:, m, c0:c0 + 128], pm[pi][:, m, 0:nq], m == 0, m == 1),
                         reads=[pm_buf[pi], mv_buf], writes=[bank_buf[b]], signal=(m == 1))
                P.op("dve", f_tt(oT[:, 2 * h + dk, q0:q0 + nq], ps[:, b, 0:nq], rs[ri][:, 0:nq], ALU.mult),
                     reads=[bank_buf[b], rs_buf[ri]], writes=[oT_buf, oT_kbuf[2 * h + dk]])
        prev = None
        for h in range(4):
            pi = s_stage(h)
            if prev is not None:
                o_stage(*prev)
            prev = (h, pi)
        o_stage(*prev)

    A, B = 0, 1

    def geom(kind, bi):
        sample = kind == "sample"
        tiles = [(0, 32)] if sample else [(i * 128, 128) for i in range(4)]
        N = 32 if sample else 512
        if kind == "halo":
            cur, xpar = 1, 1
        elif sample:
            cur, xpar = 1, 0
        else:
            cur, xpar = bi % 2, bi % 2
        return sample, tiles, N, cur, xpar

    def emit_xload(kind, bi):
        if kind == "halo":
            for i in range(4):
                P.dma("sp", f"x1{i}", xres[1][:, i, :], xh[i * 128:(i + 1) * 128, :], writes=[xres_buf[1][i]])
        elif kind == "sample":
            P.dma("sp", "x00", xres[0][0:32, 0, :], xs_d[:, :], writes=[xres_buf[0][0]])
        else:
            xpar = bi % 2
            for i in range(4):
                P.dma("sp", f"x{xpar}{i}", xres[xpar][:, i, :], xp[bi * NB + i * 128: bi * NB + (i + 1) * 128, :],
                      writes=[xres_buf[xpar][i]])

    def glu(kind, bi, src=None):
        sample, tiles, N, cur, xpar = geom(kind, bi)
        hT, hTb = src if src is not None else (actT[A], actT_buf[A])
        slab_v, sbf_v = use_slab("w_in", 3)
        slab_g, sbf_g = use_slab("w_in", 4)
        for c in range(4):
            bv = bank()
            for k in range(8):
                P.op("pe", f_mm(ps[:, bv, 0:N], slab_v[:, k, c * 128:(c + 1) * 128], hT[:, k, 0:N], k == 0, k == 7),
                     reads=[sbf_v, hTb], writes=[bank_buf[bv]], signal=(k == 7))
            bg = bank()
            for k in range(8):
                P.op("pe", f_mm(ps[:, bg, 0:N], slab_g[:, k, c * 128:(c + 1) * 128], hT[:, k, 0:N], k == 0, k == 7),
                     reads=[sbf_g, hTb], writes=[bank_buf[bg]], signal=(k == 7))
            gi = rot("sg", 2)
            P.op("act", f_act(sg[gi][:, 0:N], ps[:, bg, 0:N], AF.Tanh, scale=0.5), reads=[bank_buf[bg]], writes=[sg_buf[gi]])
            if sample:
                dst = uTs[:, c, :, 30:46]
                srcs = sg[gi][:, 0:32].rearrange("p (s t) -> p s t", s=2)
                srcv = ps[:, bv, 0:32].rearrange("p (s t) -> p s t", s=2)
                P.op("dve", f_stt(dst, srcs, 1.0, srcv, ALU.add, ALU.mult), reads=[sg_buf[gi], bank_buf[bv]], writes=[uTs_buf])
            else:
                P.op("dve", f_stt(uT[:, c, 30:30 + N], sg[gi][:, 0:N], 1.0, ps[:, bv, 0:N], ALU.add, ALU.mult),
                     reads=[sg_buf[gi], bank_buf[bv]], writes=[uT_buf])

    def queue_conv(kind, bi):
        sample, tiles, N, cur, xpar = geom(kind, bi)
        for j in range(31):
            for c in range(4):
                if sample:
                    src = uTs[:, c, :, j:j + 16]
                    dstc = cacc[:, c, 0:32].rearrange("p (s t) -> p s t", s=2)
                    ub = uTs_buf
                else:
                    src = uT[:, c, j:j + N]
                    dstc = cacc[:, c, 0:N]
                    ub = uT_buf
                if j == 0:
                    pending.append(lambda src=src, dstc=dstc, ub=ub, c=c: P.op(
                        "dve", f_ts(dstc, src, cvec[:, c, 0:1], cvec[:, c, 31:32], ALU.mult, ALU.add),
                        reads=[ub, cvec_buf], writes=[cacc_buf[c]]))
                else:
                    pending.append(lambda src=src, dstc=dstc, ub=ub, c=c, j=j: P.op(
                        "dve", f_stt(dstc, src, cvec[:, c, j:j + 1], dstc, ALU.mult, ALU.add),
                        reads=[ub, cvec_buf], writes=[cacc_buf[c]]))

    def conv_tail_out(kind, bi):
        sample, tiles, N, cur, xpar = geom(kind, bi)
        nseg = 2 if sample else 1
        for s in range(nseg):
            b = bank()
            nrow = 16 if sample else 30
            for c in range(4):
                srcu = uTs[:, c, s, 30:46] if sample else uT[:, c, N:N + 30]
                P.op("pe", f_tr(ps[0:nrow, b, c * 128:(c + 1) * 128], srcu, identf[:, :]),
                     reads=[uTs_buf if sample else uT_buf, const_buf], writes=[bank_buf[b]], signal=(c == 3))
            store_rows(cso_d[s, 14:30, :] if sample else ctail_d[:, :], b, nrow, 512, scale=0.5)

    def pre_begin(kind, bi):
        sample, tiles, N, cur, xpar = geom(kind, bi)
        xr, xb = xres[xpar], xres_buf[xpar]
        rowbufs = row_buffers()
        return norm_begin([(xr[0:nr, i, :], xb[i], nr, r0) for i, (r0, nr) in enumerate(tiles)], "g_mix_pre",
                          n_early=4, bufs=rowbufs)

    def pre_mid(kind, bi, stt):
        norm_finish(stt, actT[A], actT_buf[A])

    def pre_end(kind, bi):
        sample, tiles, N, cur, xpar = geom(kind, bi)
        if not sample:
            P.op("dve", f_copy(uT[:, :, 0:30], uT[:, :, 512:542]), reads=[uT_buf], writes=[uT_buf])
        glu(kind, bi)
        if sample or (kind == "prompt" and bi == NBLK - 1):
            conv_tail_out(kind, bi)
        queue_conv(kind, bi)

    def pre_finish(kind, bi, stt):
        pre_mid(kind, bi, stt)
        pre_end(kind, bi)

    def pre_stage(kind, bi):
        pre_finish(kind, bi, pre_begin(kind, bi))

    def halo_a():
        sample, tiles, N, cur, xpar = geom("halo", 0)
        xr, xb = xres[xpar], xres_buf[xpar]
        norm_tiles([(xr[0:nr, i, :], xb[i], nr, r0) for i, (r0, nr) in enumerate(tiles)], "g_mix_pre", actT[B], actT_buf[B])
        P.op("dve", f_copy(Vext[:, 4 * cur:4 * cur + 4, :, 64:65].rearrange("p t h o -> p (t h o)"),
                           vflag[:, 0:1].to_broadcast([128, 32])), reads=[vflag_buf], writes=[V_buf[cur]])
        glu("halo", 0, src=(actT[B], actT_buf[B]))

    def halo_b():
        sample, tiles, N, cur, xpar = geom("halo", 0)
        hT, hTb = actT[B], actT_buf[B]
        slab, sbf = use_slab("w_in", 1)
        kc0 = cur * 512
        proj_fm(slab, sbf, 4, hT, hTb, N, lambda f, b: P.op(
            "act", f_act(kT[:, f, kc0:kc0 + N], ps[:, b, 0:N], AF.Copy), reads=[bank_buf[b]], writes=[kT_buf[cur]]))
        slab, sbf = use_slab("w_in", 2)
        for i, (r0, nr) in enumerate(tiles):
            b = bank()
            proj_tm(slab, sbf, hT, hTb, r0, nr, b)
            P.op("act", f_act(Vext[0:nr, 4 * cur + i, :, 0:64], ps[0:nr, b, :].rearrange("p (h d) -> p h d", h=8), AF.Copy),
                 reads=[bank_buf[b]], writes=[V_buf[cur]])

    def main_stage(kind, bi, next_pre=None, post_b_hook=None, mid_hook=None, next_begin=None, next_mid=None):
        sample, tiles, N, cur, xpar = geom(kind, bi)
        last = (kind == "prompt" and bi == NBLK - 1)
        prev = 1 - cur
        xr, xb = xres[xpar], xres_buf[xpar]
        hT, hTb = actT[A], actT_buf[A]
        drain(10 ** 6)
        mixT, mixTb = actT[B], actT_buf[B]
        slab, sbf = use_slab("w_in", 0)
        proj_fm(slab, sbf, 4, hT, hTb, N, lambda f, b: P.op(
            "act", f_act(qT[:, f, 0:N], ps[:, b, 0:N], AF.Copy, scale=0.125), reads=[bank_buf[b]], writes=[qT_buf]))
        lninfo = []
        for i, (r0, nr) in enumerate(tiles):
            b = bank()
            for c in range(4):
                P.op("pe", f_tr(ps[0:nr, b, c * 128:(c + 1) * 128], cacc[:, c, r0:r0 + nr], identf[:, :]),
                     reads=[cacc_buf[c], const_buf], writes=[bank_buf[b]], signal=(c == 3))
            cols, smb, _ = sm_cols()
            P.op("dve", lambda e, nr=nr, b=b, i=i: e.bn_stats(out=bnst[0:nr, i, 0:6], in_=ps[0:nr, b, :]),
                 reads=[bank_buf[b]], writes=[smb])
            P.op("dve", lambda e, nr=nr, cols=cols, i=i: e.bn_aggr(out=cols[0:nr, 4:6], in_=bnst[0:nr, i, 0:6]),
                 reads=[smb], writes=[smb])
            lninfo.append((b, cols, smb))
        for i, (r0, nr) in enumerate(tiles):
            b, cols, smb = lninfo[i]
            P.op("pool", f_ts1(cols[0:nr, 3:4], cols[0:nr, 5:6], EPS, ALU.add), reads=[smb], writes=[smb])
            P.op("pool", f_tt(cols[0:nr, 2:3], cols[0:nr, 3:4], cst[0:nr, 0:1], ALU.pow), reads=[smb, const_buf], writes=[smb])
        for i, (r0, nr) in enumerate(tiles):
            b, cols, smb = lninfo[i]
            P.op("dve", f_stt(cols[0:nr, 6:7], cols[0:nr, 4:5], -1.0, cols[0:nr, 2:3], ALU.mult, ALU.mult), reads=[smb], writes=[smb])
            P.op("act", f_act(zc[i][0:nr, :], ps[0:nr, b, :], AF.Identity, scale=cols[0:nr, 2:3], bias=cols[0:nr, 6:7]),
                 reads=[bank_buf[b], smb], writes=[zc_buf[i]])

        slab, sbf = use_slab("w_in", 1)
        kc0 = cur * 512
        proj_fm(slab, sbf, 4, hT, hTb, N, lambda f, b: P.op(
            "dve", f_copy(kT[:, f, kc0:kc0 + N], ps[:, b, 0:N]), reads=[bank_buf[b]], writes=[kT_buf[cur]]))
        if last:
            for i, (r0, nr) in enumerate(tiles):
                b = bank()
                proj_tm(slab, sbf, hT, hTb, r0, nr, b)
                store_rows(ktail_d[r0:r0 + nr, :], b, nr, 512)
        if sample:
            for s in range(2):
                b = bank()
                proj_tm(slab, sbf, hT, hTb, s * 16, 16, b)
                store_rows(kso_d[s, 496:512, :], b, 16, 512)
        def c2b():
            cb = bank(2)
            st["reserved"].add(cb[0])
            st["reserved"].add(cb[1])
            for i, (r0, nr) in enumerate(tiles):
                for c in range(4):
                    pvc = ps[:, cb[c // 2], :].bitcast(BF16)[:, (c % 2) * 512:(c % 2) * 512 + 512]
                    P.op("pe", f_tr(pvc[:, r0:r0 + nr], zc[i][0:nr, c * 128:(c + 1) * 128], ident[0:nr, 0:nr]),
                         reads=[zc_buf[i], const_buf], writes=[bank_buf[cb[c // 2]]], signal=(c == 3))
            for c in range(4):
                pvc = ps[:, cb[c // 2], :].bitcast(BF16)[:, (c % 2) * 512:(c % 2) * 512 + 512]
                P.op("act", f_act(mixT[:, 4 + c, 0:N], pvc[:, 0:N], AF.Silu, scale=cvec[:, c, 32:33], bias=cvec[:, c, 33:34]),
                     reads=[bank_buf[cb[c // 2]], cvec_buf], writes=[mixTb])
            st["reserved"].discard(cb[0])
            st["reserved"].discard(cb[1])

        c2b()
        slab, sbf = use_slab("w_in", 2)
        if kind == "prompt" and bi == 1:
            P.op("pool", f_memset(Vext[:, 4 * cur:4 * cur + 4, :, 64:65], 1.0), writes=[V_buf[cur]])
        vsegs = [(s * 16, 16, 4 * cur + s) for s in range(2)] if sample else [(r0, nr, 4 * cur + i) for i, (r0, nr) in enumerate(tiles)]
        for si_, (c0, nr, vt) in enumerate(vsegs):
            b = bank()
            proj_tm(slab, sbf, hT, hTb, c0, nr, b)
            P.op("act", f_act(Vext[0:nr, vt, :, 0:64], ps[0:nr, b, :].rearrange("p (h d) -> p h d", h=8), AF.Copy),
                 reads=[bank_buf[b]], writes=[V_buf[cur]])
            if last:
                store_rows(vtail_d[c0:c0 + nr, :], b, nr, 512)
            if sample:
                store_rows(vso_d[si_, 496:512, :], b, 16, 512)
        if post_b_hook is not None:
            post_b_hook()

        if sample:
            for s in range(2):
                for t in range(4):
                    sap, sbufs_, ssem = hslot()
                    P.dma("sp", ssem, sap, ck_d[s, t * 128:(t + 1) * 128, :], writes=sbufs_)
                    hi = rot("hbf", 2)
                    P.op("dve", f_copy(hbf[hi][:, 0:512], sap), reads=sbufs_, writes=[hbf_buf[hi]])
                    b = bank()
                    pv = bk_bf(b, 4, 128)
                    for f in range(4):
                        P.op("pe", f_tr(pv[:, f, :], hbf[hi][:, f * 128:(f + 1) * 128], ident[:, :]),
                             reads=[hbf_buf[hi], const_buf], writes=[bank_buf[b]], signal=(f == 3))
                    P.op("act", f_act(kT[:, :, prev * 512 + t * 128: prev * 512 + (t + 1) * 128], pv[:, :, :], AF.Copy),
                         reads=[bank_buf[b]], writes=[kT_buf[prev]])
                    sap, sbufs_, ssem = hslot()
                    P.dma("sp", ssem, sap, cv_d[s, t * 128:(t + 1) * 128, :], writes=sbufs_)
                    P.op("dve", f_copy(Vext[:, 4 * prev + t, :, 0:64], sap.rearrange("p (h d) -> p h d", h=8)),
                         reads=sbufs_, writes=[V_buf[prev]])
                kt_l = [(prev * 512 + t * 128, 128, 4 * prev + t, kT_buf[prev], V_buf[prev]) for t in range(4)]
                kt_l.append((cur * 512 + s * 16, 16, 4 * cur + s, kT_buf[cur], V_buf[cur]))
                attention_block([(s * 16, 16, kt_l)], mixT, mixTb)
        else:
            segs = []
            for j in range(4):
                kt_l = []
                for t in range(5):
                    gt = j + t
                    half = prev if gt < 4 else cur
                    kt_l.append((half * 512 + (gt % 4) * 128, 128, 4 * half + gt % 4, kT_buf[half], V_buf[half]))
                segs.append((j * 128, 128, kt_l))
            attention_block(segs, mixT, mixTb)

        s0, s0b = use_slab("w_out", 0)
        s1, s1b = use_slab("w_out", 1)
        pn = PostNorm("g_mix_post")
        for i, (r0, nr) in enumerate(tiles):
            b0 = bank()
            proj_tm(s0, s0b, mixT, mixTb, r0, nr, b0)
            b1 = bank()
            proj_tm(s1, s1b, mixT, mixTb, r0, nr, b1)
            pn.add(b0, b1, nr, xr[0:nr, i, :], xb[i])
        pn.finish()

        if mid_hook is not None:
            mid_hook()
        norm_tiles([(xr[0:nr, i, :], xb[i], nr, r0) for i, (r0, nr) in enumerate(tiles)], "g_mem_pre", actT[A], actT_buf[A], wide=not sample)
        q2T, q2Tb = actT[B], actT_buf[B]
        for n in range(2):
            slab, sbf = use_slab("w_mq", n)
            proj_fm(slab, sbf, 4, actT[A], actT_buf[A], N, lambda f, b, n=n: P.op(
                "act", f_act(q2T[:, 4 * n + f, 0:N], ps[:, b, 0:N], AF.Copy, scale=1.0 / 16.0), reads=[bank_buf[b]], writes=[q2Tb]))
        oT, oTb = actT[A], actT_buf[A]
        if sample:
            for s in range(2):
                for m in range(2):
                    for src_d, is_k in ((cmk_d, True), (cmv_d, False)):
                        for hf in range(2):
                            sap, sbufs_, ssem = hslot()
                            P.dma("sp", ssem, sap, src_d[s, m * 128:(m + 1) * 128, hf * 512:(hf + 1) * 512], writes=sbufs_)
                            if is_k:
                                hi = rot("hbf", 2)
                                P.op("dve", f_copy(hbf[hi][:, 0:512], sap), reads=sbufs_, writes=[hbf_buf[hi]])
                                b = bank()
                                pv = bk_bf(b, 4, 128)
                                for f in range(4):
                                    P.op("pe", f_tr(pv[:, f, :], hbf[hi][:, f * 128:(f + 1) * 128], ident[:, :]),
                                         reads=[hbf_buf[hi], const_buf], writes=[bank_buf[b]], signal=(f == 3))
                                P.op("act", f_act(mkT[:, 4 * hf:4 * hf + 4, m * 128:(m + 1) * 128], pv[:, :, :], AF.Copy),
                                     reads=[bank_buf[b]], writes=[mkT_buf])
                            else:
                                P.op("dve", f_copy(mv[:, m, hf * 512:(hf + 1) * 512], sap),
                                     reads=sbufs_, writes=[mv_buf])
                mem_attention(q2T, q2Tb, s * 16, 16, oT, oTb)
        else:
            mem_attention(q2T, q2Tb, 0, N, oT, oTb)
        nstate = next_begin() if next_begin is not None else None
        s0, s0b = use_slab("w_mo", 0)
        s1, s1b = use_slab("w_mo", 1)
        pn = PostNorm("g_mem_post")
        mob = {}
        for i in range(len(tiles)):
            for n in range(2):
                b = bank()
                mob[(i, n)] = b
                st["reserved"].add(b)
        for khalf in range(2):
            for i, (r0, nr) in enumerate(tiles):
                for n, (sl, slb) in enumerate(((s0, s0b), (s1, s1b))):
                    b = mob[(i, n)]
                    for k in range(4 * khalf, 4 * khalf + 4):
                        P.op("pe", f_mm(ps[0:nr, b, :], oT[:, k, r0:r0 + nr], sl[:, k, :], k == 0, k == 7),
                             reads=[slb, oT_kbuf[k]], touch=[oTb], writes=[bank_buf[b]], signal=(k % 4 == 3))
        for i, (r0, nr) in enumerate(tiles):
            pn.add(mob[(i, 0)], mob[(i, 1)], nr, xr[0:nr, i, :], xb[i])
            st["reserved"].discard(mob[(i, 0)])
            st["reserved"].discard(mob[(i, 1)])
            if next_mid is not None and i == min(1, len(tiles) - 1):
                next_mid(nstate)
        pn.finish()

        gsrcs = [(xr[0:nr, i, :], xb[i], nr, r0) for i, (r0, nr) in enumerate(tiles)]
        gstate = norm_begin(gsrcs, "g_ffn_pre", n_early=4, bufs=row_buffers()) if not sample else norm_begin(gsrcs, "g_ffn_pre", n_early=0)
        if next_pre is not None:
            next_pre(nstate)

        norm_finish(gstate, actT[B], actT_buf[B])
        h3, h3b = actT[B], actT_buf[B]
        for s in range(6):
            sg_, sgb = use_slab("w_gate", s)
            su_, sub = use_slab("w_up", s)
            nf = 4 if s < 5 else 2
            for ff in range(nf):
                f = s * 4 + ff
                bg = bank()
                for k in range(8):
                    P.op("pe", f_mm(ps[:, bg, 0:N], sg_[:, k, ff * 128:(ff + 1) * 128], h3[:, k, 0:N], k == 0, k == 7),
                         reads=[sgb, h3b], writes=[bank_buf[bg]], signal=(k == 7))
                bu = bank()
                for k in range(8):
                    P.op("pe", f_mm(ps[:, bu, 0:N], su_[:, k, ff * 128:(ff + 1) * 128], h3[:, k, 0:N], k == 0, k == 7),
                         reads=[sub, h3b], writes=[bank_buf[bu]], signal=(k == 7))
                gi = rot("sg", 2)
                P.op("act", f_act(sg[gi][:, 0:N], ps[:, bg, 0:N], AF.Silu), reads=[bank_buf[bg]], writes=[sg_buf[gi]])
                P.op("dve", f_tt(hid[:, f, 0:N], sg[gi][:, 0:N], ps[:, bu, 0:N], ALU.mult),
                     reads=[sg_buf[gi], bank_buf[bu]], writes=[hid_buf[f]])
                drain(4)
        dbanks = {}
        for n in range(2):
            for i in range(len(tiles)):
                b = bank()
                dbanks[(i, n)] = b
                st["reserved"].add(b)
        for n in range(2):
            for ks in range(3):
                slab, sbf = use_slab("w_down", n * 3 + ks)
                kt = (8, 8, 6)[ks]
                for i, (r0, nr) in enumerate(tiles):
                    b = dbanks[(i, n)]
                    for k in range(kt):
                        fidx = ks * 8 + k
                        last_mm = (ks == 2 and k == kt - 1)
                        P.op("pe", f_mm(ps[0:nr, b, :], hid[:, fidx, r0:r0 + nr], slab[:, k, :], ks == 0 and k == 0, last_mm),
                             reads=[sbf, hid_buf[fidx]], writes=[bank_buf[b]], signal=(k == kt - 1))
                drain(8)
        drain(10 ** 6)
        pn = PostNorm("g_ffn_post")
        for i, (r0, nr) in enumerate(tiles):
            if sample:
                dsty = ys_d[r0:r0 + nr, :]
            else:
                dsty = y_d[bi * NB + r0: bi * NB + r0 + nr, :]
            pn.add(dbanks[(i, 0)], dbanks[(i, 1)], nr, xr[0:nr, i, :], xb[i],
                   after=lambda i=i, nr=nr, dsty=dsty: P.dma("sp", f"x{xpar}{i}", dsty, xr[0:nr, i, :], reads=[xb[i]]))
            st["reserved"].discard(dbanks[(i, 0)])
            st["reserved"].discard(dbanks[(i, 1)])
        pn.finish()

    def mem_kv():
        mT, mTb = actT[1], actT_buf[1]
        srcs = []
        for m in range(2):
            si = rot("tmp", 2)
            P.dma("sp", f"tmpl{si}", tmp[si][:], memp_d[m * 128:(m + 1) * 128, :], writes=[tmp_buf[si]])
            srcs.append((tmp[si][:, :], tmp_buf[si], 128, m * 128))
        norm_tiles(srcs, "g_mem_kv", mT, mTb)
        for n in range(2):
            slab, sbf = use_slab("w_mk", n)
            proj_fm(slab, sbf, 4, mT, mTb, 256, lambda f, b, n=n: P.op(
                "act", f_act(mkT[:, 4 * n + f, :], ps[:, b, 0:256], AF.Copy), reads=[bank_buf[b]], writes=[mkT_buf]))
            for m in range(2):
                b = bank()
                proj_tm(slab, sbf, mT, mTb, m * 128, 128, b)
                store_rows(mko_d[m * 128:(m + 1) * 128, n * 512:(n + 1) * 512], b, 128, 512)
        for n in range(2):
            slab, sbf = use_slab("w_mv", n)
            for m in range(2):
                b = bank()
                proj_tm(slab, sbf, mT, mTb, m * 128, 128, b)
                P.op("dve", f_copy(mv[:, m, n * 512:(n + 1) * 512], ps[:, b, :]), reads=[bank_buf[b]], writes=[mv_buf])
                store_rows(mvo_d[m * 128:(m + 1) * 128, n * 512:(n + 1) * 512], b, 128, 512, extra=[mv_buf])

    def sample_prep():
        for s in range(2):
            si = rot("stage", 1)
            P.dma("sp", f"stg{si}", stage[si][0:30, 0:512], cc_d[s], writes=[stage_buf[si]])
            b = bank()
            for c in range(4):
                P.op("pe", f_tr(ps[:, b, c * 32:c * 32 + 30], stage[si][0:30, c * 128:(c + 1) * 128], identf[0:30, 0:30]),
                     reads=[stage_buf[si], const_buf], writes=[bank_buf[b]], signal=(c == 3))
            P.op("act", f_act(uTs[:, :, s, 0:30], ps[:, b, 0:128].rearrange("p (c j) -> p c j", c=4)[:, :, 0:30], AF.Copy, scale=2.0),
                 reads=[bank_buf[b]], writes=[uTs_buf])
            P.dma("sp", "copy", kso_d[s, 0:496, :], ck_d[s, 16:512, :])
            P.dma("sp", "copy", vso_d[s, 0:496, :], cv_d[s, 16:512, :])
            P.dma("sp", "copy", cso_d[s, 0:14, :], cc_d[s, 16:30, :])
        P.op("pool", f_memset(Vext[:, :, :, 64:65], 1.0), writes=[V_buf[0], V_buf[1]])
        emit_xload("sample", 0)

    import os
    kstop = int(os.environ.get("KSTOP", "99"))
    steps = []
    gate_dummy = Buf("gate_dummy")

    def gated_casts(names, gate_bufs):
        P.op("pool", f_memset(cst[:, 2:3], 0.0), reads=list(gate_bufs), writes=[gate_dummy])
        cast_group(names)
    def queue_late_casts():
        cast_pending.extend([("w_mq", 0), ("w_mq", 1), ("w_mo", 0), ("w_mo", 1)])
        for i in range(6):
            cast_pending.extend([("w_gate", i), ("w_up", i)])
        cast_pending.extend([("w_down", i) for i in range(6)])
    steps.append(halo_a)
    steps.append(lambda: (pre_stage("prompt", 0), drain(10 ** 6)))
    steps.append(lambda: (halo_b(), gated_casts(["w_out", "w_mk", "w_mv"], [V_buf[1], kT_buf[1]]),
                          build_btab(), queue_late_casts()))
    for bi in range(NBLK):
        if bi + 1 < NBLK:
            nbeg = (lambda b=bi: (emit_xload("prompt", b + 1), pre_begin("prompt", b + 1))[1])
            nmid = (lambda stt, b=bi: pre_mid("prompt", b + 1, stt))
            nxt = (lambda stt, b=bi: pre_end("prompt", b + 1))
        else:
            nbeg = (lambda: (sample_prep(), pre_begin("sample", 0))[1])
            nmid = (lambda stt: pre_mid("sample", 0, stt))
            nxt = (lambda stt: pre_end("sample", 0))
        if bi == 0:
            steps.append(lambda nxt=nxt, nbeg=nbeg, nmid=nmid: main_stage("prompt", 0, nxt, mid_hook=mem_kv, next_begin=nbeg, next_mid=nmid))
        else:
            steps.append(lambda bi=bi, nxt=nxt, nbeg=nbeg, nmid=nmid: main_stage("prompt", bi, nxt, next_begin=nbeg, next_mid=nmid))
    steps.append(lambda: main_stage("sample", 0, None))
    for i, stp in enumerate(steps):
        if i >= kstop:
            break
        stp()
    if kstop >= len(steps):
        assert st["slab_pos"] == len(sched), (st["slab_pos"], len(sched))
        assert not pending

    sem_names = ["act", "dve", "pool", "pe"] + sorted(P.dval.keys())
    sems = {n: es.enter_context(nc.semaphore("s_" + n)) for n in sem_names}
    for n, v in P.dval.items():
        P.q["sp"].append(("wait", n, v))
    for e in ("act", "dve", "pool", "pe"):
        P.q["sp"].append(("wait", e, P.cnt[e]))

    def replay(engname, eobj):
        for it in P.q[engname]:
            if it[0] == "wait":
                eobj.wait_ge(sems[it[1]], it[2])
            elif it[0] == "op":
                ins = it[1](eobj)
                if it[2]:
                    ins.then_inc(sems[engname], 1)
            else:
                eobj.dma_start(out=it[1], in_=it[2]).then_inc(sems[it[3]], 16)

    with nc.Block() as block:
        @block.sync
        def _(e):
            replay("sp", e)

        @block.scalar
        def _(e):
            replay("act", e)

        @block.vector
        def _(e):
            replay("dve", e)

        @block.gpsimd
        def _(e):
            replay("pool", e)

        @block.tensor
        def _(e):
            replay("pe", e)
    es.close()
    nc._prog_stats = {e: len(P.q[e]) for e in P.ENG}
    return nc


_NC_CACHE = {}


def kernel(x_prompt, x_sample, cache_att_k, cache_att_v, cache_conv, cache_mem_k, cache_mem_v, mem_prompt,
           g_mix_pre, g_mix_post, w_in, rel_bias, conv_w, conv_b, cln_g, cln_b, w_out,
           g_mem_pre, g_mem_post, g_mem_kv, w_mq, w_mk, w_mv, w_mo, g_ffn_pre, g_ffn_post, w_gate, w_up, w_down):
    f = lambda a: np.ascontiguousarray(np.asarray(a, dtype=np.float32))
    x_prompt, x_sample = f(x_prompt), f(x_sample)
    if "nc" not in _NC_CACHE:
        _NC_CACHE["nc"] = build_nc()
    nc = _NC_CACHE["nc"]
    cvec = np.concatenate([f(conv_w)[0], f(conv_b)[0][None], f(cln_g)[0][None], f(cln_b)[0][None]], axis=0)
    shared = {
        "g_mix_pre": f(g_mix_pre), "g_mix_post": f(g_mix_post), "g_mem_pre": f(g_mem_pre), "g_mem_post": f(g_mem_post),
        "g_mem_kv": f(g_mem_kv), "g_ffn_pre": f(g_ffn_pre), "g_ffn_post": f(g_ffn_post),
        "rel_bias": f(rel_bias)[0], "cvec": f(cvec),
        "w_in": f(w_in)[0], "w_out": f(w_out)[0], "w_mq": f(w_mq)[0], "w_mk": f(w_mk)[0], "w_mv": f(w_mv)[0],
        "w_mo": f(w_mo)[0], "w_gate": f(w_gate)[0], "w_up": f(w_up)[0], "w_down": f(w_down)[0],
    }
    ck, cv = f(cache_att_k)[0], f(cache_att_v)[0]
    cc, cmk, cmv = f(cache_conv)[0], f(cache_mem_k)[0], f(cache_mem_v)[0]
    memp = f(mem_prompt)
    in_maps = []
    for c in range(8):
        b, qt = c // 4, c % 4
        m = dict(shared)
        m["xp"] = x_prompt[b, qt * 4096:(qt + 1) * 4096]
        m["xh"] = x_prompt[b, qt * 4096 - 512:qt * 4096] if qt > 0 else np.zeros((512, D), np.float32)
        m["vflag"] = np.full((128, 1), 1.0 if qt > 0 else 0.0, np.float32)
        m["xs"] = x_sample[2 * c:2 * c + 2].reshape(32, D)
        m["ck"] = ck[2 * c:2 * c + 2].reshape(2, 512, 512)
        m["cv"] = cv[2 * c:2 * c + 2].reshape(2, 512, 512)
        m["cc"] = cc[2 * c:2 * c + 2]
        m["cmk"] = cmk[2 * c:2 * c + 2].reshape(2, 256, D)
        m["cmv"] = cmv[2 * c:2 * c + 2].reshape(2, 256, D)
        m["memp"] = memp[b]
        in_maps.append({k: np.ascontiguousarray(v) for k, v in m.items()})
    res = run_bass_kernel_spmd(nc, in_maps, core_ids=list(range(8)))
    R = res.results
    y_prompt = np.stack([np.concatenate([R[4 * b + q]["y"] for q in range(4)], axis=0) for b in range(2)], axis=0)
    y_sample = np.concatenate([R[c]["ys"].reshape(2, 16, D) for c in range(8)], axis=0)
    nk = np.stack([R[4 * b + 3]["ktail"].reshape(512, 8, 64) for b in range(2)], axis=0)[None]
    nv = np.stack([R[4 * b + 3]["vtail"].reshape(512, 8, 64) for b in range(2)], axis=0)[None]
    ncv = np.stack([R[4 * b + 3]["ctail"] for b in range(2)], axis=0)[None]
    nmk = np.stack([R[4 * b]["mko"].reshape(256, 4, 256) for b in range(2)], axis=0)[None]
    nmv = np.stack([R[4 * b]["mvo"].reshape(256, 4, 256) for b in range(2)], axis=0)[None]
    ks = np.concatenate([R[c]["kso"].reshape(2, 512, 8, 64) for c in range(8)], axis=0)[None]
    vs = np.concatenate([R[c]["vso"].reshape(2, 512, 8, 64) for c in range(8)], axis=0)[None]
    cs = np.concatenate([R[c]["cso"] for c in range(8)], axis=0)[None]
    out = (y_prompt, y_sample, nk, nv, ncv, nmk, nmv, ks, vs, cs)
    return tuple(np.ascontiguousarray(o, dtype=np.float32) for o in out)
```
